# Optimizing a Trainium2 kernel written in Bass

```python
import math
import jax, jax.numpy as jnp
from jax import lax
import numpy as np

D_MODEL = 1024
BATCH = 32
SEQ = 256
DEPTH = 2
DEC_BATCH = 8
DEC_SEQ = 2048
PAST_LEN = 256

GRID_W = 64
N_HEADS = 8
KV_HEADS = 2
HEAD_DIM = 64
Q_PER_KV = N_HEADS // KV_HEADS
ATTN_W = N_HEADS * HEAD_DIM
KV_W = KV_HEADS * HEAD_DIM
CONV_W = D_MODEL // 4
HYENA_W = D_MODEL // 4
MIX_W = ATTN_W + CONV_W + HYENA_W
IN_W = ATTN_W + 2 * KV_W + 2 * CONV_W + 3 * HYENA_W
WINDOW = 128
BLOCK = 128
CONV_K = 31
SHORT_K = 3
HYENA_ORDER = 2
HYENA_EMB = 33
HYENA_BANDS = (HYENA_EMB - 1) // 2
HYENA_HID = 64
D_FF = 2816
N_MOD = 9
ROPE_BASE = 10000.0
EPS = 1e-6
NEG_INF = -1e30

kernel_name = 'hybrid_prefix_diffusion_step'

F32 = jnp.float32


def rmsnorm(x, g):
    xf = x.astype(F32)
    y = xf * lax.rsqrt(jnp.mean(xf * xf, axis=-1, keepdims=True) + EPS)
    return (y * g.astype(F32)).astype(x.dtype)


def layernorm(x, g, b):
    xf = x.astype(F32)
    mu = jnp.mean(xf, axis=-1, keepdims=True)
    var = jnp.mean(jnp.square(xf - mu), axis=-1, keepdims=True)
    y = (xf - mu) * lax.rsqrt(var + EPS)
    return (y * g.astype(F32) + b.astype(F32)).astype(x.dtype)


def swiglu(h, wg, wu, wd):
    return (jax.nn.silu(h @ wg) * (h @ wu)) @ wd


def depthwise_conv(x, w, b):
    k = w.shape[0]
    y = lax.conv_general_dilated(x, w[:, None, :].astype(x.dtype), (1,), ((k // 2, k // 2),),
                                 dimension_numbers=('NWC', 'WIO', 'NWC'),
                                 feature_group_count=x.shape[-1])
    return y + b


def rope_2d(x):
    L = x.shape[1]
    rows = L // GRID_W
    row = jnp.repeat(jnp.arange(rows), GRID_W)
    col = jnp.arange(rows * GRID_W) % GRID_W
    nf = HEAD_DIM // 4
    inv = ROPE_BASE ** (-jnp.arange(nf, dtype=F32) / nf)
    xf = x.astype(F32)

    def rot(xp, pos):
        ang = pos.astype(F32)[:, None] * inv[None, :]
        cos = jnp.cos(ang)[None, :, None, :]
        sin = jnp.sin(ang)[None, :, None, :]
        a, b = xp[..., :nf], xp[..., nf:]
        return jnp.concatenate([a * cos - b * sin, b * cos + a * sin], axis=-1)

    half = HEAD_DIM // 2
    out = jnp.concatenate([rot(xf[..., :half], row), rot(xf[..., half:], col)], axis=-1)
    return out.astype(x.dtype)


def context_attention(q, k, v, sink):
    B, L = q.shape[0], q.shape[1]
    nb = L // BLOCK
    qb = q.reshape(B, nb, BLOCK, KV_HEADS, Q_PER_KV, HEAD_DIM).swapaxes(0, 1)
    sk = jnp.broadcast_to(sink.astype(F32).reshape(1, KV_HEADS, Q_PER_KV, 1, 1),
                          (B, KV_HEADS, Q_PER_KV, BLOCK, 1))
    scale = HEAD_DIM ** -0.5

    def one(qblk):
        s = jnp.einsum('bqgrd,bkgd->bgrqk', qblk, k).astype(F32) * scale
        p = jax.nn.softmax(jnp.concatenate([s, sk], axis=-1), axis=-1)[..., :-1]
        return jnp.einsum('bgrqk,bkgd->bqgrd', p.astype(v.dtype), v)

    o = lax.map(one, qb)
    return o.swapaxes(0, 1).reshape(B, L, ATTN_W)


def latent_attention(q, k, v, kc, vc, sink):
    B, L = q.shape[0], q.shape[1]
    nb = L // BLOCK
    nloc = 3 * BLOCK
    qb = q.reshape(B, nb, BLOCK, KV_HEADS, Q_PER_KV, HEAD_DIM).swapaxes(0, 1)
    pad = ((0, 0), (BLOCK, BLOCK), (0, 0), (0, 0))
    kp = jnp.pad(k, pad)
    vp = jnp.pad(v, pad)
    qi = jnp.arange(BLOCK)
    kj = jnp.arange(nloc)
    band = jnp.abs(kj[None, :] - BLOCK - qi[:, None]) <= WINDOW
    sk = jnp.broadcast_to(sink.astype(F32).reshape(1, KV_HEADS, Q_PER_KV, 1, 1),
                          (B, KV_HEADS, Q_PER_KV, BLOCK, 1))
    scale = HEAD_DIM ** -0.5

    def one(args):
        qblk, b = args
        kw = lax.dynamic_slice_in_dim(kp, b * BLOCK, nloc, axis=1)
        vw = lax.dynamic_slice_in_dim(vp, b * BLOCK, nloc, axis=1)
        j = (b - 1) * BLOCK + kj
        valid = band & ((j >= 0) & (j < L))[None, :]
        s_loc = jnp.einsum('bqgrd,bkgd->bgrqk', qblk, kw).astype(F32) * scale
        s_loc = jnp.where(valid, s_loc, NEG_INF)
        s_ctx = jnp.einsum('bqgrd,bcgd->bgrqc', qblk, kc).astype(F32) * scale
        p = jax.nn.softmax(jnp.concatenate([s_loc, s_ctx, sk], axis=-1), axis=-1)
        o = jnp.einsum('bgrqk,bkgd->bqgrd', p[..., :nloc].astype(v.dtype), vw)
        o = o + jnp.einsum('bgrqc,bcgd->bqgrd', p[..., nloc:-1].astype(vc.dtype), vc)
        return o

    o = lax.map(one, (qb, jnp.arange(nb)))
    return o.swapaxes(0, 1).reshape(B, L, ATTN_W)


def conformer_conv(u, dw, dwb, lng, lnb, pw):
    a, g = jnp.split(u, 2, axis=-1)
    y = a * jax.nn.sigmoid(g)
    y = depthwise_conv(y, dw, dwb)
    y = jax.nn.silu(layernorm(y, lng, lnb))
    return y @ pw


def hyena_filters(L, w1, b1, f1, w2, b2, f2, w3, log_decay):
    t = jnp.arange(L, dtype=F32)
    tn = t / (L - 1)
    bands = jnp.linspace(1e-4, HYENA_BANDS - 1, HYENA_BANDS, dtype=F32)
    ang = 2.0 * math.pi * t[:, None] * bands[None, :] / L
    z = jnp.concatenate([tn[:, None], jnp.cos(ang), -jnp.sin(ang)], axis=-1)
    h = jnp.sin(f1.astype(F32) * (z @ w1.astype(F32) + b1.astype(F32)))
    h = jnp.sin(f2.astype(F32) * (h @ w2.astype(F32) + b2.astype(F32)))
    h = (h @ w3.astype(F32)).reshape(L, HYENA_ORDER, 2, HYENA_W)
    decay = jnp.exp(log_decay.astype(F32)).reshape(HYENA_ORDER, 2, HYENA_W)
    h = h * jnp.exp(-tn[:, None, None, None] * decay[None])
    h = h * lax.rsqrt(jnp.sum(h * h, axis=(0, 2), keepdims=True) + EPS)
    return h


def bidir_longconv(u, hf, hb):
    L, C = u.shape[1], u.shape[2]
    hc = jnp.concatenate([hf, jnp.zeros((1, C), F32), hb[1:][::-1]], axis=0)
    U = jnp.fft.rfft(u.astype(F32), n=2 * L, axis=1)
    H = jnp.fft.rfft(hc, axis=0)
    y = jnp.fft.irfft(U * H[None], n=2 * L, axis=1)[:, :L]
    return y.astype(u.dtype)


def hyena_mixer(u, sw, sb, w1, b1, f1, w2, b2, f2, w3, log_decay, bias):
    L = u.shape[1]
    u = depthwise_conv(u, sw, sb)
    vv, x1, x2 = jnp.split(u, 3, axis=-1)
    h = hyena_filters(L, w1, b1, f1, w2, b2, f2, w3, log_decay)
    y = x1 * (bidir_longconv(vv, h[:, 0, 0], h[:, 0, 1]) + bias[0] * vv)
    y = x2 * (bidir_longconv(y, h[:, 1, 0], h[:, 1, 1]) + bias[1] * y)
    return y


def token_mix(h, l, P, cache):
    B, L = h.shape[0], h.shape[1]
    z = h @ P['w_in'][l]
    o1 = ATTN_W
    o2 = o1 + KV_W
    o3 = o2 + KV_W
    o4 = o3 + 2 * CONV_W
    q = z[..., :o1].reshape(B, L, N_HEADS, HEAD_DIM)
    k = z[..., o1:o2].reshape(B, L, KV_HEADS, HEAD_DIM)
    v = z[..., o2:o3].reshape(B, L, KV_HEADS, HEAD_DIM)
    if cache is None:
        att = context_attention(q, k, v, P['attn_sink'][l])
        kv_out = (k, v)
    else:
        att = latent_attention(rope_2d(q), rope_2d(k), v, cache[0], cache[1], P['attn_sink'][l])
        kv_out = None
    conv = conformer_conv(z[..., o3:o4], P['conv_dw'][l], P['conv_dw_b'][l],
                          P['conv_ln_g'][l], P['conv_ln_b'][l], P['conv_pw'][l])
    hy = hyena_mixer(z[..., o4:], P['hy_short_w'][l], P['hy_short_b'][l],
                     P['hy_w1'][l], P['hy_b1'][l], P['hy_f1'][l],
                     P['hy_w2'][l], P['hy_b2'][l], P['hy_f2'][l],
                     P['hy_w3'][l], P['hy_log_decay'][l], P['hy_bias'][l])
    out = jnp.concatenate([att, conv, hy], axis=-1) @ P['w_out'][l]
    return out, kv_out


def trunk_layer(x, cond, l, P, cache):
    mod = (jax.nn.silu(cond) @ P['w_mod'][l] + P['b_mod'][l])[:, None, :]
    sh1, sc1, g1, sh2, sc2, g2, sh3, sc3, g3 = jnp.split(mod, N_MOD, axis=-1)
    h = rmsnorm(x, P['g_ffn1'][l]) * (1 + sc1) + sh1
    x = x + 0.5 * g1 * swiglu(h, P['w1_gate'][l], P['w1_up'][l], P['w1_down'][l])
    h = rmsnorm(x, P['g_mix'][l]) * (1 + sc2) + sh2
    m, kv = token_mix(h, l, P, cache)
    x = x + g2 * m
    h = rmsnorm(x, P['g_ffn2'][l]) * (1 + sc3) + sh3
    x = x + 0.5 * g3 * swiglu(h, P['w2_gate'][l], P['w2_up'][l], P['w2_down'][l])
    return x, kv


def setup_inputs(seed: int = 0) -> dict:
    key = jax.random.key(seed)
    keys = iter(jax.random.split(key, 48))

    def nrm(shape, std):
        return std * jax.random.normal(next(keys), shape, F32)

    def gain(shape):
        return 1.0 + nrm(shape, 0.05)

    D = D_MODEL
    cache_shape = (DEC_BATCH, DEPTH, PAST_LEN, KV_HEADS, HEAD_DIM)
    return {
        'x_prompt': nrm((BATCH, SEQ, D), 1.0),
        'x_sample': nrm((DEC_BATCH, DEC_SEQ, D), 1.0),
        'c': nrm((DEC_BATCH, D), 1.0),
        'cache_k': nrm(cache_shape, 1.0),
        'cache_v': nrm(cache_shape, 1.0),
        'c_ctx': nrm((D,), 1.0),
        'w_mod': nrm((DEPTH, D, N_MOD * D), 0.5 * D ** -0.5),
        'b_mod': nrm((DEPTH, N_MOD * D), 0.02),
        'g_ffn1': gain((DEPTH, D)),
        'g_mix': gain((DEPTH, D)),
        'g_ffn2': gain((DEPTH, D)),
        'g_final': gain((D,)),
        'w1_gate': nrm((DEPTH, D, D_FF), D ** -0.5),
        'w1_up': nrm((DEPTH, D, D_FF), D ** -0.5),
        'w1_down': nrm((DEPTH, D_FF, D), D_FF ** -0.5),
        'w2_gate': nrm((DEPTH, D, D_FF), D ** -0.5),
        'w2_up': nrm((DEPTH, D, D_FF), D ** -0.5),
        'w2_down': nrm((DEPTH, D_FF, D), D_FF ** -0.5),
        'w_in': nrm((DEPTH, D, IN_W), D ** -0.5),
        'w_out': nrm((DEPTH, MIX_W, D), MIX_W ** -0.5),
        'attn_sink': nrm((DEPTH, N_HEADS), 0.5),
        'conv_dw': nrm((DEPTH, CONV_K, CONV_W), CONV_K ** -0.5),
        'conv_dw_b': nrm((DEPTH, CONV_W), 0.02),
        'conv_ln_g': gain((DEPTH, CONV_W)),
        'conv_ln_b': nrm((DEPTH, CONV_W), 0.02),
        'conv_pw': nrm((DEPTH, CONV_W, CONV_W), CONV_W ** -0.5),
        'hy_short_w': nrm((DEPTH, SHORT_K, 3 * HYENA_W), SHORT_K ** -0.5),
        'hy_short_b': nrm((DEPTH, 3 * HYENA_W), 0.02),
        'hy_w1': nrm((DEPTH, HYENA_EMB, HYENA_HID), HYENA_EMB ** -0.5),
        'hy_b1': nrm((DEPTH, HYENA_HID), 0.1),
        'hy_f1': gain((DEPTH, HYENA_HID)),
        'hy_w2': nrm((DEPTH, HYENA_HID, HYENA_HID), HYENA_HID ** -0.5),
        'hy_b2': nrm((DEPTH, HYENA_HID), 0.1),
        'hy_f2': gain((DEPTH, HYENA_HID)),
        'hy_w3': nrm((DEPTH, HYENA_HID, HYENA_ORDER * 2 * HYENA_W), HYENA_HID ** -0.5),
        'hy_log_decay': jax.random.uniform(next(keys), (DEPTH, HYENA_ORDER * 2 * HYENA_W), F32,
                                           math.log(3.0), math.log(15.0)),
        'hy_bias': nrm((DEPTH, HYENA_ORDER, HYENA_W), 0.5),
    }


def reference(x_prompt, x_sample, c, cache_k, cache_v, c_ctx, w_mod, b_mod, g_ffn1, g_mix, g_ffn2,
              g_final, w1_gate, w1_up, w1_down, w2_gate, w2_up, w2_down, w_in, w_out, attn_sink,
              conv_dw, conv_dw_b, conv_ln_g, conv_ln_b, conv_pw, hy_short_w, hy_short_b,
              hy_w1, hy_b1, hy_f1, hy_w2, hy_b2, hy_f2, hy_w3, hy_log_decay, hy_bias):
    P = dict(w_mod=w_mod, b_mod=b_mod, g_ffn1=g_ffn1, g_mix=g_mix, g_ffn2=g_ffn2,
             w1_gate=w1_gate, w1_up=w1_up, w1_down=w1_down,
             w2_gate=w2_gate, w2_up=w2_up, w2_down=w2_down,
             w_in=w_in, w_out=w_out, attn_sink=attn_sink,
             conv_dw=conv_dw, conv_dw_b=conv_dw_b, conv_ln_g=conv_ln_g, conv_ln_b=conv_ln_b,
             conv_pw=conv_pw, hy_short_w=hy_short_w, hy_short_b=hy_short_b,
             hy_w1=hy_w1, hy_b1=hy_b1, hy_f1=hy_f1, hy_w2=hy_w2, hy_b2=hy_b2, hy_f2=hy_f2,
             hy_w3=hy_w3, hy_log_decay=hy_log_decay, hy_bias=hy_bias)

    xp = x_prompt
    cond_ctx = c_ctx[None, :]
    ks = []
    vs = []
    for l in range(DEPTH):
        xp, kv = trunk_layer(xp, cond_ctx, l, P, None)
        ks.append(kv[0])
        vs.append(kv[1])

    xs = x_sample
    for l in range(DEPTH):
        xs, _ = trunk_layer(xs, c, l, P, (cache_k[:, l], cache_v[:, l]))

    y_prompt = rmsnorm(xp, g_final)
    y_sample = rmsnorm(xs, g_final)
    new_k = jnp.stack(ks, axis=1)
    new_v = jnp.stack(vs, axis=1)
    return (y_prompt, y_sample, new_k, new_v)
```

```python
import contextlib
import math
import numpy as np
import ml_dtypes
import concourse.bass as bass
import concourse.mybir as mybir
from concourse.bass_utils import run_bass_kernel_spmd

F32 = mybir.dt.float32
BF16 = mybir.dt.bfloat16
AF = mybir.ActivationFunctionType
ALU = mybir.AluOpType

ENGS = ("tensor", "vector", "scalar", "gpsimd", "sync")
EPOCH = 30000
D = 1024
DFF = 2816
NFC = 22
EPS = 1e-6


class Reg:
    __slots__ = ("w", "r", "psum")

    def __init__(self, psum=False):
        self.w = None
        self.r = {}
        self.psum = psum


class Sched:
    def __init__(self, sems):
        self.free_sems = list(sems)
        self.prog = {e: [] for e in ENGS}
        self.cur = {}
        self.cnt = {}
        self.waited = {e: {} for e in ENGS}
        self.pe_sems = set()
        self.own = {e: set() for e in ENGS}
        for e in ENGS:
            self._new_epoch(e)
        self.pools = {}
        self.ninst = 0

    def _new_epoch(self, e):
        self.cur[e] = self.free_sems.pop()
        self.own[e].add(self.cur[e])
        if e == "tensor":
            self.pe_sems.add(self.cur[e])
        self.cnt[e] = 0

    def make_pool(self, q, n):
        self.pools[q] = {"sems": [self.free_sems.pop() for _ in range(n)], "cnt": [0] * n, "i": 0}

    def _need(self, eng, toks):
        need = {}
        for k, v in toks:
            if need.get(k, 0) < v:
                need[k] = v
        wd = self.waited[eng]
        for k, v in need.items():
            if eng == "tensor" and k in self.pe_sems:
                continue
            if wd.get(k, 0) < v:
                self.prog[eng].append(("wait", k, v))
                wd[k] = v

    def _deps(self, eng, reads, writes):
        toks = []
        for r in reads:
            if r.w is not None:
                toks.append(r.w)
            if r.psum:
                own = self.own[eng]
                toks.extend((k, v) for k, v in r.r.items() if k not in own)
        own = self.own[eng]
        for w in writes:
            if w.w is not None and w.w[0] not in own:
                toks.append(w.w)
            toks.extend((k, v) for k, v in w.r.items() if k not in own)
        self._need(eng, toks)

    def _commit(self, tok, reads, writes):
        for r in reads:
            if r.r.get(tok[0], 0) < tok[1]:
                r.r[tok[0]] = tok[1]
        for w in writes:
            w.w = tok
            w.r = {}

    def op(self, eng, fns, reads=(), writes=()):
        if not isinstance(fns, (list, tuple)):
            fns = [fns]
        self._deps(eng, reads, writes)
        if self.cnt[eng] >= EPOCH:
            self._new_epoch(eng)
        self.cnt[eng] += 1
        tok = (self.cur[eng], self.cnt[eng])
        for f in fns[:-1]:
            self.prog[eng].append(("inst", f, None, 0))
        self.prog[eng].append(("inst", fns[-1], tok[0], 1))
        self.ninst += len(fns)
        self._commit(tok, reads, writes)

    def dma(self, q, out, in_, reads=(), writes=()):
        self._deps(q, reads, writes)
        p = self.pools[q]
        i = p["i"]
        p["i"] = (i + 1) % len(p["sems"])
        s = p["sems"][i]
        if p["cnt"][i] > 0:
            self._need(q, [(s, 16 * p["cnt"][i])])
        p["cnt"][i] += 1
        tok = (s, 16 * p["cnt"][i])
        self.prog[q].append(("inst", lambda e, out=out, in_=in_: e.dma_start(out=out, in_=in_), s, 16))
        self.ninst += 1
        self._commit(tok, reads, writes)

    def barrier(self):
        toks = [(self.cur[e], self.cnt[e]) for e in ENGS if self.cnt[e] > 0]
        for p in self.pools.values():
            for s, c in zip(p["sems"], p["cnt"]):
                if c > 0:
                    toks.append((s, 16 * c))
        for e in ENGS:
            self._need(e, toks)

    def emit(self, block):
        for e in ENGS:
            entries = self.prog[e]

            def body(eng, entries=entries):
                for ent in entries:
                    if ent[0] == "wait":
                        eng.wait_ge(ent[1], ent[2])
                    else:
                        ins = ent[1](eng)
                        if ent[2] is not None:
                            ins.then_inc(ent[2], ent[3])
            getattr(block, e)(body)


def host_consts():
    c = {}
    c["ident"] = np.eye(128, dtype=np.float32)
    c["onesb"] = np.ones((128, 128), dtype=ml_dtypes.bfloat16)
    P = np.zeros((128, 128), np.float32)
    for hb in (0, 64):
        for part in (0, 32):
            for i in range(16):
                a = hb + part + i
                b = a + 16
                P[b, a] = -1.0
                P[a, b] = 1.0
    c["prope"] = P.astype(ml_dtypes.bfloat16)
    t = np.arange(2048)
    row = (t // 64).astype(np.float64)
    col = (t % 64).astype(np.float64)
    inv = (10000.0 ** (-np.arange(16, dtype=np.float32) / 16)).astype(np.float64)
    ang = np.zeros((64, 2048))
    ang[0:16] = inv[:, None] * row[None]
    ang[16:32] = inv[:, None] * row[None]
    ang[32:48] = inv[:, None] * col[None]
    ang[48:64] = inv[:, None] * col[None]
    cs = np.zeros((128, 2, 2048), np.float32)
    cs[:, 0] = np.tile(np.cos(ang), (2, 1))
    cs[:, 1] = np.tile(np.sin(ang), (2, 1))
    c["ropecs"] = cs.astype(ml_dtypes.bfloat16)
    kk = np.arange(128)[:, None]
    qi = np.arange(128)[None, :]
    m = np.zeros((128, 2, 128), np.float32)
    m[:, 0] = (kk >= qi)
    m[:, 1] = (kk <= qi)
    c["masks"] = m.astype(ml_dtypes.bfloat16)

    def dft(L):
        N = 2 * L
        tt = np.arange(L, dtype=np.float64)[:, None] + 0.5
        ff = np.arange(L, dtype=np.float64)[None, :] + 0.5
        th = 2 * np.pi * tt * ff / N
        return np.stack([np.cos(th), np.sin(th)]).astype(ml_dtypes.bfloat16)

    def phis(L):
        N = 2 * L
        f = np.arange(L, dtype=np.float64)
        ph = np.stack([np.cos(np.pi * (f + .5) / N), np.sin(np.pi * (f + .5) / N), (-1.0) ** f], -1)
        return np.ascontiguousarray(ph.reshape(L // 128, 128, 3).transpose(1, 0, 2)).astype(np.float32)

    def zfeat(L):
        tf = np.arange(L)
        pos = np.concatenate([tf, (L - tf) % L]).astype(np.float64)
        tn = pos / (L - 1)
        bands = np.linspace(1e-4, 15, 16)
        ang = 2 * np.pi * pos[:, None] * bands[None, :] / L
        z = np.concatenate([tn[:, None], np.cos(ang), -np.sin(ang)], -1)
        tnneg = np.ascontiguousarray((-tn).reshape(2 * L // 128, 128).T).astype(np.float32)
        return np.ascontiguousarray(z.T).astype(np.float32), tnneg

    c["dftS"] = dft(2048)
    dp = dft(256)
    c["dftP"] = np.ascontiguousarray(dp.reshape(2, 2, 128, 256).transpose(2, 0, 1, 3))
    c["phiS"] = phis(2048)
    c["phiP"] = phis(256)
    c["zfS"], c["tnS"] = zfeat(2048)
    c["zfP"], c["tnP"] = zfeat(256)
    return c


WEIGHT_NAMES = ["w_mod", "b_mod", "g_ffn1", "g_mix", "g_ffn2", "g_final", "w1_gate", "w1_up", "w1_down",
                "w2_gate", "w2_up", "w2_down", "w_in", "w_out", "attn_sink", "conv_dw", "conv_dw_b",
                "conv_ln_g", "conv_ln_b", "conv_pw", "hy_short_w", "hy_short_b", "hy_w1", "hy_b1", "hy_f1",
                "hy_w2", "hy_b2", "hy_f2", "hy_w3", "hy_log_decay", "hy_bias"]


class StopBuild(Exception):
    pass


def build_nc(shapes, cshapes, stop=None):
    nc = bass.Bass("TRN2", target_bir_lowering=False)
    dbg = nc.dram_tensor("dbg", [128, 8, 2048], F32, kind="ExternalOutput").ap() if stop is not None else None
    stage = {"n": 0}

    def chk(name):
        stage["n"] += 1
        if stop is not None:
            print("stage", stage["n"], name)
            if stage["n"] >= stop:
                raise StopBuild()

    Hd = {}
    for n, (shp, dt) in {**shapes, **cshapes}.items():
        Hd[n] = nc.dram_tensor(n, list(shp), dt, kind="ExternalInput")
    A = {n: h.ap() for n, h in Hd.items()}
    yp = nc.dram_tensor("yp", [1024, D], F32, kind="ExternalOutput").ap()
    ys = nc.dram_tensor("ys", [2048, D], F32, kind="ExternalOutput").ap()
    nk = nc.dram_tensor("nk", [4, 2, 256, 128], F32, kind="ExternalOutput").ap()
    nv = nc.dram_tensor("nv", [4, 2, 256, 128], F32, kind="ExternalOutput").ap()

    with contextlib.ExitStack() as es:
        def sb(name, shape, dt):
            return es.enter_context(nc.sbuf_tensor("sb_" + name, shape, dt))
        sems = [es.enter_context(nc.semaphore(f"s{i}")) for i in range(100)]
        S = Sched(sems)
        S.make_pool("sync", 16)
        S.make_pool("gpsimd", 16)
        S.make_pool("scalar", 8)

        xT = sb("xT", [128, 8, 2048], F32)
        ident = sb("ident", [128, 128], F32)
        onesb = sb("onesb", [128, 128], BF16)
        prope = sb("prope", [128, 128], BF16)
        masks = sb("masks", [128, 2, 128], BF16)
        dftP = sb("dftP", [128, 2, 2, 256], BF16)
        phiS = sb("phiS", [128, 16, 3], F32)
        phiP = sb("phiP", [128, 2, 3], F32)
        tnS = sb("tnS", [128, 32], F32)
        tnP = sb("tnP", [128, 4], F32)
        COEF = sb("COEF", [128, 2, 2, 9, 8], F32)
        GT = sb("GT", [128, 1, 56], F32)
        MODT = sb("MODT", [128, 2, 72, 2], F32)
        epsc = sb("epsc", [128, 1], F32)
        identb = sb("identb", [128, 128], BF16)
        WSW = 32 * 1024 + 1536
        WS = sb("WS", [128, WSW], F32)
        PS = [es.enter_context(nc.psum_tensor(f"ps{i}", [128, 512], F32)) for i in range(8)]
        PR = [Reg(psum=True) for _ in range(8)]
        st = {"bank": 0, "off": 0, "flip": 0}
        RC = Reg()

        def nb():
            i = st["bank"]
            st["bank"] = (i + 1) % 7
            return PS[i], PR[i]

        def arena_reset():
            S.barrier()
            st["off"] = 0
            st["lim"] = WSW

        def mkview(off, words, shape, dt):
            v = WS[:, off:off + words]
            if dt != F32:
                v = v.bitcast(BF16)
            if len(shape) == 2:
                v = v.rearrange("p (a b) -> p a b", a=shape[0])
            return v

        def alloc_top(shape, dt):
            n = int(np.prod(shape))
            words = n if dt == F32 else (n + 1) // 2
            st["lim"] = WSW - words
            return mkview(WSW - words, words, shape, dt)

        def alloc(shape, dt):
            n = int(np.prod(shape))
            words = n if dt == F32 else (n + 1) // 2
            off = st["off"]
            st["off"] = off + words
            assert st["off"] <= st["lim"], ("arena overflow", st["off"], st["lim"])
            v = WS[:, off:off + words]
            if dt != F32:
                v = v.bitcast(BF16)
            if len(shape) == 2:
                v = v.rearrange("p (a b) -> p a b", a=shape[0])
            elif len(shape) == 3:
                v = v.rearrange("p (a b c) -> p a b c", a=shape[0], b=shape[1])
            elif len(shape) == 4:
                v = v.rearrange("p (a b c d) -> p a b c d", a=shape[0], b=shape[1], c=shape[2])
            return v

        def mm(out, pairs, reads, writes):
            n = len(pairs)
            fns = []
            for i, (l, r) in enumerate(pairs):
                fns.append(lambda e, l=l, r=r, s0=(i == 0), s1=(i == n - 1): e.matmul(out, lhsT=l, rhs=r, start=s0, stop=s1))
            S.op("tensor", fns, reads, writes)

        def tr(out, in_, reads, writes):
            k = in_.shape[0]
            S.op("tensor", lambda e: e.transpose(out, in_, ident[0:k, 0:k]), list(reads) + [RC], writes)

        def act(out, in_, func, reads, writes, scale=1.0, bias=None, accum=None):
            kw = {}
            if bias is not None:
                kw["bias"] = bias
            if accum is not None:
                kw["accum_out"] = accum
            S.op("scalar", lambda e: e.activation(out=out, in_=in_, func=func, scale=scale, **kw), reads, writes)

        def tt(eng, out, in0, in1, op, reads, writes):
            S.op(eng, lambda e: e.tensor_tensor(out=out, in0=in0, in1=in1, op=op), reads, writes)

        def ts(eng, out, in0, s1, op0, reads, writes, s2=None, op1=None):
            if op1 is None:
                S.op(eng, lambda e: e.tensor_scalar(out=out, in0=in0, scalar1=s1, scalar2=None, op0=op0), reads, writes)
            else:
                S.op(eng, lambda e: e.tensor_scalar(out=out, in0=in0, scalar1=s1, scalar2=s2, op0=op0, op1=op1), reads, writes)

        def stt(out, in0, scalar, in1, op0, op1, reads, writes):
            S.op("vector", lambda e: e.scalar_tensor_tensor(out=out, in0=in0, scalar=scalar, in1=in1, op0=op0, op1=op1), reads, writes)

        def cp(eng, out, in_, reads, writes):
            if eng == "scalar":
                act(out, in_, AF.Copy, reads, writes)
            else:
                S.op(eng, lambda e: e.tensor_copy(out=out, in_=in_), reads, writes)

        def evac(out, in_, reads, writes):
            st["flip"] ^= 1
            cp("scalar" if st["flip"] else "vector", out, in_, reads, writes)

        def recip(ap, reg):
            S.op("vector", lambda e: e.reciprocal(out=ap, in_=ap), [reg], [reg])

        def memset(eng, ap, val, writes):
            S.op(eng, lambda e: e.memset(ap, val), (), writes)

        def dap(name, off, pat):
            return bass.AP(Hd[name], off, pat)

        def panel(dst, dreg, W2d, c0, ncols):
            S.dma("gpsimd", dst, W2d[:, c0:c0 + ncols].rearrange("(k p) c -> p k c", p=128), (), [dreg])

        for nm, t_ in [("ident", ident), ("onesb", onesb), ("prope", prope), ("masks", masks), ("dftP", dftP),
                       ("phiS", phiS), ("phiP", phiP), ("tnS", tnS), ("tnP", tnP)]:
            S.dma("sync", t_[:], A[nm], (), [RC])
        memset("vector", epsc[:], EPS, [RC])
        cp("vector", identb[:], ident[:], [RC], [RC])

        def loadT(pieces, W, out, oreg):
            R = sum(p.shape[0] for p in pieces)
            stg = alloc([W], F32)
            sreg = Reg()
            r0 = 0
            for p in pieces:
                S.dma("sync", stg[r0:r0 + p.shape[0], :], p, (), [sreg])
                r0 += p.shape[0]
            for k in range(W // 128):
                pt, pr = nb()
                tr(pt[:, 0:R], stg[0:R, k * 128:(k + 1) * 128], [sreg], [pr])
                cp("vector", out[:, k, 0:R], pt[:, 0:R], [pr], [oreg])

        arena_reset()
        RX = Reg()
        ROUT = Reg()
        xin0 = [alloc([D], F32) for _ in range(2)]
        xr0 = [Reg() for _ in range(2)]
        for b in range(2048 // 128):
            i = b % 2
            S.dma("sync", xin0[i], A["xs"][b * 128:(b + 1) * 128, :], (), [xr0[i]])
            for hf in range(2):
                pt, pr = nb()
                for c4 in range(4):
                    c = hf * 4 + c4
                    tr(pt[:, c4 * 128:(c4 + 1) * 128], xin0[i][:, c * 128:(c + 1) * 128], [xr0[i]], [pr])
                evac(xT[:, hf * 4:hf * 4 + 4, b * 128:(b + 1) * 128], pt[:].rearrange("p (a b) -> p a b", a=4), [pr], [RX])
        gp = [A[n].rearrange("l (c p) -> (l c) p", p=128) for n in ("g_ffn1", "g_mix", "g_ffn2")]
        gp.append(A["g_final"].rearrange("(c p) -> c p", p=128))
        loadT(gp, 128, GT, RC)
        cst = alloc([D], F32)
        creg = Reg()
        S.dma("sync", cst[0:2, :], A["cc"], (), [creg])
        csl = alloc([D], F32)
        act(csl[0:2, :], cst[0:2, :], AF.Silu, [creg], [creg])
        scT = alloc([8, 2], BF16)
        for k in range(8):
            pt, pr = nb()
            tr(pt[:, 0:2], csl[0:2, k * 128:(k + 1) * 128], [creg], [pr])
            cp("vector", scT[:, k, :], pt[:, 0:2], [pr], [creg])
        bmT = alloc([2, 72], F32)
        for l in range(2):
            loadT([A["b_mod"][l].rearrange("(j p) -> j p", p=128)], 128, bmT[:, l:l + 1, :], creg)
        wmp = [alloc([8, 512], BF16) for _ in range(3)]
        wmr = [Reg() for _ in range(3)]
        RM = Reg()
        for l in range(2):
            for jb in range(18):
                i = (l * 18 + jb) % 3
                panel(wmp[i], wmr[i], A["w_mod"][l], jb * 512, 512)
                pt, pr = nb()
                for jc in range(4):
                    mm(pt[:, jc * 2:jc * 2 + 2],
                       [(wmp[i][:, k, jc * 128:(jc + 1) * 128], scT[:, k, :]) for k in range(8)],
                       [wmr[i], creg], [pr])
                j0 = jb * 4
                tt("vector", MODT[:, l, j0:j0 + 4, :], pt[:, 0:8].rearrange("p (a b) -> p a b", a=4),
                   bmT[:, l, j0:j0 + 4].unsqueeze(2).broadcast_to([128, 4, 2]), ALU.add, [pr, creg], [RM])
        for l in range(2):
            for j in range(2):
                def mv(k):
                    return MODT[:, l, k * 8:(k + 1) * 8, j]
                for blk, goff, half in ((0, 0, 0.5), (1, 16, 1.0), (2, 32, 0.5)):
                    gv = GT[:, 0, goff + l * 8: goff + l * 8 + 8]
                    stt(COEF[:, l, j, blk * 3 + 0, :], mv(blk * 3 + 1), 1.0, gv, ALU.add, ALU.mult, [RM, RC], [RM])
                    cp("vector", COEF[:, l, j, blk * 3 + 1, :], mv(blk * 3 + 0), [RM], [RM])
                    ts("vector", COEF[:, l, j, blk * 3 + 2, :], mv(blk * 3 + 2), half, ALU.mult, [RM], [RM])

        def load_x(xd, T, reset=True):
            if reset:
                arena_reset()
            xin = [alloc([D], F32) for _ in range(2)]
            xr = [Reg() for _ in range(2)]
            for b in range(T // 128):
                i = b % 2
                S.dma("sync", xin[i], xd[b * 128:(b + 1) * 128, :], (), [xr[i]])
                for hf in range(2):
                    pt, pr = nb()
                    for c4 in range(4):
                        c = hf * 4 + c4
                        tr(pt[:, c4 * 128:(c4 + 1) * 128], xin[i][:, c * 128:(c + 1) * 128], [xr[i]], [pr])
                    evac(xT[:, hf * 4:hf * 4 + 4, b * 128:(b + 1) * 128], pt[:].rearrange("p (a b) -> p a b", a=4), [pr], [RX])

        def pipeline(n, phases):
            for step in range(n + len(phases) - 1):
                for p, f in enumerate(phases):
                    i = step - p
                    if 0 <= i < n:
                        f(i)

        def norm_A(t0, n, sc_):
            sq, tmp, rt = sc_["sq"], sc_["tmp"], sc_["rt"]
            act(sq[:, :, 0:n], xT[:, :, t0:t0 + n], AF.Square, [RX], [sc_["r"]])
            pt, pr = nb()
            mm(pt[:, 0:n], [(onesb[:], sq[:, c, 0:n]) for c in range(8)], [RC, sc_["r"]], [pr])
            act(rt[:, 0:n], pt[:, 0:n], AF.Sqrt, [pr, RC], [sc_["r2"]], scale=1.0 / D, bias=epsc[:, 0:1])
            recip(rt[:, 0:n], sc_["r2"])
            tt("vector", tmp[:, :, 0:n], xT[:, :, t0:t0 + n], rt[:, 0:n].unsqueeze(1).broadcast_to([128, 8, n]), ALU.mult,
               [RX, sc_["r2"]], [sc_["r3"]])

        def norm_B(n, Aco, Bco, hdst, hreg, sc_):
            tmp = sc_["tmp"]
            for c in range(8):
                if Bco is None:
                    ts("vector", hdst[:, c, 0:n], tmp[:, c, 0:n], Aco[:, c:c + 1], ALU.mult, [sc_["r3"], RM, RC], [hreg])
                else:
                    act(hdst[:, c, 0:n], tmp[:, c, 0:n], AF.Identity, [sc_["r3"], RM], [hreg], scale=Aco[:, c:c + 1], bias=Bco[:, c:c + 1])

        def norm_many(items, scr, extra=None):
            sets = scr["sets"]
            ph = [lambda i: norm_A(items[i][0], items[i][1], sets[i % 2]),
                  lambda i: norm_B(items[i][1], items[i][2], items[i][3], items[i][4], items[i][5], sets[i % 2])]
            if extra is not None:
                ph.append(extra)
            pipeline(len(items), ph)

        def norm_scratch():
            return {"i": 0, "sets": [{"sq": alloc([8, 256], BF16), "tmp": alloc([8, 256], F32), "rt": alloc([256], F32),
                                      "r": Reg(), "r2": Reg(), "r3": Reg()} for _ in range(2)]}

        def ffn(l, which, j, T):
            Wg = A["w%d_gate" % which][l]
            Wu = A["w%d_up" % which][l]
            Wd = A["w%d_down" % which][l]
            kb = 0 if which == 1 else 6
            Aco, Bco, Gco = COEF[:, l, j, kb + 0, :], COEF[:, l, j, kb + 1, :], COEF[:, l, j, kb + 2, :]
            arena_reset()
            scr = norm_scratch()
            hT = alloc([8, 1024], BF16)
            actb = alloc([NFC, 1024], BF16)
            NWB = 3
            wg = [alloc([8, 256], BF16) for _ in range(NWB)]
            wu = [alloc([8, 256], BF16) for _ in range(NWB)]
            wd = [alloc([NFC, 128], BF16) for _ in range(2)]
            sg = [alloc([512], F32) for _ in range(2)]
            wgr = [Reg() for _ in range(NWB)]
            wur = [Reg() for _ in range(NWB)]
            wdr = [Reg() for _ in range(2)]
            sgr = [Reg() for _ in range(2)]
            cnt = 0
            for u0 in range(0, T, 1024):
                hreg = Reg()
                areg = [Reg() for _ in range(NFC)]
                norm_many([(u0 + s * 256, 256, Aco, Bco, hT[:, :, s * 256:(s + 1) * 256], hreg) for s in range(4)], scr)
                for fb in range(11):
                    i = fb % NWB
                    panel(wg[i], wgr[i], Wg, fb * 256, 256)
                    panel(wu[i], wur[i], Wu, fb * 256, 256)
                    for f2 in range(2):
                        fc = fb * 2 + f2
                        for tl in range(2):
                            pg, pgr = nb()
                            pu, pur = nb()
                            rhs = [hT[:, k, tl * 512:(tl + 1) * 512] for k in range(8)]
                            mm(pg[:], [(wg[i][:, k, f2 * 128:(f2 + 1) * 128], rhs[k]) for k in range(8)], [wgr[i], hreg], [pgr])
                            mm(pu[:], [(wu[i][:, k, f2 * 128:(f2 + 1) * 128], rhs[k]) for k in range(8)], [wur[i], hreg], [pur])
                            si = cnt % 2
                            cnt += 1
                            act(sg[si], pg[:], AF.Silu, [pgr], [sgr[si]])
                            tt("vector", actb[:, fc, tl * 512:(tl + 1) * 512], sg[si], pu[:], ALU.mult, [sgr[si], pur], [areg[fc]])
                for dc in range(8):
                    i = dc % 2
                    S.dma("gpsimd", wd[i], Wd[:, dc * 128:(dc + 1) * 128].rearrange("(k p) c -> p k c", p=128), (), [wdr[i]])
                    for tl in range(2):
                        pt, pr = nb()
                        mm(pt[:], [(wd[i][:, k, :], actb[:, k, tl * 512:(tl + 1) * 512]) for k in range(NFC)],
                           [wdr[i]] + areg, [pr])
                        xs_ = xT[:, dc, u0 + tl * 512:u0 + (tl + 1) * 512]
                        stt(xs_, pt[:], Gco[:, dc:dc + 1], xs_, ALU.mult, ALU.add, [pr, RM, RX], [RX])

        def inproj_fm(wpanel_ap, wreg, hT, hreg, T, dst_fn, dreg):
            for tl in range(T // 512):
                pt, pr = nb()
                mm(pt[:], [(wpanel_ap[:, k, :], hT[:, k, tl * 512:(tl + 1) * 512]) for k in range(8)], [wreg, hreg], [pr])
                d = dst_fn(tl)
                src = pt[:]
                if len(d.shape) == 3:
                    src = src.rearrange("p (a b) -> p a b", a=d.shape[1])
                evac(d, src, [pr], [dreg])

        def out_accum(l, j, srcT, sreg, row0, nch, T):
            Gco = COEF[:, l, j, 5, :]
            wo = alloc([nch, D], BF16)
            wor = Reg()
            S.dma("gpsimd", wo, A["w_out"][l][row0:row0 + nch * 128, :].rearrange("(k p) c -> p k c", p=128), (), [wor])
            for dc in range(8):
                for tl in range(T // 512):
                    pt, pr = nb()
                    mm(pt[:], [(wo[:, k, dc * 128:(dc + 1) * 128], srcT[:, k, tl * 512:(tl + 1) * 512]) for k in range(nch)],
                       [wor, sreg], [pr])
                    xs_ = xT[:, dc, tl * 512:(tl + 1) * 512]
                    stt(xs_, pt[:], Gco[:, dc:dc + 1], xs_, ALU.mult, ALU.add, [pr, RM, RX], [RX])

        def mix_attention(l, j, grp, hT, hreg):
            T, NSEQ, L = grp["T"], grp["NSEQ"], grp["L"]
            sample = grp["sample"]
            Win = A["w_in"][l]
            qT = alloc([4, T], BF16)
            kT = alloc([4, T], BF16)
            nblk = T // 128
            vo = alloc([nblk, 2, 128], BF16)
            attT = alloc([4, T], BF16)
            Eall = [alloc([512], BF16) for _ in range(10)]
            Erall = [Reg() for _ in range(10)]
            dnall = [alloc([512], F32) for _ in range(2)]
            dnrall = [Reg() for _ in range(2)]
            ESb = alloc([8], F32)
            sinkL = alloc([128], BF16)
            esfull = alloc([8, 128], BF16)
            pmark = st["off"]
            wq = [alloc([8, 256], BF16) for _ in range(2)]
            wkv = alloc([8, 256], BF16)
            wkb = alloc([8, 128], BF16)
            wr = [Reg() for _ in range(4)]
            qr, kr, vr, ar, esr = Reg(), Reg(), Reg(), Reg(), Reg()
            S.dma("sync", ESb, dap("attn_sink", l * 8, [[0, 128], [1, 8]]), (), [esr])
            act(ESb, ESb, AF.Exp, [esr], [esr])
            memset("vector", sinkL, 0.0, [esr])
            memset("vector", sinkL[0:1, 64:128], 1.0, [esr])
            memset("vector", esfull, 0.0, [esr])
            cp("vector", esfull[0:1, :, :], ESb[0:1, :].unsqueeze(2).broadcast_to([1, 8, 128]), [esr], [esr])
            memset("vector", vo[:, :, :, 64:128], 1.0, [vr])
            for pi in range(2):
                panel(wq[pi], wr[pi], Win, pi * 256, 256)
            panel(wkv, wr[2], Win, 512, 256)
            S.dma("gpsimd", wkb[:, :, 0:64], Win[:, 576:640].rearrange("(k p) c -> p k c", p=128), (), [wr[3]])
            S.dma("gpsimd", wkb[:, :, 64:128], Win[:, 512:576].rearrange("(k p) c -> p k c", p=128), (), [wr[3]])
            for c in range(4):
                inproj_fm(wq[c // 2][:, :, (c % 2) * 128:(c % 2 + 1) * 128], wr[c // 2], hT, hreg, T,
                          lambda tl, c=c: qT[:, c, tl * 512:(tl + 1) * 512], qr)
            memset("gpsimd", kT, 0.0, [kr])
            for wpan, wreg_, vtop, vbot in ((wkv[:, :, 0:128], wr[2], 0, 3), (wkb[:, :, :], wr[3], 2, 1)):
                for tl in range(T // 512):
                    pt, pr = nb()
                    mm(pt[:], [(wpan[:, k, :], hT[:, k, tl * 512:(tl + 1) * 512]) for k in range(8)], [wreg_, hreg], [pr])
                    cp("scalar", kT[0:64, vtop, tl * 512:(tl + 1) * 512], pt[0:64, :], [pr], [kr])
                    cp("scalar", kT[64:128, vbot, tl * 512:(tl + 1) * 512], pt[64:128, :], [pr], [kr])
            chk("att_inproj")
            if not sample:
                kvf = [alloc([256], F32) for _ in range(2)]
                kvr = [Reg() for _ in range(2)]
            for b in range(nblk):
                pt, pr = nb()
                if sample:
                    mm(pt[:, 0:128], [(hT[:, k, b * 128:(b + 1) * 128], wkv[:, k, 128:256]) for k in range(8)], [hreg, wr[2]], [pr])
                    evac(vo[:, b, :, 0:64], pt[:, 0:128].rearrange("p (g d) -> p g d", g=2), [pr], [vr])
                else:
                    mm(pt[:, 0:256], [(hT[:, k, b * 128:(b + 1) * 128], wkv[:, k, :]) for k in range(8)], [hreg, wr[2]], [pr])
                    i = b % 2
                    cp("scalar", kvf[i], pt[:, 0:256], [pr], [kvr[i]])
                    cp("vector", vo[:, b, :, 0:64], kvf[i][:, 128:256].rearrange("p (g d) -> p g d", g=2), [kvr[i]], [vr])
                    s_, bl = b // 2, b % 2
                    import os
                    if not os.environ.get("NO_KVDMA"):
                        S.dma("sync", nk[s_, l, bl * 128:(bl + 1) * 128, :], kvf[i][:, 0:128], [kvr[i]], [ROUT])
                        S.dma("sync", nv[s_, l, bl * 128:(bl + 1) * 128, :], kvf[i][:, 128:256], [kvr[i]], [ROUT])
            chk("att_kvtok")
            if sample:
                S.barrier()
                st["off"] = pmark
                cs = alloc([2, 2048], BF16)
                csr = Reg()
                S.dma("sync", cs, A["ropecs"], (), [csr])
                t12 = [(alloc([512], F32), alloc([512], F32), Reg(), Reg()) for _ in range(2)]
                targets = [(qT[:, c, :], Reg()) for c in range(4)] + [(kT[:, v_, :], Reg()) for v_ in range(4)]
                items_ = [(tgt, treg, tl) for tgt, treg in targets for tl in range(4)]

                def rope_p0(i_):
                    tgt, treg, tl = items_[i_]
                    t1, t2, t1r, t2r = t12[i_ % 2]
                    sl = slice(tl * 512, (tl + 1) * 512)
                    pt, pr = nb()
                    mm(pt[:], [(prope[:], tgt[:, sl])], [RC, treg], [pr])
                    tt("vector", t1, pt[:], cs[:, 1, sl], ALU.mult, [pr, csr], [t1r])
                    tt("gpsimd", t2, tgt[:, sl], cs[:, 0, sl], ALU.mult, [treg, csr], [t2r])

                def rope_p1(i_):
                    tgt, treg, tl = items_[i_]
                    t1, t2, t1r, t2r = t12[i_ % 2]
                    sl = slice(tl * 512, (tl + 1) * 512)
                    tt("vector", tgt[:, sl], t1, t2, ALU.add, [t1r, t2r], [treg, qr if i_ < 16 else kr])
                pipeline(len(items_), [rope_p0, rope_p1])
                chk("att_rope")
                kcT = alloc([4, 256], BF16)
                vco = alloc([2, 2, 128], BF16)
                kcr, vcr = Reg(), Reg()
                memset("vector", vco[:, :, :, 64:128], 1.0, [vcr])
                memset("vector", kcT, 0.0, [kcr])
                ckf = alloc([2, 2, 128], F32)
                ckr = Reg()
                for blk in range(2):
                    S.dma("sync", ckf[:, 0, blk, :], A["ck"][l, blk * 128:(blk + 1) * 128, :], (), [ckr])
                    S.dma("sync", ckf[:, 1, blk, 0:64], A["ck"][l, blk * 128:(blk + 1) * 128, 64:128], (), [ckr])
                    S.dma("sync", ckf[:, 1, blk, 64:128], A["ck"][l, blk * 128:(blk + 1) * 128, 0:64], (), [ckr])
                    S.dma("gpsimd", vco[:, blk, :, 0:64], A["cv"][l, blk * 128:(blk + 1) * 128, :].rearrange("p (g d) -> p g d", g=2), (), [vcr])
                for var in range(2):
                    for blk in range(2):
                        pt, pr = nb()
                        tr(pt[:, 0:128], ckf[:, var, blk, :], [ckr], [pr])
                        vtop, vbot = (0, 3) if var == 0 else (2, 1)
                        cp("vector", kcT[0:64, vtop, blk * 128:(blk + 1) * 128], pt[0:64, 0:128], [pr], [kcr])
                        cp("vector", kcT[64:128, vbot, blk * 128:(blk + 1) * 128], pt[64:128, 0:128], [pr], [kcr])
            chk("att_cache")
            nqb = T // 128
            itst = {}

            def att_p0(it):
                qb, g = it // 2, it % 2
                par_ = it % 2
                E = Eall[par_ * 5:par_ * 5 + 5]
                Er = Erall[par_ * 5:par_ * 5 + 5]
                keys = []
                if sample:
                    if qb >= 1:
                        keys.append((kT, kr, (qb - 1) * 128, vo[:, qb - 1, g, :], vr, 0))
                    if qb + 1 < nqb:
                        keys.append((kT, kr, (qb + 1) * 128, vo[:, qb + 1, g, :], vr, 1))
                    keys.append((kT, kr, qb * 128, vo[:, qb, g, :], vr, None))
                    for blk in range(2):
                        keys.append((kcT, kcr, blk * 128, vco[:, blk, g, :], vcr, None))
                else:
                    s_ = qb // 2
                    for blk in range(2):
                        kb_ = s_ * 2 + blk
                        keys.append((kT, kr, kb_ * 128, vo[:, kb_, g, :], vr, None))
                itst[it] = keys
                for ki, (ksrc, ksr, k0, vap, vreg, mk) in enumerate(keys):
                    pt, pr = nb()
                    for r in range(4):
                        h = g * 4 + r
                        var = g * 2 + (h % 2)
                        mm(pt[:, r * 128:(r + 1) * 128],
                           [(ksrc[:, var, k0:k0 + 128], qT[:, h // 2, qb * 128:(qb + 1) * 128])],
                           [ksr, qr], [pr])
                    act(E[ki], pt[:], AF.Exp, [pr], [Er[ki]], scale=0.125)
                    if mk is not None:
                        ev = E[ki].rearrange("p (r q) -> p r q", r=4)
                        tt("gpsimd", ev, ev, masks[:, mk, :].unsqueeze(1).broadcast_to([128, 4, 128]), ALU.mult,
                           [Er[ki], RC], [Er[ki]])

            def att_p1(it):
                qb, g = it // 2, it % 2
                par_ = it % 2
                E = Eall[par_ * 5:par_ * 5 + 5]
                Er = Erall[par_ * 5:par_ * 5 + 5]
                dn, dnr = dnall[par_], dnrall[par_]
                keys = itst.pop(it)
                po, por = nb()
                mm(po[:], [(keys[ki][3], E[ki]) for ki in range(len(keys))] +
                   [(sinkL, esfull[:, g * 4:(g + 1) * 4, :].rearrange("p a b -> p (a b)"))],
                   [Er[ki] for ki in range(len(keys))] + [keys[0][4], keys[-1][4], esr], [por])
                act(dn[64:128, :], po[64:128, :], AF.Ln, [por], [dnr])
                act(dn[64:128, :], dn[64:128, :], AF.Exp, [dnr], [dnr], scale=-1.0)
                for r in range(4):
                    h = g * 4 + r
                    base = (h % 2) * 64
                    tt("vector", attT[base:base + 64, h // 2, qb * 128:(qb + 1) * 128], po[0:64, r * 128:(r + 1) * 128],
                       dn[64:128, r * 128:(r + 1) * 128], ALU.mult, [por, dnr], [ar])
            pipeline(nqb * 2, [att_p0, att_p1])
            chk("att_main")
            out_accum(l, j, attT, ar, 0, 4, T)

        def make_diag(Dm, taps, ntap, treg, dreg):
            tt("vector", Dm, identb[:].unsqueeze(1).broadcast_to([128, ntap, 128]),
               taps.unsqueeze(2).broadcast_to([128, ntap, 128]), ALU.mult, [RC, treg], [dreg])

        def mix_conv(l, j, grp, hT, hreg):
            T, NSEQ, L = grp["T"], grp["NSEQ"], grp["L"]
            Win = A["w_in"][l]
            cw = alloc([2, 34], F32)
            cwr = Reg()
            loadT([A["conv_dw"][l], A["conv_dw_b"][l:l + 1, :], A["conv_ln_g"][l:l + 1, :], A["conv_ln_b"][l:l + 1, :]], 256, cw, cwr)
            Dm = alloc([2, 31, 128], BF16)
            dmr = Reg()
            for c in range(2):
                make_diag(Dm[:, c], cw[:, c, 0:31], 31, cwr, dmr)
            pw = alloc([2, 256], BF16)
            pwr = Reg()
            S.dma("gpsimd", pw, A["conv_pw"][l].rearrange("(k p) c -> p k c", p=128), (), [pwr])
            wp = [alloc([8, 256], BF16) for _ in range(2)]
            wpr = [Reg() for _ in range(2)]
            for pi in range(2):
                panel(wp[pi], wpr[pi], Win, 768 + pi * 256, 256)
            zc = alloc([4, T], BF16)
            zr = Reg()
            for c in range(4):
                inproj_fm(wp[c // 2][:, :, (c % 2) * 128:(c % 2 + 1) * 128], wpr[c // 2], hT, hreg, T,
                          lambda tl, c=c: zc[:, c, tl * 512:(tl + 1) * 512], zr)
            LP = L + 30
            ypad = alloc([2, NSEQ, LP], BF16)
            yr = Reg()
            memset("vector", ypad, 0.0, [yr])
            sgm = alloc([T], BF16)
            sr = Reg()
            for c in range(2):
                act(sgm, zc[:, 2 + c, :], AF.Sigmoid, [zr], [sr])
                tt("vector", ypad[:, c, :, 15:15 + L], zc[:, c, :].rearrange("p (s t) -> p s t", s=NSEQ),
                   sgm.rearrange("p (s t) -> p s t", s=NSEQ), ALU.mult, [zr, sr], [yr])
            n = min(L, 512)
            sets = []
            for _ in range(2):
                sets.append((alloc([2, n], F32), alloc([2, n], BF16), alloc([2, n], BF16), alloc([n], F32), alloc([n], F32),
                             alloc([2, n], BF16), Reg(), Reg(), Reg(), Reg()))
            cvT = alloc([2, T], BF16)
            cvr = Reg()
            tiles_ = [(s_, t0) for s_ in range(NSEQ) for t0 in range(0, L, n)]

            def conv_p0(ti):
                s_, t0 = tiles_[ti]
                yc, ycb, ysq, mean, var, sT, r1, r2, r3, r4 = sets[ti % 2]
                for c in range(2):
                    pt, pr = nb()
                    mm(pt[:, 0:n], [(Dm[:, c, k, :], ypad[:, c, s_, t0 + k:t0 + k + n]) for k in range(31)], [dmr, yr], [pr])
                    act(yc[:, c, :], pt[:, 0:n], AF.Identity, [pr, cwr], [r1], bias=cw[:, c, 31:32])
                cp("vector", ycb, yc, [r1], [r2])
                act(ysq, yc, AF.Square, [r1], [r2])

            def conv_p1(ti):
                s_, t0 = tiles_[ti]
                yc, ycb, ysq, mean, var, sT, r1, r2, r3, r4 = sets[ti % 2]
                p1, p1r = nb()
                p2, p2r = nb()
                mm(p1[:, 0:n], [(onesb[:], ycb[:, c, :]) for c in range(2)], [RC, r2], [p1r])
                mm(p2[:, 0:n], [(onesb[:], ysq[:, c, :]) for c in range(2)], [RC, r2], [p2r])
                act(mean, p1[:, 0:n], AF.Copy, [p1r], [r3], scale=1.0 / 256)
                tt("vector", var, mean, mean, ALU.mult, [r3], [r3])
                stt(var, p2[:, 0:n], 1.0 / 256, var, ALU.mult, ALU.subtract, [p2r, r3], [r3])
                act(var, var, AF.Sqrt, [r3, RC], [r3], bias=epsc[:, 0:1])
                recip(var, r3)
                for c in range(2):
                    tt("vector", yc[:, c, :], yc[:, c, :], mean, ALU.subtract, [r1, r3], [r1])
                    tt("vector", yc[:, c, :], yc[:, c, :], var, ALU.mult, [r1, r3], [r1])
                    act(sT[:, c, :], yc[:, c, :], AF.Silu, [r1, cwr], [r4], scale=cw[:, c, 32:33], bias=cw[:, c, 33:34])
                for co in range(2):
                    pt, pr = nb()
                    mm(pt[:, 0:n], [(pw[:, ci, co * 128:(co + 1) * 128], sT[:, ci, :]) for ci in range(2)], [pwr, r4], [pr])
                    evac(cvT[:, co, s_ * L + t0:s_ * L + t0 + n], pt[:, 0:n], [pr], [cvr])
            pipeline(len(tiles_), [conv_p0, conv_p1])
            out_accum(l, j, cvT, cvr, 512, 2, T)

        def mix_hyena(l, j, grp, hT, hreg):
            T, NSEQ, L = grp["T"], grp["NSEQ"], grp["L"]
            sample = grp["sample"]
            Win = A["w_in"][l]
            NTB = L // 128
            zfn, tn, phi = ("zfS", tnS, phiS) if sample else ("zfP", tnP, phiP)
            NR = min(L, 512)
            hw_ = alloc([6, 4], F32)
            hwr = Reg()
            loadT([A["hy_short_w"][l], A["hy_short_b"][l:l + 1, :]], 768, hw_, hwr)
            hb_ = alloc([2, 2], F32)
            loadT([A["hy_bias"][l]], 256, hb_, hwr)
            Dm = alloc([6, 3, 128], BF16)
            dmr = Reg()
            for c in range(6):
                make_diag(Dm[:, c], hw_[:, c, 0:3], 3, hwr, dmr)
            fw = alloc([64 + 64 + 1024 + 8], F32)
            fr = Reg()
            S.dma("sync", fw[0:33, 0:64], A["hy_w1"][l], (), [fr])
            S.dma("sync", fw[0:64, 64:128], A["hy_w2"][l], (), [fr])
            S.dma("sync", fw[0:64, 128:1152], A["hy_w3"][l], (), [fr])
            for i_, nm in enumerate(["hy_f1", "hy_b1", "hy_f2", "hy_b2"]):
                S.dma("sync", fw[0:64, 1152 + i_:1153 + i_], dap(nm, l * 64, [[1, 64], [1, 1]]), (), [fr])
            sc = alloc([8], F32)
            scr_ = Reg()
            for m_ in range(2):
                f_ = fw[0:64, 1152 + 2 * m_:1153 + 2 * m_]
                b_ = fw[0:64, 1153 + 2 * m_:1154 + 2 * m_]
                ts("vector", sc[0:64, 4 * m_ + 0:4 * m_ + 1], f_, 0.5, ALU.mult, [fr], [scr_])
                ts("vector", sc[0:64, 4 * m_ + 2:4 * m_ + 3], f_, 0.25, ALU.mult, [fr], [scr_])
                tt("vector", sc[0:64, 4 * m_ + 1:4 * m_ + 2], sc[0:64, 4 * m_ + 0:4 * m_ + 1], b_, ALU.mult, [fr, scr_], [scr_])
                tt("vector", sc[0:64, 4 * m_ + 3:4 * m_ + 4], sc[0:64, 4 * m_ + 2:4 * m_ + 3], b_, ALU.mult, [fr, scr_], [scr_])
            dec = alloc([1024], F32)
            dcr = Reg()
            S.dma("sync", dec, dap("hy_log_decay", l * 1024, [[0, 128], [1, 1024]]), (), [dcr])
            act(dec, dec, AF.Exp, [dcr], [dcr])
            vvx = alloc([6, T], BF16)
            vxr = Reg()
            mark = st["off"]
            upad = alloc([6, NSEQ, L + 2], BF16)
            ur = Reg()
            memset("vector", upad, 0.0, [ur])
            wp = [alloc([8, 256], BF16) for _ in range(3)]
            wpr = [Reg() for _ in range(3)]
            for pi in range(3):
                panel(wp[pi], wpr[pi], Win, 1280 + pi * 256, 256)
            for c in range(6):
                inproj_fm(wp[c // 2][:, :, (c % 2) * 128:(c % 2 + 1) * 128], wpr[c // 2], hT, hreg, T,
                          (lambda tl, c=c: upad[:, c, tl * (512 // L):(tl + 1) * (512 // L), 1:1 + L]) if L < 512 else
                          (lambda tl, c=c: upad[:, c, 0, 1 + tl * 512:1 + (tl + 1) * 512]), ur)
            n = min(L, 512)
            for c in range(6):
                for s_ in range(NSEQ):
                    for t0 in range(0, L, n):
                        pt, pr = nb()
                        mm(pt[:, 0:n], [(Dm[:, c, k, :], upad[:, c, s_, t0 + k:t0 + k + n]) for k in range(3)], [dmr, ur], [pr])
                        act(vvx[:, c, s_ * L + t0:s_ * L + t0 + n], pt[:, 0:n], AF.Identity, [pr, hwr], [vxr], bias=hw_[:, c, 3:4])
            S.barrier()
            st["off"] = mark
            st["lim"] = WSW - (4096 if sample else 0)

            def alloc_fixed(i_):
                return mkview(WSW - 4096 + i_ * 1024, 1024, [4, 512], BF16)
            y1T = alloc([2, T], BF16)
            y1r = Reg()
            mark2 = st["off"]
            for o in range(2):
                S.barrier()
                st["off"] = mark2
                Hs = alloc([NTB, 2, 256], BF16)
                Hr_ = Reg()
                nrm = alloc([256], F32)
                nr = Reg()
                markB = st["off"]
                hf_ = alloc([NTB, 512], BF16)
                hfr = Reg()
                markA = st["off"]
                assert markA <= WSW - 4096
                st["lim"] = WSW
                zfs = [alloc([512], F32) for _ in range(2)]
                zrs = [Reg() for _ in range(2)]
                h2 = alloc([2 * L], F32)
                hr2 = Reg()
                h1s = [alloc([512], F32) for _ in range(2)]
                hr1s = [Reg() for _ in range(2)]
                sa = [(alloc([512], F32), alloc([512], F32), Reg(), Reg()) for _ in range(2)]
                sb_ = [(alloc([512], F32), alloc([512], F32), Reg(), Reg()) for _ in range(2)]
                t0s = list(range(0, 2 * L, 512))

                def sin_pair(pt, pr, m_, s2, s4, s2r, s4r):
                    act(s2[0:64, :], pt[0:64, :], AF.Sin, [pr, scr_], [s2r], scale=sc[0:64, 4 * m_:4 * m_ + 1], bias=sc[0:64, 4 * m_ + 1:4 * m_ + 2])
                    act(s4[0:64, :], pt[0:64, :], AF.Sin, [pr, scr_], [s4r], scale=sc[0:64, 4 * m_ + 2:4 * m_ + 3], bias=sc[0:64, 4 * m_ + 3:4 * m_ + 4])

                def sin_fin(dst_ap, dreg_, s2, s4, s2r, s4r):
                    tt("vector", s4[0:64, :], s4[0:64, :], s4[0:64, :], ALU.mult, [s4r], [s4r])
                    ts("vector", s4[0:64, :], s4[0:64, :], -4.0, ALU.mult, [s4r], [s4r], s2=2.0, op1=ALU.add)
                    tt("vector", dst_ap, s2[0:64, :], s4[0:64, :], ALU.mult, [s2r, s4r], [dreg_])

                def mlp_p0(ti_):
                    zf, zr_ = zfs[ti_ % 2], zrs[ti_ % 2]
                    S.dma("sync", zf[0:33, :], A[zfn][:, t0s[ti_]:t0s[ti_] + 512], (), [zr_])
                    pt, pr = nb()
                    mm(pt[0:64, :], [(fw[0:33, 0:64], zf[0:33, :])], [fr, zr_], [pr])
                    sin_pair(pt, pr, 0, *sa[ti_ % 2])

                def mlp_p1(ti_):
                    h1, hr1 = h1s[ti_ % 2], hr1s[ti_ % 2]
                    sin_fin(h1[0:64, :], hr1, *sa[ti_ % 2])
                    pt, pr = nb()
                    mm(pt[0:64, :], [(fw[0:64, 64:128], h1[0:64, :])], [fr, hr1], [pr])
                    sin_pair(pt, pr, 1, *sb_[ti_ % 2])

                def mlp_p2(ti_):
                    sin_fin(h2[0:64, t0s[ti_]:t0s[ti_] + 512], hr2, *sb_[ti_ % 2])
                pipeline(len(t0s), [mlp_p0, mlp_p1, mlp_p2])
                ets = [(alloc([512], F32), alloc([512], F32), alloc([512], BF16), Reg(), Reg(), Reg()) for _ in range(2)]
                def hf_p0(tb, o=o):
                    et, hwt, sqb, etr, hwr2, sqr = ets[tb % 2]
                    pt, pr = nb()
                    for d_ in range(2):
                        cols = slice(128 + o * 512 + d_ * 256, 128 + o * 512 + (d_ + 1) * 256)
                        tbb = tb + d_ * NTB
                        mm(pt[:, d_ * 256:(d_ + 1) * 256], [(h2[0:64, tbb * 128:(tbb + 1) * 128], fw[0:64, cols])], [hr2, fr], [pr])
                        act(et[:, d_ * 256:(d_ + 1) * 256], dec[:, o * 512 + d_ * 256:o * 512 + (d_ + 1) * 256], AF.Exp, [dcr, RC], [etr],
                            scale=tn[:, tbb:tbb + 1])
                    tt("vector", hwt, pt[:], et, ALU.mult, [pr, etr], [hwr2])

                def hf_p1(tb, hf_=hf_, hfr=hfr):
                    et, hwt, sqb, etr, hwr2, sqr = ets[tb % 2]
                    cp("vector", hf_[:, tb, :], hwt, [hwr2], [hfr])
                    act(sqb, hwt, AF.Square, [hwr2], [sqr])
                    S.op("tensor", lambda e, tb=tb, sqb=sqb, NTB=NTB: e.matmul(PS[7][:], lhsT=onesb[:], rhs=sqb, start=(tb == 0), stop=(tb == NTB - 1)),
                         [RC, sqr], [PR[7]])
                pipeline(NTB, [hf_p0, hf_p1])
                memset("vector", hf_[0:1, 0, 256:512], 0.0, [hfr])
                tt("vector", nrm, PS[7][:, 0:256], epsc[:, 0:1].broadcast_to([128, 256]), ALU.add, [PR[7], RC], [nr])
                tt("vector", nrm, nrm, PS[7][:, 256:512], ALU.add, [PR[7], nr], [nr])
                act(nrm, nrm, AF.Sqrt, [nr], [nr])
                recip(nrm, nr)
                S.barrier()
                st["off"] = markA
                st["lim"] = WSW - (4096 if sample else 0)
                if sample:
                    dbuf = [alloc_fixed(i_) for i_ in range(4)]
                    dbr = [Reg() for _ in range(4)]
                    dst_ = {"i": 0}

                    def dft_tiles(mat, fr_, tg):
                        i = dst_["i"]
                        dst_["i"] = (i + 1) % 4
                        S.dma("sync", dbuf[i], A["dftS"][mat, tg * 512:(tg + 1) * 512, fr_ * 512:(fr_ + 1) * 512].rearrange("(k p) c -> p k c", p=128),
                              (), [dbr[i]])
                        return [dbuf[i][:, k, :] for k in range(4)], dbr[i]
                    NTG, TPG = 4, 4
                else:
                    def dft_tiles(mat, fr_, tg):
                        return [dftP[:, mat, k, :] for k in range(2)], RC
                    NTG, TPG = 1, 2
                NFR = L // NR
                FPR = NR // 128
                acss = [(alloc([FPR, 512], F32), Reg()) for _ in range(2)]
                tABs = [(alloc([512], F32), alloc([512], F32), Reg(), Reg()) for _ in range(2)]
                for fr_ in range(NFR):
                    acs, acr = acss[fr_ % 2]
                    for mat in range(2):
                        banks = [nb() for _ in range(FPR)]
                        for tg in range(NTG):
                            tiles, treg = dft_tiles(mat, fr_, tg)
                            for k in range(TPG):
                                tb = tg * TPG + k
                                for fc in range(FPR):
                                    S.op("tensor", lambda e, fc=fc, k=k, tb=tb, tiles=tiles, banks=banks, hf_=hf_, NTB=NTB: e.matmul(
                                        banks[fc][0][:], lhsT=tiles[k][:, fc * 128:(fc + 1) * 128], rhs=hf_[:, tb, :],
                                        start=(tb == 0), stop=(tb == NTB - 1)), [treg, hfr], [banks[fc][1]])
                        if mat == 0:
                            for fc in range(FPR):
                                evac(acs[:, fc, :], banks[fc][0][:], [banks[fc][1]], [acr])
                        else:
                            for fc in range(FPR):
                                fg = fr_ * FPR + fc
                                tA, tB, tAr, tBr = tABs[fc % 2]
                                Pc, Qc = acs[:, fc, 0:256], acs[:, fc, 256:512]
                                Ps_, Qs = banks[fc][0][:, 0:256], banks[fc][0][:, 256:512]
                                sg_, cph, sph = phi[:, fg, 2:3], phi[:, fg, 0:1], phi[:, fg, 1:2]
                                stt(tA[:, 0:256], Qs, sg_, Pc, ALU.mult, ALU.add, [banks[fc][1], acr, RC], [tAr])
                                stt(tA[:, 256:512], Qc, sg_, Ps_, ALU.mult, ALU.subtract, [banks[fc][1], acr, RC], [tAr])
                                act(tB[:, 0:256], tA[:, 256:512], AF.Identity, [tAr, RC], [tBr], scale=sph)
                                stt(tB[:, 0:256], tA[:, 0:256], cph, tB[:, 0:256], ALU.mult, ALU.subtract, [tAr, tBr, RC], [tBr])
                                act(tB[:, 256:512], tA[:, 256:512], AF.Identity, [tAr, RC], [tBr], scale=cph)
                                stt(tB[:, 256:512], tA[:, 0:256], sph, tB[:, 256:512], ALU.mult, ALU.add, [tAr, tBr, RC], [tBr])
                                tt("vector", Hs[:, fg, :, :], tB.rearrange("p (a c) -> p a c", a=2),
                                   nrm.unsqueeze(1).broadcast_to([128, 2, 256]), ALU.mult, [tBr, nr], [Hr_])
                S.barrier()
                st["off"] = markB
                tABs = [(alloc([512], F32), alloc([512], F32), Reg(), Reg()) for _ in range(2)]
                ncp = 2 if NSEQ > 1 else 1
                vtoks = [(alloc([NTB, 256], BF16), Reg()) for _ in range(ncp)]
                Yss = [(alloc([NTB, 2, 256], BF16), Reg()) for _ in range(ncp)]
                ucss = [(alloc([FPR, 256], F32), Reg()) for _ in range(2)]
                yscs = [(alloc([512], F32), Reg()) for _ in range(2)]
                ysi = {"i": 0}

                def next_ysc():
                    ysi["i"] += 1
                    return yscs[ysi["i"] % 2]
                def c_p0(s_):
                    vtok, vtr = vtoks[s_ % ncp]
                    for tb in range(NTB):
                        pt, pr = nb()
                        ysc, ysr = next_ysc()
                        for c in range(2):
                            if o == 0:
                                cp("gpsimd", ysc[:, c * 128:(c + 1) * 128], vvx[:, c, s_ * L + tb * 128:s_ * L + (tb + 1) * 128], [vxr], [ysr])
                            else:
                                cp("gpsimd", ysc[:, c * 128:(c + 1) * 128], y1T[:, c, s_ * L + tb * 128:s_ * L + (tb + 1) * 128], [y1r], [ysr])
                        for c in range(2):
                            tr(pt[:, c * 128:(c + 1) * 128], ysc[:, c * 128:(c + 1) * 128], [ysr], [pr])
                        evac(vtok[:, tb, :], pt[:, 0:256], [pr], [vtr])

                def c_p1(s_):
                    vtok, vtr = vtoks[s_ % ncp]
                    Ys, Yr_ = Yss[s_ % ncp]
                    for fr_ in range(NFR):
                        ucs, ucr = ucss[(s_ * NFR + fr_) % 2]
                        for mat in range(2):
                            banks = [nb() for _ in range(FPR)]
                            for tg in range(NTG):
                                tiles, treg = dft_tiles(mat, fr_, tg)
                                for k in range(TPG):
                                    tb = tg * TPG + k
                                    for fc in range(FPR):
                                        S.op("tensor", lambda e, fc=fc, k=k, tb=tb, tiles=tiles, banks=banks, vtok=vtok, NTB=NTB: e.matmul(
                                            banks[fc][0][:, 0:256], lhsT=tiles[k][:, fc * 128:(fc + 1) * 128], rhs=vtok[:, tb, :],
                                            start=(tb == 0), stop=(tb == NTB - 1)), [treg, vtr], [banks[fc][1]])
                            if mat == 0:
                                for fc in range(FPR):
                                    evac(ucs[:, fc, :], banks[fc][0][:, 0:256], [banks[fc][1]], [ucr])
                            else:
                                sc2 = 2.0 / (2 * L)
                                for fc in range(FPR):
                                    fg = fr_ * FPR + fc
                                    tA, tB, tAr, tBr = tABs[fc % 2]
                                    Uc, Us = ucs[:, fc, :], banks[fc][0][:, 0:256]
                                    Hre, Him = Hs[:, fg, 0, :], Hs[:, fg, 1, :]
                                    tt("vector", tA[:, 0:256], Us, Him, ALU.mult, [banks[fc][1], Hr_], [tAr])
                                    tt("vector", tA[:, 256:512], Us, Hre, ALU.mult, [banks[fc][1], Hr_], [tAr])
                                    tt("gpsimd", tB[:, 0:256], Uc, Hre, ALU.mult, [ucr, Hr_], [tBr])
                                    tt("gpsimd", tB[:, 256:512], Uc, Him, ALU.mult, [ucr, Hr_], [tBr])
                                    tt("vector", tA[:, 0:256], tA[:, 0:256], tB[:, 0:256], ALU.add, [tAr, tBr], [tAr])
                                    tt("vector", tA[:, 256:512], tA[:, 256:512], tB[:, 256:512], ALU.subtract, [tAr, tBr], [tAr])
                                    act(Ys[:, fg, :, :], tA.rearrange("p (a c) -> p a c", a=2), AF.Copy, [tAr], [Yr_], scale=sc2)

                def c_p2(s_):
                    Ys, Yr_ = Yss[s_ % ncp]
                    for tr_ in range(NFR):
                        banks = [nb() for _ in range(2)]
                        first = True
                        for mat in range(2):
                            for fgp in range(NTG):
                                tiles, treg = dft_tiles(mat, tr_, fgp)
                                for k in range(TPG):
                                    fg = fgp * TPG + k
                                    last = (mat == 1 and fg == NTB - 1)
                                    for c in range(2):
                                        S.op("tensor", lambda e, c=c, k=k, fg=fg, mat=mat, tiles=tiles, banks=banks, first=first, last=last, Ys=Ys, NR=NR: e.matmul(
                                            banks[c][0][:, 0:NR], lhsT=Ys[:, fg, mat, c * 128:(c + 1) * 128], rhs=tiles[k],
                                            start=first, stop=last), [treg, Yr_], [banks[c][1]])
                                    first = False
                        for c in range(2):
                            tsl = slice(s_ * L + tr_ * NR, s_ * L + (tr_ + 1) * NR)
                            ysc, ysr = next_ysc()
                            if o == 0:
                                stt(ysc[:, 0:NR], vvx[:, c, tsl], hb_[:, c, 0:1], banks[c][0][:, 0:NR], ALU.mult, ALU.add, [vxr, hwr, banks[c][1], ysr], [ysr])
                                tt("vector", y1T[:, c, tsl], ysc[:, 0:NR], vvx[:, 2 + c, tsl], ALU.mult, [ysr, vxr], [y1r])
                            else:
                                stt(ysc[:, 0:NR], y1T[:, c, tsl], hb_[:, c, 1:2], banks[c][0][:, 0:NR], ALU.mult, ALU.add, [y1r, hwr, banks[c][1], ysr], [ysr])
                                tt("vector", vvx[:, c, tsl], ysc[:, 0:NR], vvx[:, 4 + c, tsl], ALU.mult, [ysr, vxr], [vxr])
                pipeline(NSEQ, [c_p0, c_p1, c_p2])
            S.barrier()
            st["lim"] = WSW
            out_accum(l, j, vvx, vxr, 768, 2, T)

        def mix(l, j, grp):
            T = grp["T"]
            arena_reset()
            hT = alloc_top([8, T], BF16)
            hreg = Reg()
            base = st["off"]
            scr = norm_scratch()
            norm_many([(t0, 256, COEF[:, l, j, 3, :], COEF[:, l, j, 4, :], hT[:, :, t0:t0 + 256], hreg) for t0 in range(0, T, 256)], scr)
            for fn in (mix_attention, mix_conv, mix_hyena):
                S.barrier()
                st["off"] = base
                fn(l, j, grp, hT, hreg)
                chk(fn.__name__)

        def final_out(yd, T):
            arena_reset()
            scr = norm_scratch()
            yTs = [alloc([8, 256], F32) for _ in range(2)]
            yregs = [Reg() for _ in range(2)]
            yo = [alloc([D], F32) for _ in range(2)]
            yor = [Reg() for _ in range(2)]
            nt = T // 256

            def outphase(it):
                yT, yreg = yTs[it % 2], yregs[it % 2]
                for b2 in range(2):
                    b = it * 2 + b2
                    i = b % 2
                    for hf in range(2):
                        pt, pr = nb()
                        for c4 in range(4):
                            tr(pt[:, c4 * 128:(c4 + 1) * 128], yT[:, hf * 4 + c4, b2 * 128:(b2 + 1) * 128], [yreg], [pr])
                        evac(yo[i][:, hf * 512:(hf + 1) * 512], pt[:], [pr], [yor[i]])
                    S.dma("sync", yd[b * 128:(b + 1) * 128, :], yo[i], [yor[i]], [ROUT])
            norm_many([(it * 256, 256, GT[:, 0, 48:56], None, yTs[it % 2], yregs[it % 2]) for it in range(nt)], scr, extra=outphase)

        groups = [dict(T=2048, NSEQ=1, L=2048, sample=True, x="xs", y=ys, j=1),
                  dict(T=1024, NSEQ=4, L=256, sample=False, x="xp", y=yp, j=0)]
        try:
            chk("mod")
            for gi_, grp in enumerate(groups):
                if gi_ > 0:
                    load_x(A[grp["x"]], grp["T"])
                chk("load_x")
                for l in range(2):
                    ffn(l, 1, grp["j"], grp["T"])
                    chk("ffn1")
                    mix(l, grp["j"], grp)
                    ffn(l, 2, grp["j"], grp["T"])
                    chk("ffn2")
                final_out(grp["y"], grp["T"])
                chk("final")
        except StopBuild:
            pass
        if dbg is not None:
            S.barrier()
            S.dma("sync", dbg, xT[:], [RX], [ROUT])
        S.barrier()
        print("instructions:", S.ninst, {e: len(S.prog[e]) for e in ENGS})
        with nc.Block() as block:
            S.emit(block)
    return nc


_CACHE = {}


def kernel(**inputs):
    consts = host_consts()
    n = 8
    f32 = np.float32
    shared = {k: np.ascontiguousarray(inputs[k], dtype=f32) for k in WEIGHT_NAMES}
    in_maps = []
    for i in range(n):
        m = dict(shared)
        m["xs"] = np.ascontiguousarray(inputs["x_sample"][i], dtype=f32)
        m["xp"] = np.ascontiguousarray(inputs["x_prompt"][4 * i:4 * i + 4].reshape(1024, D), dtype=f32)
        m["cc"] = np.ascontiguousarray(np.stack([inputs["c_ctx"], inputs["c"][i]]), dtype=f32)
        m["ck"] = np.ascontiguousarray(inputs["cache_k"][i].reshape(2, 256, 128), dtype=f32)
        m["cv"] = np.ascontiguousarray(inputs["cache_v"][i].reshape(2, 256, 128), dtype=f32)
        m.update(consts)
        in_maps.append(m)
    if "nc" not in _CACHE:
        shapes = {k: (v.shape, F32) for k, v in in_maps[0].items() if k not in consts}
        cshapes = {k: (v.shape, BF16 if v.dtype == ml_dtypes.bfloat16 else F32) for k, v in consts.items()}
        _CACHE["nc"] = build_nc(shapes, cshapes)
    res = run_bass_kernel_spmd(_CACHE["nc"], in_maps, core_ids=list(range(n)))
    R = res.results
    y_prompt = np.concatenate([r["yp"].reshape(4, 256, D) for r in R], 0).astype(f32)
    y_sample = np.stack([r["ys"] for r in R], 0).astype(f32)
    new_k = np.concatenate([r["nk"].reshape(4, 2, 256, 2, 64) for r in R], 0).astype(f32)
    new_v = np.concatenate([r["nv"].reshape(4, 2, 256, 2, 64) for r in R], 0).astype(f32)
    return (y_prompt, y_sample, new_k, new_v)
```

```python
import contextlib
import math
import numpy as np
import ml_dtypes
import concourse.bass as bass
import concourse.mybir as mybir
from concourse.bass_utils import run_bass_kernel_spmd

F32 = mybir.dt.float32
BF16 = mybir.dt.bfloat16
AF = mybir.ActivationFunctionType
ALU = mybir.AluOpType

ENGS = ("tensor", "vector", "scalar", "gpsimd", "sync")
EPOCH = 30000
D = 1024
DFF = 2816
NFC = 22
EPS = 1e-6


class Reg:
    __slots__ = ("w", "r", "psum")

    def __init__(self, psum=False):
        self.w = None
        self.r = {}
        self.psum = psum


class Sched:
    def __init__(self, sems):
        self.free_sems = list(sems)
        self.prog = {e: [] for e in ENGS}
        self.cur = {}
        self.cnt = {}
        self.waited = {e: {} for e in ENGS}
        self.pe_sems = set()
        self.own = {e: set() for e in ENGS}
        for e in ENGS:
            self._new_epoch(e)
        self.pools = {}
        self.ninst = 0

    def _new_epoch(self, e):
        self.cur[e] = self.free_sems.pop()
        self.own[e].add(self.cur[e])
        if e == "tensor":
            self.pe_sems.add(self.cur[e])
        self.cnt[e] = 0

    def make_pool(self, q, n):
        self.pools[q] = {"sems": [self.free_sems.pop() for _ in range(n)], "cnt": [0] * n, "i": 0}

    def _need(self, eng, toks):
        need = {}
        for k, v in toks:
            if need.get(k, 0) < v:
                need[k] = v
        wd = self.waited[eng]
        for k, v in need.items():
            if eng == "tensor" and k in self.pe_sems:
                continue
            if wd.get(k, 0) < v:
                self.prog[eng].append(("wait", k, v))
                wd[k] = v

    def _deps(self, eng, reads, writes):
        toks = []
        for r in reads:
            if r.w is not None:
                toks.append(r.w)
            if r.psum:
                own = self.own[eng]
                toks.extend((k, v) for k, v in r.r.items() if k not in own)
        own = self.own[eng]
        for w in writes:
            if w.w is not None and w.w[0] not in own:
                toks.append(w.w)
            toks.extend((k, v) for k, v in w.r.items() if k not in own)
        self._need(eng, toks)

    def _commit(self, tok, reads, writes):
        for r in reads:
            if r.r.get(tok[0], 0) < tok[1]:
                r.r[tok[0]] = tok[1]
        for w in writes:
            w.w = tok
            w.r = {}

    def op(self, eng, fns, reads=(), writes=()):
        if not isinstance(fns, (list, tuple)):
            fns = [fns]
        self._deps(eng, reads, writes)
        if self.cnt[eng] >= EPOCH:
            self._new_epoch(eng)
        self.cnt[eng] += 1
        tok = (self.cur[eng], self.cnt[eng])
        for f in fns[:-1]:
            self.prog[eng].append(("inst", f, None, 0))
        self.prog[eng].append(("inst", fns[-1], tok[0], 1))
        self.ninst += len(fns)
        self._commit(tok, reads, writes)

    def dma(self, q, out, in_, reads=(), writes=()):
        self._deps(q, reads, writes)
        p = self.pools[q]
        i = p["i"]
        p["i"] = (i + 1) % len(p["sems"])
        s = p["sems"][i]
        if p["cnt"][i] > 0:
            self._need(q, [(s, 16 * p["cnt"][i])])
        p["cnt"][i] += 1
        tok = (s, 16 * p["cnt"][i])
        self.prog[q].append(("inst", lambda e, out=out, in_=in_: e.dma_start(out=out, in_=in_), s, 16))
        self.ninst += 1
        self._commit(tok, reads, writes)

    def barrier(self):
        toks = [(self.cur[e], self.cnt[e]) for e in ENGS if self.cnt[e] > 0]
        for p in self.pools.values():
            for s, c in zip(p["sems"], p["cnt"]):
                if c > 0:
                    toks.append((s, 16 * c))
        for e in ENGS:
            self._need(e, toks)

    def emit(self, block):
        for e in ENGS:
            entries = self.prog[e]

            def body(eng, entries=entries):
                for ent in entries:
                    if ent[0] == "wait":
                        eng.wait_ge(ent[1], ent[2])
                    else:
                        ins = ent[1](eng)
                        if ent[2] is not None:
                            ins.then_inc(ent[2], ent[3])
            getattr(block, e)(body)


def host_consts():
    c = {}
    c["ident"] = np.eye(128, dtype=np.float32)
    c["onesb"] = np.ones((128, 128), dtype=ml_dtypes.bfloat16)
    P = np.zeros((128, 128), np.float32)
    for hb in (0, 64):
        for part in (0, 32):
            for i in range(16):
                a = hb + part + i
                b = a + 16
                P[b, a] = -1.0
                P[a, b] = 1.0
    c["prope"] = P.astype(ml_dtypes.bfloat16)
    t = np.arange(2048)
    row = (t // 64).astype(np.float64)
    col = (t % 64).astype(np.float64)
    inv = (10000.0 ** (-np.arange(16, dtype=np.float32) / 16)).astype(np.float64)
    ang = np.zeros((64, 2048))
    ang[0:16] = inv[:, None] * row[None]
    ang[16:32] = inv[:, None] * row[None]
    ang[32:48] = inv[:, None] * col[None]
    ang[48:64] = inv[:, None] * col[None]
    cs = np.zeros((128, 2, 2048), np.float32)
    cs[:, 0] = np.tile(np.cos(ang), (2, 1))
    cs[:, 1] = np.tile(np.sin(ang), (2, 1))
    c["ropecs"] = cs.astype(ml_dtypes.bfloat16)
    kk = np.arange(128)[:, None]
    qi = np.arange(128)[None, :]
    m = np.zeros((128, 2, 128), np.float32)
    m[:, 0] = (kk >= qi)
    m[:, 1] = (kk <= qi)
    c["masks"] = m.astype(ml_dtypes.bfloat16)

    def dft(L):
        N = 2 * L
        tt = np.arange(L, dtype=np.float64)[:, None] + 0.5
        ff = np.arange(L, dtype=np.float64)[None, :] + 0.5
        th = 2 * np.pi * tt * ff / N
        return np.stack([np.cos(th), np.sin(th)]).astype(ml_dtypes.bfloat16)

    def phis(L):
        N = 2 * L
        f = np.arange(L, dtype=np.float64)
        ph = np.stack([np.cos(np.pi * (f + .5) / N), np.sin(np.pi * (f + .5) / N), (-1.0) ** f], -1)
        return np.ascontiguousarray(ph.reshape(L // 128, 128, 3).transpose(1, 0, 2)).astype(np.float32)

    def zfeat(L):
        tf = np.arange(L)
        pos = np.concatenate([tf, (L - tf) % L]).astype(np.float64)
        tn = pos / (L - 1)
        bands = np.linspace(1e-4, 15, 16)
        ang = 2 * np.pi * pos[:, None] * bands[None, :] / L
        z = np.concatenate([tn[:, None], np.cos(ang), -np.sin(ang)], -1)
        tnneg = np.ascontiguousarray((-tn).reshape(2 * L // 128, 128).T).astype(np.float32)
        return np.ascontiguousarray(z.T).astype(np.float32), tnneg

    c["dftS"] = dft(2048)
    dp = dft(256)
    c["dftP"] = np.ascontiguousarray(dp.reshape(2, 2, 128, 256).transpose(2, 0, 1, 3))
    c["phiS"] = phis(2048)
    c["phiP"] = phis(256)
    c["zfS"], c["tnS"] = zfeat(2048)
    c["zfP"], c["tnP"] = zfeat(256)
    return c


WEIGHT_NAMES = ["w_mod", "b_mod", "g_ffn1", "g_mix", "g_ffn2", "g_final", "w1_gate", "w1_up", "w1_down",
                "w2_gate", "w2_up", "w2_down", "w_in", "w_out", "attn_sink", "conv_dw", "conv_dw_b",
                "conv_ln_g", "conv_ln_b", "conv_pw", "hy_short_w", "hy_short_b", "hy_w1", "hy_b1", "hy_f1",
                "hy_w2", "hy_b2", "hy_f2", "hy_w3", "hy_log_decay", "hy_bias"]


class StopBuild(Exception):
    pass


def build_nc(shapes, cshapes, stop=None):
    nc = bass.Bass("TRN2", target_bir_lowering=False)
    dbg = nc.dram_tensor("dbg", [128, 8, 2048], F32, kind="ExternalOutput").ap() if stop is not None else None
    stage = {"n": 0}

    def chk(name):
        stage["n"] += 1
        if stop is not None:
            print("stage", stage["n"], name)
            if stage["n"] >= stop:
                raise StopBuild()

    Hd = {}
    for n, (shp, dt) in {**shapes, **cshapes}.items():
        Hd[n] = nc.dram_tensor(n, list(shp), dt, kind="ExternalInput")
    A = {n: h.ap() for n, h in Hd.items()}
    yp = nc.dram_tensor("yp", [1024, D], F32, kind="ExternalOutput").ap()
    ys = nc.dram_tensor("ys", [2048, D], F32, kind="ExternalOutput").ap()
    nk = nc.dram_tensor("nk", [4, 2, 256, 128], F32, kind="ExternalOutput").ap()
    nv = nc.dram_tensor("nv", [4, 2, 256, 128], F32, kind="ExternalOutput").ap()

    with contextlib.ExitStack() as es:
        def sb(name, shape, dt):
            return es.enter_context(nc.sbuf_tensor("sb_" + name, shape, dt))
        sems = [es.enter_context(nc.semaphore(f"s{i}")) for i in range(100)]
        S = Sched(sems)
        S.make_pool("sync", 16)
        S.make_pool("gpsimd", 16)
        S.make_pool("scalar", 8)

        xT = sb("xT", [128, 8, 2048], F32)
        ident = sb("ident", [128, 128], F32)
        onesb = sb("onesb", [128, 128], BF16)
        prope = sb("prope", [128, 128], BF16)
        masks = sb("masks", [128, 2, 128], BF16)
        dftP = sb("dftP", [128, 2, 2, 256], BF16)
        phiS = sb("phiS", [128, 16, 3], F32)
        phiP = sb("phiP", [128, 2, 3], F32)
        tnS = sb("tnS", [128, 32], F32)
        tnP = sb("tnP", [128, 4], F32)
        COEF = sb("COEF", [128, 2, 2, 9, 8], F32)
        GT = sb("GT", [128, 1, 56], F32)
        MODT = sb("MODT", [128, 2, 72, 2], F32)
        epsc = sb("epsc", [128, 1], F32)
        identb = sb("identb", [128, 128], BF16)
        WSW = 32 * 1024 + 1536
        WS = sb("WS", [128, WSW], F32)
        PS = [es.enter_context(nc.psum_tensor(f"ps{i}", [128, 512], F32)) for i in range(8)]
        PR = [Reg(psum=True) for _ in range(8)]
        st = {"bank": 0, "off": 0, "flip": 0}
        RC = Reg()

        def nb():
            i = st["bank"]
            st["bank"] = (i + 1) % 7
            return PS[i], PR[i]

        def arena_reset():
            S.barrier()
            st["off"] = 0
            st["lim"] = WSW

        def mkview(off, words, shape, dt):
            v = WS[:, off:off + words]
            if dt != F32:
                v = v.bitcast(BF16)
            if len(shape) == 2:
                v = v.rearrange("p (a b) -> p a b", a=shape[0])
            return v

        def alloc_top(shape, dt):
            n = int(np.prod(shape))
            words = n if dt == F32 else (n + 1) // 2
            st["lim"] = WSW - words
            return mkview(WSW - words, words, shape, dt)

        def alloc(shape, dt):
            n = int(np.prod(shape))
            words = n if dt == F32 else (n + 1) // 2
            off = st["off"]
            st["off"] = off + words
            assert st["off"] <= st["lim"], ("arena overflow", st["off"], st["lim"])
            v = WS[:, off:off + words]
            if dt != F32:
                v = v.bitcast(BF16)
            if len(shape) == 2:
                v = v.rearrange("p (a b) -> p a b", a=shape[0])
            elif len(shape) == 3:
                v = v.rearrange("p (a b c) -> p a b c", a=shape[0], b=shape[1])
            elif len(shape) == 4:
                v = v.rearrange("p (a b c d) -> p a b c d", a=shape[0], b=shape[1], c=shape[2])
            return v

        def mm(out, pairs, reads, writes):
            n = len(pairs)
            fns = []
            for i, (l, r) in enumerate(pairs):
                fns.append(lambda e, l=l, r=r, s0=(i == 0), s1=(i == n - 1): e.matmul(out, lhsT=l, rhs=r, start=s0, stop=s1))
            S.op("tensor", fns, reads, writes)

        def tr(out, in_, reads, writes):
            k = in_.shape[0]
            S.op("tensor", lambda e: e.transpose(out, in_, ident[0:k, 0:k]), list(reads) + [RC], writes)

        def act(out, in_, func, reads, writes, scale=1.0, bias=None, accum=None):
            kw = {}
            if bias is not None:
                kw["bias"] = bias
            if accum is not None:
                kw["accum_out"] = accum
            S.op("scalar", lambda e: e.activation(out=out, in_=in_, func=func, scale=scale, **kw), reads, writes)

        def tt(eng, out, in0, in1, op, reads, writes):
            S.op(eng, lambda e: e.tensor_tensor(out=out, in0=in0, in1=in1, op=op), reads, writes)

        def ts(eng, out, in0, s1, op0, reads, writes, s2=None, op1=None):
            if op1 is None:
                S.op(eng, lambda e: e.tensor_scalar(out=out, in0=in0, scalar1=s1, scalar2=None, op0=op0), reads, writes)
            else:
                S.op(eng, lambda e: e.tensor_scalar(out=out, in0=in0, scalar1=s1, scalar2=s2, op0=op0, op1=op1), reads, writes)

        def stt(out, in0, scalar, in1, op0, op1, reads, writes):
            S.op("vector", lambda e: e.scalar_tensor_tensor(out=out, in0=in0, scalar=scalar, in1=in1, op0=op0, op1=op1), reads, writes)

        def cp(eng, out, in_, reads, writes):
            if eng == "scalar":
                act(out, in_, AF.Copy, reads, writes)
            else:
                S.op(eng, lambda e: e.tensor_copy(out=out, in_=in_), reads, writes)

        def evac(out, in_, reads, writes):
            st["flip"] ^= 1
            cp("scalar" if st["flip"] else "vector", out, in_, reads, writes)

        def recip(ap, reg):
            S.op("vector", lambda e: e.reciprocal(out=ap, in_=ap), [reg], [reg])

        def memset(eng, ap, val, writes):
            S.op(eng, lambda e: e.memset(ap, val), (), writes)

        def dap(name, off, pat):
            return bass.AP(Hd[name], off, pat)

        def panel(dst, dreg, W2d, c0, ncols):
            S.dma("gpsimd", dst, W2d[:, c0:c0 + ncols].rearrange("(k p) c -> p k c", p=128), (), [dreg])

        for nm, t_ in [("ident", ident), ("onesb", onesb), ("prope", prope), ("masks", masks), ("dftP", dftP),
                       ("phiS", phiS), ("phiP", phiP), ("tnS", tnS), ("tnP", tnP)]:
            S.dma("sync", t_[:], A[nm], (), [RC])
        memset("vector", epsc[:], EPS, [RC])
        cp("vector", identb[:], ident[:], [RC], [RC])

        def loadT(pieces, W, out, oreg):
            R = sum(p.shape[0] for p in pieces)
            stg = alloc([W], F32)
            sreg = Reg()
            r0 = 0
            for p in pieces:
                S.dma("sync", stg[r0:r0 + p.shape[0], :], p, (), [sreg])
                r0 += p.shape[0]
            for k in range(W // 128):
                pt, pr = nb()
                tr(pt[:, 0:R], stg[0:R, k * 128:(k + 1) * 128], [sreg], [pr])
                cp("vector", out[:, k, 0:R], pt[:, 0:R], [pr], [oreg])

        arena_reset()
        RX = Reg()
        ROUT = Reg()
        xin0 = [alloc([D], F32) for _ in range(2)]
        xr0 = [Reg() for _ in range(2)]
        for b in range(2048 // 128):
            i = b % 2
            S.dma("sync", xin0[i], A["xs"][b * 128:(b + 1) * 128, :], (), [xr0[i]])
            for hf in range(2):
                pt, pr = nb()
                for c4 in range(4):
                    c = hf * 4 + c4
                    tr(pt[:, c4 * 128:(c4 + 1) * 128], xin0[i][:, c * 128:(c + 1) * 128], [xr0[i]], [pr])
                evac(xT[:, hf * 4:hf * 4 + 4, b * 128:(b + 1) * 128], pt[:].rearrange("p (a b) -> p a b", a=4), [pr], [RX])
        gp = [A[n].rearrange("l (c p) -> (l c) p", p=128) for n in ("g_ffn1", "g_mix", "g_ffn2")]
        gp.append(A["g_final"].rearrange("(c p) -> c p", p=128))
        loadT(gp, 128, GT, RC)
        cst = alloc([D], F32)
        creg = Reg()
        S.dma("sync", cst[0:2, :], A["cc"], (), [creg])
        csl = alloc([D], F32)
        act(csl[0:2, :], cst[0:2, :], AF.Silu, [creg], [creg])
        scT = alloc([8, 2], BF16)
        for k in range(8):
            pt, pr = nb()
            tr(pt[:, 0:2], csl[0:2, k * 128:(k + 1) * 128], [creg], [pr])
            cp("vector", scT[:, k, :], pt[:, 0:2], [pr], [creg])
        bmT = alloc([2, 72], F32)
        for l in range(2):
            loadT([A["b_mod"][l].rearrange("(j p) -> j p", p=128)], 128, bmT[:, l:l + 1, :], creg)
        wmp = [alloc([8, 512], BF16) for _ in range(3)]
        wmr = [Reg() for _ in range(3)]
        RM = Reg()
        for l in range(2):
            for jb in range(18):
                i = (l * 18 + jb) % 3
                panel(wmp[i], wmr[i], A["w_mod"][l], jb * 512, 512)
                pt, pr = nb()
                for jc in range(4):
                    mm(pt[:, jc * 2:jc * 2 + 2],
                       [(wmp[i][:, k, jc * 128:(jc + 1) * 128], scT[:, k, :]) for k in range(8)],
                       [wmr[i], creg], [pr])
                j0 = jb * 4
                tt("vector", MODT[:, l, j0:j0 + 4, :], pt[:, 0:8].rearrange("p (a b) -> p a b", a=4),
                   bmT[:, l, j0:j0 + 4].unsqueeze(2).broadcast_to([128, 4, 2]), ALU.add, [pr, creg], [RM])
        for l in range(2):
            for j in range(2):
                def mv(k):
                    return MODT[:, l, k * 8:(k + 1) * 8, j]
                for blk, goff, half in ((0, 0, 0.5), (1, 16, 1.0), (2, 32, 0.5)):
                    gv = GT[:, 0, goff + l * 8: goff + l * 8 + 8]
                    stt(COEF[:, l, j, blk * 3 + 0, :], mv(blk * 3 + 1), 1.0, gv, ALU.add, ALU.mult, [RM, RC], [RM])
                    cp("vector", COEF[:, l, j, blk * 3 + 1, :], mv(blk * 3 + 0), [RM], [RM])
                    ts("vector", COEF[:, l, j, blk * 3 + 2, :], mv(blk * 3 + 2), half, ALU.mult, [RM], [RM])

        def load_x(xd, T, reset=True):
            if reset:
                arena_reset()
            xin = [alloc([D], F32) for _ in range(2)]
            xr = [Reg() for _ in range(2)]
            for b in range(T // 128):
                i = b % 2
                S.dma("sync", xin[i], xd[b * 128:(b + 1) * 128, :], (), [xr[i]])
                for hf in range(2):
                    pt, pr = nb()
                    for c4 in range(4):
                        c = hf * 4 + c4
                        tr(pt[:, c4 * 128:(c4 + 1) * 128], xin[i][:, c * 128:(c + 1) * 128], [xr[i]], [pr])
                    evac(xT[:, hf * 4:hf * 4 + 4, b * 128:(b + 1) * 128], pt[:].rearrange("p (a b) -> p a b", a=4), [pr], [RX])

        def pipeline(n, phases):
            for step in range(n + len(phases) - 1):
                for p, f in enumerate(phases):
                    i = step - p
                    if 0 <= i < n:
                        f(i)

        def norm_A(t0, n, sc_):
            sq, tmp, rt = sc_["sq"], sc_["tmp"], sc_["rt"]
            act(sq[:, :, 0:n], xT[:, :, t0:t0 + n], AF.Square, [RX], [sc_["r"]])
            pt, pr = nb()
            mm(pt[:, 0:n], [(onesb[:], sq[:, c, 0:n]) for c in range(8)], [RC, sc_["r"]], [pr])
            act(rt[:, 0:n], pt[:, 0:n], AF.Sqrt, [pr, RC], [sc_["r2"]], scale=1.0 / D, bias=epsc[:, 0:1])
            recip(rt[:, 0:n], sc_["r2"])
            tt("vector", tmp[:, :, 0:n], xT[:, :, t0:t0 + n], rt[:, 0:n].unsqueeze(1).broadcast_to([128, 8, n]), ALU.mult,
               [RX, sc_["r2"]], [sc_["r3"]])

        def norm_B(n, Aco, Bco, hdst, hreg, sc_):
            tmp = sc_["tmp"]
            for c in range(8):
                if Bco is None:
                    ts("vector", hdst[:, c, 0:n], tmp[:, c, 0:n], Aco[:, c:c + 1], ALU.mult, [sc_["r3"], RM, RC], [hreg])
                else:
                    act(hdst[:, c, 0:n], tmp[:, c, 0:n], AF.Identity, [sc_["r3"], RM], [hreg], scale=Aco[:, c:c + 1], bias=Bco[:, c:c + 1])

        def norm_many(items, scr, extra=None):
            sets = scr["sets"]
            ph = [lambda i: norm_A(items[i][0], items[i][1], sets[i % 2]),
                  lambda i: norm_B(items[i][1], items[i][2], items[i][3], items[i][4], items[i][5], sets[i % 2])]
            if extra is not None:
                ph.append(extra)
            pipeline(len(items), ph)

        def norm_scratch():
            return {"i": 0, "sets": [{"sq": alloc([8, 256], BF16), "tmp": alloc([8, 256], F32), "rt": alloc([256], F32),
                                      "r": Reg(), "r2": Reg(), "r3": Reg()} for _ in range(2)]}

        def ffn(l, which, j, T):
            Wg = A["w%d_gate" % which][l]
            Wu = A["w%d_up" % which][l]
            Wd = A["w%d_down" % which][l]
            kb = 0 if which == 1 else 6
            Aco, Bco, Gco = COEF[:, l, j, kb + 0, :], COEF[:, l, j, kb + 1, :], COEF[:, l, j, kb + 2, :]
            arena_reset()
            scr = norm_scratch()
            hT = alloc([8, 1024], BF16)
            actb = alloc([NFC, 1024], BF16)
            NWB = 3
            wg = [alloc([8, 256], BF16) for _ in range(NWB)]
            wu = [alloc([8, 256], BF16) for _ in range(NWB)]
            wd = [alloc([NFC, 128], BF16) for _ in range(2)]
            sg = [alloc([512], F32) for _ in range(2)]
            wgr = [Reg() for _ in range(NWB)]
            wur = [Reg() for _ in range(NWB)]
            wdr = [Reg() for _ in range(2)]
            sgr = [Reg() for _ in range(2)]
            cnt = 0
            for u0 in range(0, T, 1024):
                hreg = Reg()
                areg = [Reg() for _ in range(NFC)]
                norm_many([(u0 + s * 256, 256, Aco, Bco, hT[:, :, s * 256:(s + 1) * 256], hreg) for s in range(4)], scr)
                for fb in range(11):
                    i = fb % NWB
                    panel(wg[i], wgr[i], Wg, fb * 256, 256)
                    panel(wu[i], wur[i], Wu, fb * 256, 256)
                    for f2 in range(2):
                        fc = fb * 2 + f2
                        for tl in range(2):
                            pg, pgr = nb()
                            pu, pur = nb()
                            rhs = [hT[:, k, tl * 512:(tl + 1) * 512] for k in range(8)]
                            mm(pg[:], [(wg[i][:, k, f2 * 128:(f2 + 1) * 128], rhs[k]) for k in range(8)], [wgr[i], hreg], [pgr])
                            mm(pu[:], [(wu[i][:, k, f2 * 128:(f2 + 1) * 128], rhs[k]) for k in range(8)], [wur[i], hreg], [pur])
                            si = cnt % 2
                            cnt += 1
                            act(sg[si], pg[:], AF.Silu, [pgr], [sgr[si]])
                            tt("vector", actb[:, fc, tl * 512:(tl + 1) * 512], sg[si], pu[:], ALU.mult, [sgr[si], pur], [areg[fc]])
                for dc in range(8):
                    i = dc % 2
                    S.dma("gpsimd", wd[i], Wd[:, dc * 128:(dc + 1) * 128].rearrange("(k p) c -> p k c", p=128), (), [wdr[i]])
                    for tl in range(2):
                        pt, pr = nb()
                        mm(pt[:], [(wd[i][:, k, :], actb[:, k, tl * 512:(tl + 1) * 512]) for k in range(NFC)],
                           [wdr[i]] + areg, [pr])
                        xs_ = xT[:, dc, u0 + tl * 512:u0 + (tl + 1) * 512]
                        stt(xs_, pt[:], Gco[:, dc:dc + 1], xs_, ALU.mult, ALU.add, [pr, RM, RX], [RX])

        def inproj_fm(wpanel_ap, wreg, hT, hreg, T, dst_fn, dreg):
            for tl in range(T // 512):
                pt, pr = nb()
                mm(pt[:], [(wpanel_ap[:, k, :], hT[:, k, tl * 512:(tl + 1) * 512]) for k in range(8)], [wreg, hreg], [pr])
                d = dst_fn(tl)
                src = pt[:]
                if len(d.shape) == 3:
                    src = src.rearrange("p (a b) -> p a b", a=d.shape[1])
                evac(d, src, [pr], [dreg])

        def out_accum(l, j, srcT, sreg, row0, nch, T):
            Gco = COEF[:, l, j, 5, :]
            wo = alloc([nch, D], BF16)
            wor = Reg()
            S.dma("gpsimd", wo, A["w_out"][l][row0:row0 + nch * 128, :].rearrange("(k p) c -> p k c", p=128), (), [wor])
            for dc in range(8):
                for tl in range(T // 512):
                    pt, pr = nb()
                    mm(pt[:], [(wo[:, k, dc * 128:(dc + 1) * 128], srcT[:, k, tl * 512:(tl + 1) * 512]) for k in range(nch)],
                       [wor, sreg], [pr])
                    xs_ = xT[:, dc, tl * 512:(tl + 1) * 512]
                    stt(xs_, pt[:], Gco[:, dc:dc + 1], xs_, ALU.mult, ALU.add, [pr, RM, RX], [RX])

        def mix_attention(l, j, grp, hT, hreg):
            T, NSEQ, L = grp["T"], grp["NSEQ"], grp["L"]
            sample = grp["sample"]
            Win = A["w_in"][l]
            qT = alloc([4, T], BF16)
            kT = alloc([4, T], BF16)
            nblk = T // 128
            vo = alloc([nblk, 2, 128], BF16)
            attT = alloc([4, T], BF16)
            Eall = [alloc([512], BF16) for _ in range(10)]
            Erall = [Reg() for _ in range(10)]
            dnall = [alloc([512], F32) for _ in range(2)]
            dnrall = [Reg() for _ in range(2)]
            ESb = alloc([8], F32)
            sinkL = alloc([128], BF16)
            esfull = alloc([8, 128], BF16)
            pmark = st["off"]
            wq = [alloc([8, 256], BF16) for _ in range(2)]
            wkv = alloc([8, 256], BF16)
            wkb = alloc([8, 128], BF16)
            wr = [Reg() for _ in range(4)]
            qr, kr, vr, ar, esr = Reg(), Reg(), Reg(), Reg(), Reg()
            S.dma("sync", ESb, dap("attn_sink", l * 8, [[0, 128], [1, 8]]), (), [esr])
            act(ESb, ESb, AF.Exp, [esr], [esr])
            memset("vector", sinkL, 0.0, [esr])
            memset("vector", sinkL[0:1, 64:128], 1.0, [esr])
            memset("vector", esfull, 0.0, [esr])
            cp("vector", esfull[0:1, :, :], ESb[0:1, :].unsqueeze(2).broadcast_to([1, 8, 128]), [esr], [esr])
            memset("vector", vo[:, :, :, 64:128], 1.0, [vr])
            for pi in range(2):
                panel(wq[pi], wr[pi], Win, pi * 256, 256)
            panel(wkv, wr[2], Win, 512, 256)
            S.dma("gpsimd", wkb[:, :, 0:64], Win[:, 576:640].rearrange("(k p) c -> p k c", p=128), (), [wr[3]])
            S.dma("gpsimd", wkb[:, :, 64:128], Win[:, 512:576].rearrange("(k p) c -> p k c", p=128), (), [wr[3]])
            for c in range(4):
                inproj_fm(wq[c // 2][:, :, (c % 2) * 128:(c % 2 + 1) * 128], wr[c // 2], hT, hreg, T,
                          lambda tl, c=c: qT[:, c, tl * 512:(tl + 1) * 512], qr)
            memset("gpsimd", kT, 0.0, [kr])
            for wpan, wreg_, vtop, vbot in ((wkv[:, :, 0:128], wr[2], 0, 3), (wkb[:, :, :], wr[3], 2, 1)):
                for tl in range(T // 512):
                    pt, pr = nb()
                    mm(pt[:], [(wpan[:, k, :], hT[:, k, tl * 512:(tl + 1) * 512]) for k in range(8)], [wreg_, hreg], [pr])
                    cp("scalar", kT[0:64, vtop, tl * 512:(tl + 1) * 512], pt[0:64, :], [pr], [kr])
                    cp("scalar", kT[64:128, vbot, tl * 512:(tl + 1) * 512], pt[64:128, :], [pr], [kr])
            chk("att_inproj")
            if not sample:
                kvf = [alloc([256], F32) for _ in range(2)]
                kvr = [Reg() for _ in range(2)]
            for b in range(nblk):
                pt, pr = nb()
                if sample:
                    mm(pt[:, 0:128], [(hT[:, k, b * 128:(b + 1) * 128], wkv[:, k, 128:256]) for k in range(8)], [hreg, wr[2]], [pr])
                    evac(vo[:, b, :, 0:64], pt[:, 0:128].rearrange("p (g d) -> p g d", g=2), [pr], [vr])
                else:
                    mm(pt[:, 0:256], [(hT[:, k, b * 128:(b + 1) * 128], wkv[:, k, :]) for k in range(8)], [hreg, wr[2]], [pr])
                    i = b % 2
                    cp("scalar", kvf[i], pt[:, 0:256], [pr], [kvr[i]])
                    cp("vector", vo[:, b, :, 0:64], kvf[i][:, 128:256].rearrange("p (g d) -> p g d", g=2), [kvr[i]], [vr])
                    s_, bl = b // 2, b % 2
                    import os
                    if not os.environ.get("NO_KVDMA"):
                        S.dma("sync", nk[s_, l, bl * 128:(bl + 1) * 128, :], kvf[i][:, 0:128], [kvr[i]], ())
                        S.dma("sync", nv[s_, l, bl * 128:(bl + 1) * 128, :], kvf[i][:, 128:256], [kvr[i]], ())
            chk("att_kvtok")
            if sample:
                S.barrier()
                st["off"] = pmark
                cs = alloc([2, 2048], BF16)
                csr = Reg()
                S.dma("sync", cs, A["ropecs"], (), [csr])
                t12 = [(alloc([512], F32), alloc([512], F32), Reg(), Reg()) for _ in range(2)]
                targets = [(qT[:, c, :], Reg()) for c in range(4)] + [(kT[:, v_, :], Reg()) for v_ in range(4)]
                items_ = [(tgt, treg, tl) for tgt, treg in targets for tl in range(4)]

                def rope_p0(i_):
                    tgt, treg, tl = items_[i_]
                    t1, t2, t1r, t2r = t12[i_ % 2]
                    sl = slice(tl * 512, (tl + 1) * 512)
                    pt, pr = nb()
                    mm(pt[:], [(prope[:], tgt[:, sl])], [RC, treg], [pr])
                    tt("vector", t1, pt[:], cs[:, 1, sl], ALU.mult, [pr, csr], [t1r])
                    tt("gpsimd", t2, tgt[:, sl], cs[:, 0, sl], ALU.mult, [treg, csr], [t2r])

                def rope_p1(i_):
                    tgt, treg, tl = items_[i_]
                    t1, t2, t1r, t2r = t12[i_ % 2]
                    sl = slice(tl * 512, (tl + 1) * 512)
                    tt("vector", tgt[:, sl], t1, t2, ALU.add, [t1r, t2r], [treg, qr if i_ < 16 else kr])
                pipeline(len(items_), [rope_p0, rope_p1])
                chk("att_rope")
                kcT = alloc([4, 256], BF16)
                vco = alloc([2, 2, 128], BF16)
                kcr, vcr = Reg(), Reg()
                memset("vector", vco[:, :, :, 64:128], 1.0, [vcr])
                memset("vector", kcT, 0.0, [kcr])
                ckf = alloc([2, 2, 128], F32)
                ckr = Reg()
                for blk in range(2):
                    S.dma("sync", ckf[:, 0, blk, :], A["ck"][l, blk * 128:(blk + 1) * 128, :], (), [ckr])
                    S.dma("sync", ckf[:, 1, blk, 0:64], A["ck"][l, blk * 128:(blk + 1) * 128, 64:128], (), [ckr])
                    S.dma("sync", ckf[:, 1, blk, 64:128], A["ck"][l, blk * 128:(blk + 1) * 128, 0:64], (), [ckr])
                    S.dma("gpsimd", vco[:, blk, :, 0:64], A["cv"][l, blk * 128:(blk + 1) * 128, :].rearrange("p (g d) -> p g d", g=2), (), [vcr])
                for var in range(2):
                    for blk in range(2):
                        pt, pr = nb()
                        tr(pt[:, 0:128], ckf[:, var, blk, :], [ckr], [pr])
                        vtop, vbot = (0, 3) if var == 0 else (2, 1)
                        cp("vector", kcT[0:64, vtop, blk * 128:(blk + 1) * 128], pt[0:64, 0:128], [pr], [kcr])
                        cp("vector", kcT[64:128, vbot, blk * 128:(blk + 1) * 128], pt[64:128, 0:128], [pr], [kcr])
            chk("att_cache")
            nqb = T // 128
            itst = {}

            def att_p0(it):
                qb, g = it // 2, it % 2
                par_ = it % 2
                E = Eall[par_ * 5:par_ * 5 + 5]
                Er = Erall[par_ * 5:par_ * 5 + 5]
                keys = []
                if sample:
                    if qb >= 1:
                        keys.append((kT, kr, (qb - 1) * 128, vo[:, qb - 1, g, :], vr, 0))
                    if qb + 1 < nqb:
                        keys.append((kT, kr, (qb + 1) * 128, vo[:, qb + 1, g, :], vr, 1))
                    keys.append((kT, kr, qb * 128, vo[:, qb, g, :], vr, None))
                    for blk in range(2):
                        keys.append((kcT, kcr, blk * 128, vco[:, blk, g, :], vcr, None))
                else:
                    s_ = qb // 2
                    for blk in range(2):
                        kb_ = s_ * 2 + blk
                        keys.append((kT, kr, kb_ * 128, vo[:, kb_, g, :], vr, None))
                itst[it] = keys
                for ki, (ksrc, ksr, k0, vap, vreg, mk) in enumerate(keys):
                    pt, pr = nb()
                    for r in range(4):
                        h = g * 4 + r
                        var = g * 2 + (h % 2)
                        mm(pt[:, r * 128:(r + 1) * 128],
                           [(ksrc[:, var, k0:k0 + 128], qT[:, h // 2, qb * 128:(qb + 1) * 128])],
                           [ksr, qr], [pr])
                    act(E[ki], pt[:], AF.Exp, [pr], [Er[ki]], scale=0.125)
                    if mk is not None:
                        ev = E[ki].rearrange("p (r q) -> p r q", r=4)
                        tt("gpsimd", ev, ev, masks[:, mk, :].unsqueeze(1).broadcast_to([128, 4, 128]), ALU.mult,
                           [Er[ki], RC], [Er[ki]])

            def att_p1(it):
                qb, g = it // 2, it % 2
                par_ = it % 2
                E = Eall[par_ * 5:par_ * 5 + 5]
                Er = Erall[par_ * 5:par_ * 5 + 5]
                dn, dnr = dnall[par_], dnrall[par_]
                keys = itst.pop(it)
                po, por = nb()
                mm(po[:], [(keys[ki][3], E[ki]) for ki in range(len(keys))] +
                   [(sinkL, esfull[:, g * 4:(g + 1) * 4, :].rearrange("p a b -> p (a b)"))],
                   [Er[ki] for ki in range(len(keys))] + [keys[0][4], keys[-1][4], esr], [por])
                act(dn[64:128, :], po[64:128, :], AF.Ln, [por], [dnr])
                act(dn[64:128, :], dn[64:128, :], AF.Exp, [dnr], [dnr], scale=-1.0)
                for r in range(4):
                    h = g * 4 + r
                    base = (h % 2) * 64
                    tt("vector", attT[base:base + 64, h // 2, qb * 128:(qb + 1) * 128], po[0:64, r * 128:(r + 1) * 128],
                       dn[64:128, r * 128:(r + 1) * 128], ALU.mult, [por, dnr], [ar])
            pipeline(nqb * 2, [att_p0, att_p1])
            chk("att_main")
            out_accum(l, j, attT, ar, 0, 4, T)

        def make_diag(Dm, taps, ntap, treg, dreg):
            tt("vector", Dm, identb[:].unsqueeze(1).broadcast_to([128, ntap, 128]),
               taps.unsqueeze(2).broadcast_to([128, ntap, 128]), ALU.mult, [RC, treg], [dreg])

        def mix_conv(l, j, grp, hT, hreg):
            T, NSEQ, L = grp["T"], grp["NSEQ"], grp["L"]
            Win = A["w_in"][l]
            cw = alloc([2, 34], F32)
            cwr = Reg()
            loadT([A["conv_dw"][l], A["conv_dw_b"][l:l + 1, :], A["conv_ln_g"][l:l + 1, :], A["conv_ln_b"][l:l + 1, :]], 256, cw, cwr)
            Dm = alloc([2, 31, 128], BF16)
            dmr = Reg()
            for c in range(2):
                make_diag(Dm[:, c], cw[:, c, 0:31], 31, cwr, dmr)
            pw = alloc([2, 256], BF16)
            pwr = Reg()
            S.dma("gpsimd", pw, A["conv_pw"][l].rearrange("(k p) c -> p k c", p=128), (), [pwr])
            wp = [alloc([8, 256], BF16) for _ in range(2)]
            wpr = [Reg() for _ in range(2)]
            for pi in range(2):
                panel(wp[pi], wpr[pi], Win, 768 + pi * 256, 256)
            zc = alloc([4, T], BF16)
            zr = Reg()
            for c in range(4):
                inproj_fm(wp[c // 2][:, :, (c % 2) * 128:(c % 2 + 1) * 128], wpr[c // 2], hT, hreg, T,
                          lambda tl, c=c: zc[:, c, tl * 512:(tl + 1) * 512], zr)
            LP = L + 30
            ypad = alloc([2, NSEQ, LP], BF16)
            yr = Reg()
            memset("vector", ypad, 0.0, [yr])
            sgm = alloc([T], BF16)
            sr = Reg()
            for c in range(2):
                act(sgm, zc[:, 2 + c, :], AF.Sigmoid, [zr], [sr])
                tt("vector", ypad[:, c, :, 15:15 + L], zc[:, c, :].rearrange("p (s t) -> p s t", s=NSEQ),
                   sgm.rearrange("p (s t) -> p s t", s=NSEQ), ALU.mult, [zr, sr], [yr])
            n = min(L, 512)
            sets = []
            for _ in range(2):
                sets.append((alloc([2, n], F32), alloc([2, n], BF16), alloc([2, n], BF16), alloc([n], F32), alloc([n], F32),
                             alloc([2, n], BF16), Reg(), Reg(), Reg(), Reg()))
            cvT = alloc([2, T], BF16)
            cvr = Reg()
            tiles_ = [(s_, t0) for s_ in range(NSEQ) for t0 in range(0, L, n)]

            def conv_p0(ti):
                s_, t0 = tiles_[ti]
                yc, ycb, ysq, mean, var, sT, r1, r2, r3, r4 = sets[ti % 2]
                for c in range(2):
                    pt, pr = nb()
                    mm(pt[:, 0:n], [(Dm[:, c, k, :], ypad[:, c, s_, t0 + k:t0 + k + n]) for k in range(31)], [dmr, yr], [pr])
                    act(yc[:, c, :], pt[:, 0:n], AF.Identity, [pr, cwr], [r1], bias=cw[:, c, 31:32])
                cp("vector", ycb, yc, [r1], [r2])
                act(ysq, yc, AF.Square, [r1], [r2])

            def conv_p1(ti):
                s_, t0 = tiles_[ti]
                yc, ycb, ysq, mean, var, sT, r1, r2, r3, r4 = sets[ti % 2]
                p1, p1r = nb()
                p2, p2r = nb()
                mm(p1[:, 0:n], [(onesb[:], ycb[:, c, :]) for c in range(2)], [RC, r2], [p1r])
                mm(p2[:, 0:n], [(onesb[:], ysq[:, c, :]) for c in range(2)], [RC, r2], [p2r])
                act(mean, p1[:, 0:n], AF.Copy, [p1r], [r3], scale=1.0 / 256)
                tt("vector", var, mean, mean, ALU.mult, [r3], [r3])
                stt(var, p2[:, 0:n], 1.0 / 256, var, ALU.mult, ALU.subtract, [p2r, r3], [r3])
                act(var, var, AF.Sqrt, [r3, RC], [r3], bias=epsc[:, 0:1])
                recip(var, r3)
                for c in range(2):
                    tt("vector", yc[:, c, :], yc[:, c, :], mean, ALU.subtract, [r1, r3], [r1])
                    tt("vector", yc[:, c, :], yc[:, c, :], var, ALU.mult, [r1, r3], [r1])
                    act(sT[:, c, :], yc[:, c, :], AF.Silu, [r1, cwr], [r4], scale=cw[:, c, 32:33], bias=cw[:, c, 33:34])
                for co in range(2):
                    pt, pr = nb()
                    mm(pt[:, 0:n], [(pw[:, ci, co * 128:(co + 1) * 128], sT[:, ci, :]) for ci in range(2)], [pwr, r4], [pr])
                    evac(cvT[:, co, s_ * L + t0:s_ * L + t0 + n], pt[:, 0:n], [pr], [cvr])
            pipeline(len(tiles_), [conv_p0, conv_p1])
            out_accum(l, j, cvT, cvr, 512, 2, T)

        def mix_hyena(l, j, grp, hT, hreg):
            T, NSEQ, L = grp["T"], grp["NSEQ"], grp["L"]
            sample = grp["sample"]
            Win = A["w_in"][l]
            NTB = L // 128
            zfn, tn, phi = ("zfS", tnS, phiS) if sample else ("zfP", tnP, phiP)
            NR = min(L, 512)
            hw_ = alloc([6, 4], F32)
            hwr = Reg()
            loadT([A["hy_short_w"][l], A["hy_short_b"][l:l + 1, :]], 768, hw_, hwr)
            hb_ = alloc([2, 2], F32)
            loadT([A["hy_bias"][l]], 256, hb_, hwr)
            Dm = alloc([6, 3, 128], BF16)
            dmr = Reg()
            for c in range(6):
                make_diag(Dm[:, c], hw_[:, c, 0:3], 3, hwr, dmr)
            fw = alloc([64 + 64 + 1024 + 8], F32)
            fr = Reg()
            S.dma("sync", fw[0:33, 0:64], A["hy_w1"][l], (), [fr])
            S.dma("sync", fw[0:64, 64:128], A["hy_w2"][l], (), [fr])
            S.dma("sync", fw[0:64, 128:1152], A["hy_w3"][l], (), [fr])
            for i_, nm in enumerate(["hy_f1", "hy_b1", "hy_f2", "hy_b2"]):
                S.dma("sync", fw[0:64, 1152 + i_:1153 + i_], dap(nm, l * 64, [[1, 64], [1, 1]]), (), [fr])
            sc = alloc([8], F32)
            scr_ = Reg()
            for m_ in range(2):
                f_ = fw[0:64, 1152 + 2 * m_:1153 + 2 * m_]
                b_ = fw[0:64, 1153 + 2 * m_:1154 + 2 * m_]
                ts("vector", sc[0:64, 4 * m_ + 0:4 * m_ + 1], f_, 0.5, ALU.mult, [fr], [scr_])
                ts("vector", sc[0:64, 4 * m_ + 2:4 * m_ + 3], f_, 0.25, ALU.mult, [fr], [scr_])
                tt("vector", sc[0:64, 4 * m_ + 1:4 * m_ + 2], sc[0:64, 4 * m_ + 0:4 * m_ + 1], b_, ALU.mult, [fr, scr_], [scr_])
                tt("vector", sc[0:64, 4 * m_ + 3:4 * m_ + 4], sc[0:64, 4 * m_ + 2:4 * m_ + 3], b_, ALU.mult, [fr, scr_], [scr_])
            dec = alloc([1024], F32)
            dcr = Reg()
            S.dma("sync", dec, dap("hy_log_decay", l * 1024, [[0, 128], [1, 1024]]), (), [dcr])
            act(dec, dec, AF.Exp, [dcr], [dcr])
            vvx = alloc([6, T], BF16)
            vxr = Reg()
            mark = st["off"]
            upad = alloc([6, NSEQ, L + 2], BF16)
            ur = Reg()
            memset("vector", upad, 0.0, [ur])
            wp = [alloc([8, 256], BF16) for _ in range(3)]
            wpr = [Reg() for _ in range(3)]
            for pi in range(3):
                panel(wp[pi], wpr[pi], Win, 1280 + pi * 256, 256)
            for c in range(6):
                inproj_fm(wp[c // 2][:, :, (c % 2) * 128:(c % 2 + 1) * 128], wpr[c // 2], hT, hreg, T,
                          (lambda tl, c=c: upad[:, c, tl * (512 // L):(tl + 1) * (512 // L), 1:1 + L]) if L < 512 else
                          (lambda tl, c=c: upad[:, c, 0, 1 + tl * 512:1 + (tl + 1) * 512]), ur)
            n = min(L, 512)
            for c in range(6):
                for s_ in range(NSEQ):
                    for t0 in range(0, L, n):
                        pt, pr = nb()
                        mm(pt[:, 0:n], [(Dm[:, c, k, :], upad[:, c, s_, t0 + k:t0 + k + n]) for k in range(3)], [dmr, ur], [pr])
                        act(vvx[:, c, s_ * L + t0:s_ * L + t0 + n], pt[:, 0:n], AF.Identity, [pr, hwr], [vxr], bias=hw_[:, c, 3:4])
            S.barrier()
            st["off"] = mark
            st["lim"] = WSW - (4096 if sample else 0)

            def alloc_fixed(i_):
                return mkview(WSW - 4096 + i_ * 1024, 1024, [4, 512], BF16)
            y1T = alloc([2, T], BF16)
            y1r = Reg()
            mark2 = st["off"]
            for o in range(2):
                S.barrier()
                st["off"] = mark2
                Hs = alloc([NTB, 2, 256], BF16)
                Hr_ = Reg()
                nrm = alloc([256], F32)
                nr = Reg()
                markB = st["off"]
                hf_ = alloc([NTB, 512], BF16)
                hfr = Reg()
                markA = st["off"]
                assert markA <= WSW - 4096
                st["lim"] = WSW
                zfs = [alloc([512], F32) for _ in range(2)]
                zrs = [Reg() for _ in range(2)]
                h2 = alloc([2 * L], F32)
                hr2 = Reg()
                h1s = [alloc([512], F32) for _ in range(2)]
                hr1s = [Reg() for _ in range(2)]
                sa = [(alloc([512], F32), alloc([512], F32), Reg(), Reg()) for _ in range(2)]
                sb_ = [(alloc([512], F32), alloc([512], F32), Reg(), Reg()) for _ in range(2)]
                t0s = list(range(0, 2 * L, 512))

                def sin_pair(pt, pr, m_, s2, s4, s2r, s4r):
                    act(s2[0:64, :], pt[0:64, :], AF.Sin, [pr, scr_], [s2r], scale=sc[0:64, 4 * m_:4 * m_ + 1], bias=sc[0:64, 4 * m_ + 1:4 * m_ + 2])
                    act(s4[0:64, :], pt[0:64, :], AF.Sin, [pr, scr_], [s4r], scale=sc[0:64, 4 * m_ + 2:4 * m_ + 3], bias=sc[0:64, 4 * m_ + 3:4 * m_ + 4])

                def sin_fin(dst_ap, dreg_, s2, s4, s2r, s4r):
                    tt("vector", s4[0:64, :], s4[0:64, :], s4[0:64, :], ALU.mult, [s4r], [s4r])
                    ts("vector", s4[0:64, :], s4[0:64, :], -4.0, ALU.mult, [s4r], [s4r], s2=2.0, op1=ALU.add)
                    tt("vector", dst_ap, s2[0:64, :], s4[0:64, :], ALU.mult, [s2r, s4r], [dreg_])

                def mlp_p0(ti_):
                    zf, zr_ = zfs[ti_ % 2], zrs[ti_ % 2]
                    S.dma("sync", zf[0:33, :], A[zfn][:, t0s[ti_]:t0s[ti_] + 512], (), [zr_])
                    pt, pr = nb()
                    mm(pt[0:64, :], [(fw[0:33, 0:64], zf[0:33, :])], [fr, zr_], [pr])
                    sin_pair(pt, pr, 0, *sa[ti_ % 2])

                def mlp_p1(ti_):
                    h1, hr1 = h1s[ti_ % 2], hr1s[ti_ % 2]
                    sin_fin(h1[0:64, :], hr1, *sa[ti_ % 2])
                    pt, pr = nb()
                    mm(pt[0:64, :], [(fw[0:64, 64:128], h1[0:64, :])], [fr, hr1], [pr])
                    sin_pair(pt, pr, 1, *sb_[ti_ % 2])

                def mlp_p2(ti_):
                    sin_fin(h2[0:64, t0s[ti_]:t0s[ti_] + 512], hr2, *sb_[ti_ % 2])
                pipeline(len(t0s), [mlp_p0, mlp_p1, mlp_p2])
                ets = [(alloc([512], F32), alloc([512], F32), alloc([512], BF16), Reg(), Reg(), Reg()) for _ in range(2)]
                def hf_p0(tb, o=o):
                    et, hwt, sqb, etr, hwr2, sqr = ets[tb % 2]
                    pt, pr = nb()
                    for d_ in range(2):
                        cols = slice(128 + o * 512 + d_ * 256, 128 + o * 512 + (d_ + 1) * 256)
                        tbb = tb + d_ * NTB
                        mm(pt[:, d_ * 256:(d_ + 1) * 256], [(h2[0:64, tbb * 128:(tbb + 1) * 128], fw[0:64, cols])], [hr2, fr], [pr])
                        act(et[:, d_ * 256:(d_ + 1) * 256], dec[:, o * 512 + d_ * 256:o * 512 + (d_ + 1) * 256], AF.Exp, [dcr, RC], [etr],
                            scale=tn[:, tbb:tbb + 1])
                    tt("vector", hwt, pt[:], et, ALU.mult, [pr, etr], [hwr2])

                def hf_p1(tb, hf_=hf_, hfr=hfr):
                    et, hwt, sqb, etr, hwr2, sqr = ets[tb % 2]
                    cp("vector", hf_[:, tb, :], hwt, [hwr2], [hfr])
                    act(sqb, hwt, AF.Square, [hwr2], [sqr])
                    S.op("tensor", lambda e, tb=tb, sqb=sqb, NTB=NTB: e.matmul(PS[7][:], lhsT=onesb[:], rhs=sqb, start=(tb == 0), stop=(tb == NTB - 1)),
                         [RC, sqr], [PR[7]])
                pipeline(NTB, [hf_p0, hf_p1])
                memset("vector", hf_[0:1, 0, 256:512], 0.0, [hfr])
                tt("vector", nrm, PS[7][:, 0:256], epsc[:, 0:1].broadcast_to([128, 256]), ALU.add, [PR[7], RC], [nr])
                tt("vector", nrm, nrm, PS[7][:, 256:512], ALU.add, [PR[7], nr], [nr])
                act(nrm, nrm, AF.Sqrt, [nr], [nr])
                recip(nrm, nr)
                S.barrier()
                st["off"] = markA
                st["lim"] = WSW - (4096 if sample else 0)
                if sample:
                    dbuf = [alloc_fixed(i_) for i_ in range(4)]
                    dbr = [Reg() for _ in range(4)]
                    dst_ = {"i": 0}

                    def dft_tiles(mat, fr_, tg):
                        i = dst_["i"]
                        dst_["i"] = (i + 1) % 4
                        S.dma("sync", dbuf[i], A["dftS"][mat, tg * 512:(tg + 1) * 512, fr_ * 512:(fr_ + 1) * 512].rearrange("(k p) c -> p k c", p=128),
                              (), [dbr[i]])
                        return [dbuf[i][:, k, :] for k in range(4)], dbr[i]
                    NTG, TPG = 4, 4
                else:
                    def dft_tiles(mat, fr_, tg):
                        return [dftP[:, mat, k, :] for k in range(2)], RC
                    NTG, TPG = 1, 2
                NFR = L // NR
                FPR = NR // 128
                acss = [(alloc([FPR, 512], F32), Reg()) for _ in range(2)]
                tABs = [(alloc([512], F32), alloc([512], F32), Reg(), Reg()) for _ in range(2)]
                for fr_ in range(NFR):
                    acs, acr = acss[fr_ % 2]
                    for mat in range(2):
                        banks = [nb() for _ in range(FPR)]
                        for tg in range(NTG):
                            tiles, treg = dft_tiles(mat, fr_, tg)
                            for k in range(TPG):
                                tb = tg * TPG + k
                                for fc in range(FPR):
                                    S.op("tensor", lambda e, fc=fc, k=k, tb=tb, tiles=tiles, banks=banks, hf_=hf_, NTB=NTB: e.matmul(
                                        banks[fc][0][:], lhsT=tiles[k][:, fc * 128:(fc + 1) * 128], rhs=hf_[:, tb, :],
                                        start=(tb == 0), stop=(tb == NTB - 1)), [treg, hfr], [banks[fc][1]])
                        if mat == 0:
                            for fc in range(FPR):
                                evac(acs[:, fc, :], banks[fc][0][:], [banks[fc][1]], [acr])
                        else:
                            for fc in range(FPR):
                                fg = fr_ * FPR + fc
                                tA, tB, tAr, tBr = tABs[fc % 2]
                                Pc, Qc = acs[:, fc, 0:256], acs[:, fc, 256:512]
                                Ps_, Qs = banks[fc][0][:, 0:256], banks[fc][0][:, 256:512]
                                sg_, cph, sph = phi[:, fg, 2:3], phi[:, fg, 0:1], phi[:, fg, 1:2]
                                stt(tA[:, 0:256], Qs, sg_, Pc, ALU.mult, ALU.add, [banks[fc][1], acr, RC], [tAr])
                                stt(tA[:, 256:512], Qc, sg_, Ps_, ALU.mult, ALU.subtract, [banks[fc][1], acr, RC], [tAr])
                                act(tB[:, 0:256], tA[:, 256:512], AF.Identity, [tAr, RC], [tBr], scale=sph)
                                stt(tB[:, 0:256], tA[:, 0:256], cph, tB[:, 0:256], ALU.mult, ALU.subtract, [tAr, tBr, RC], [tBr])
                                act(tB[:, 256:512], tA[:, 256:512], AF.Identity, [tAr, RC], [tBr], scale=cph)
                                stt(tB[:, 256:512], tA[:, 0:256], sph, tB[:, 256:512], ALU.mult, ALU.add, [tAr, tBr, RC], [tBr])
                                tt("vector", Hs[:, fg, :, :], tB.rearrange("p (a c) -> p a c", a=2),
                                   nrm.unsqueeze(1).broadcast_to([128, 2, 256]), ALU.mult, [tBr, nr], [Hr_])
                S.barrier()
                st["off"] = markB
                tABs = [(alloc([512], F32), alloc([512], F32), Reg(), Reg()) for _ in range(2)]
                ncp = 2 if NSEQ > 1 else 1
                vtoks = [(alloc([NTB, 256], BF16), Reg()) for _ in range(ncp)]
                Yss = [(alloc([NTB, 2, 256], BF16), Reg()) for _ in range(ncp)]
                ucss = [(alloc([FPR, 256], F32), Reg()) for _ in range(2)]
                yscs = [(alloc([512], F32), Reg()) for _ in range(2)]
                ysi = {"i": 0}

                def next_ysc():
                    ysi["i"] += 1
                    return yscs[ysi["i"] % 2]
                def c_p0(s_):
                    vtok, vtr = vtoks[s_ % ncp]
                    for tb in range(NTB):
                        pt, pr = nb()
                        ysc, ysr = next_ysc()
                        for c in range(2):
                            if o == 0:
                                cp("gpsimd", ysc[:, c * 128:(c + 1) * 128], vvx[:, c, s_ * L + tb * 128:s_ * L + (tb + 1) * 128], [vxr], [ysr])
                            else:
                                cp("gpsimd", ysc[:, c * 128:(c + 1) * 128], y1T[:, c, s_ * L + tb * 128:s_ * L + (tb + 1) * 128], [y1r], [ysr])
                        for c in range(2):
                            tr(pt[:, c * 128:(c + 1) * 128], ysc[:, c * 128:(c + 1) * 128], [ysr], [pr])
                        evac(vtok[:, tb, :], pt[:, 0:256], [pr], [vtr])

                def c_p1(s_):
                    vtok, vtr = vtoks[s_ % ncp]
                    Ys, Yr_ = Yss[s_ % ncp]
                    for fr_ in range(NFR):
                        ucs, ucr = ucss[(s_ * NFR + fr_) % 2]
                        for mat in range(2):
                            banks = [nb() for _ in range(FPR)]
                            for tg in range(NTG):
                                tiles, treg = dft_tiles(mat, fr_, tg)
                                for k in range(TPG):
                                    tb = tg * TPG + k
                                    for fc in range(FPR):
                                        S.op("tensor", lambda e, fc=fc, k=k, tb=tb, tiles=tiles, banks=banks, vtok=vtok, NTB=NTB: e.matmul(
                                            banks[fc][0][:, 0:256], lhsT=tiles[k][:, fc * 128:(fc + 1) * 128], rhs=vtok[:, tb, :],
                                            start=(tb == 0), stop=(tb == NTB - 1)), [treg, vtr], [banks[fc][1]])
                            if mat == 0:
                                for fc in range(FPR):
                                    evac(ucs[:, fc, :], banks[fc][0][:, 0:256], [banks[fc][1]], [ucr])
                            else:
                                sc2 = 2.0 / (2 * L)
                                for fc in range(FPR):
                                    fg = fr_ * FPR + fc
                                    tA, tB, tAr, tBr = tABs[fc % 2]
                                    Uc, Us = ucs[:, fc, :], banks[fc][0][:, 0:256]
                                    Hre, Him = Hs[:, fg, 0, :], Hs[:, fg, 1, :]
                                    tt("vector", tA[:, 0:256], Us, Him, ALU.mult, [banks[fc][1], Hr_], [tAr])
                                    tt("vector", tA[:, 256:512], Us, Hre, ALU.mult, [banks[fc][1], Hr_], [tAr])
                                    tt("gpsimd", tB[:, 0:256], Uc, Hre, ALU.mult, [ucr, Hr_], [tBr])
                                    tt("gpsimd", tB[:, 256:512], Uc, Him, ALU.mult, [ucr, Hr_], [tBr])
                                    tt("vector", tA[:, 0:256], tA[:, 0:256], tB[:, 0:256], ALU.add, [tAr, tBr], [tAr])
                                    tt("vector", tA[:, 256:512], tA[:, 256:512], tB[:, 256:512], ALU.subtract, [tAr, tBr], [tAr])
                                    act(Ys[:, fg, :, :], tA.rearrange("p (a c) -> p a c", a=2), AF.Copy, [tAr], [Yr_], scale=sc2)

                def c_p2(s_):
                    Ys, Yr_ = Yss[s_ % ncp]
                    for tr_ in range(NFR):
                        banks = [nb() for _ in range(2)]
                        first = True
                        for mat in range(2):
                            for fgp in range(NTG):
                                tiles, treg = dft_tiles(mat, tr_, fgp)
                                for k in range(TPG):
                                    fg = fgp * TPG + k
                                    last = (mat == 1 and fg == NTB - 1)
                                    for c in range(2):
                                        S.op("tensor", lambda e, c=c, k=k, fg=fg, mat=mat, tiles=tiles, banks=banks, first=first, last=last, Ys=Ys, NR=NR: e.matmul(
                                            banks[c][0][:, 0:NR], lhsT=Ys[:, fg, mat, c * 128:(c + 1) * 128], rhs=tiles[k],
                                            start=first, stop=last), [treg, Yr_], [banks[c][1]])
                                    first = False
                        for c in range(2):
                            tsl = slice(s_ * L + tr_ * NR, s_ * L + (tr_ + 1) * NR)
                            ysc, ysr = next_ysc()
                            if o == 0:
                                stt(ysc[:, 0:NR], vvx[:, c, tsl], hb_[:, c, 0:1], banks[c][0][:, 0:NR], ALU.mult, ALU.add, [vxr, hwr, banks[c][1], ysr], [ysr])
                                tt("vector", y1T[:, c, tsl], ysc[:, 0:NR], vvx[:, 2 + c, tsl], ALU.mult, [ysr, vxr], [y1r])
                            else:
                                stt(ysc[:, 0:NR], y1T[:, c, tsl], hb_[:, c, 1:2], banks[c][0][:, 0:NR], ALU.mult, ALU.add, [y1r, hwr, banks[c][1], ysr], [ysr])
                                tt("vector", vvx[:, c, tsl], ysc[:, 0:NR], vvx[:, 4 + c, tsl], ALU.mult, [ysr, vxr], [vxr])
                pipeline(NSEQ, [c_p0, c_p1, c_p2])
            S.barrier()
            st["lim"] = WSW
            out_accum(l, j, vvx, vxr, 768, 2, T)

        def mix(l, j, grp):
            T = grp["T"]
            arena_reset()
            hT = alloc_top([8, T], BF16)
            hreg = Reg()
            base = st["off"]
            scr = norm_scratch()
            norm_many([(t0, 256, COEF[:, l, j, 3, :], COEF[:, l, j, 4, :], hT[:, :, t0:t0 + 256], hreg) for t0 in range(0, T, 256)], scr)
            for fn in (mix_attention, mix_conv, mix_hyena):
                S.barrier()
                st["off"] = base
                fn(l, j, grp, hT, hreg)
                chk(fn.__name__)

        def final_out(yd, T):
            arena_reset()
            scr = norm_scratch()
            yTs = [alloc([8, 256], F32) for _ in range(2)]
            yregs = [Reg() for _ in range(2)]
            yo = [alloc([D], F32) for _ in range(2)]
            yor = [Reg() for _ in range(2)]
            nt = T // 256

            def outphase(it):
                yT, yreg = yTs[it % 2], yregs[it % 2]
                for b2 in range(2):
                    b = it * 2 + b2
                    i = b % 2
                    for hf in range(2):
                        pt, pr = nb()
                        for c4 in range(4):
                            tr(pt[:, c4 * 128:(c4 + 1) * 128], yT[:, hf * 4 + c4, b2 * 128:(b2 + 1) * 128], [yreg], [pr])
                        evac(yo[i][:, hf * 512:(hf + 1) * 512], pt[:], [pr], [yor[i]])
                    S.dma("sync", yd[b * 128:(b + 1) * 128, :], yo[i], [yor[i]], ())
            norm_many([(it * 256, 256, GT[:, 0, 48:56], None, yTs[it % 2], yregs[it % 2]) for it in range(nt)], scr, extra=outphase)

        groups = [dict(T=2048, NSEQ=1, L=2048, sample=True, x="xs", y=ys, j=1),
                  dict(T=1024, NSEQ=4, L=256, sample=False, x="xp", y=yp, j=0)]
        try:
            chk("mod")
            for gi_, grp in enumerate(groups):
                if gi_ > 0:
                    load_x(A[grp["x"]], grp["T"])
                chk("load_x")
                for l in range(2):
                    ffn(l, 1, grp["j"], grp["T"])
                    chk("ffn1")
                    mix(l, grp["j"], grp)
                    ffn(l, 2, grp["j"], grp["T"])
                    chk("ffn2")
                final_out(grp["y"], grp["T"])
                chk("final")
        except StopBuild:
            pass
        if dbg is not None:
            S.barrier()
            S.dma("sync", dbg, xT[:], [RX], [ROUT])
        S.barrier()
        print("instructions:", S.ninst, {e: len(S.prog[e]) for e in ENGS})
        with nc.Block() as block:
            S.emit(block)
    return nc


_CACHE = {}


def kernel(**inputs):
    consts = host_consts()
    n = 8
    f32 = np.float32
    shared = {k: np.ascontiguousarray(inputs[k], dtype=f32) for k in WEIGHT_NAMES}
    in_maps = []
    for i in range(n):
        m = dict(shared)
        m["xs"] = np.ascontiguousarray(inputs["x_sample"][i], dtype=f32)
        m["xp"] = np.ascontiguousarray(inputs["x_prompt"][4 * i:4 * i + 4].reshape(1024, D), dtype=f32)
        m["cc"] = np.ascontiguousarray(np.stack([inputs["c_ctx"], inputs["c"][i]]), dtype=f32)
        m["ck"] = np.ascontiguousarray(inputs["cache_k"][i].reshape(2, 256, 128), dtype=f32)
        m["cv"] = np.ascontiguousarray(inputs["cache_v"][i].reshape(2, 256, 128), dtype=f32)
        m.update(consts)
        in_maps.append(m)
    if "nc" not in _CACHE:
        shapes = {k: (v.shape, F32) for k, v in in_maps[0].items() if k not in consts}
        cshapes = {k: (v.shape, BF16 if v.dtype == ml_dtypes.bfloat16 else F32) for k, v in consts.items()}
        _CACHE["nc"] = build_nc(shapes, cshapes)
    res = run_bass_kernel_spmd(_CACHE["nc"], in_maps, core_ids=list(range(n)))
    R = res.results
    y_prompt = np.concatenate([r["yp"].reshape(4, 256, D) for r in R], 0).astype(f32)
    y_sample = np.stack([r["ys"] for r in R], 0).astype(f32)
    new_k = np.concatenate([r["nk"].reshape(4, 2, 256, 2, 64) for r in R], 0).astype(f32)
    new_v = np.concatenate([r["nv"].reshape(4, 2, 256, 2, 64) for r in R], 0).astype(f32)
    return (y_prompt, y_sample, new_k, new_v)
```

```python
import contextlib
import math
import numpy as np
import ml_dtypes
import concourse.bass as bass
import concourse.mybir as mybir
from concourse.bass_utils import run_bass_kernel_spmd

F32 = mybir.dt.float32
BF16 = mybir.dt.bfloat16
AF = mybir.ActivationFunctionType
ALU = mybir.AluOpType

ENGS = ("tensor", "vector", "scalar", "gpsimd", "sync")
EPOCH = 30000
D = 1024
DFF = 2816
NFC = 22
EPS = 1e-6


class Reg:
    __slots__ = ("w", "r", "psum")

    def __init__(self, psum=False):
        self.w = None
        self.r = {}
        self.psum = psum


class Sched:
    def __init__(self, sems):
        self.free_sems = list(sems)
        self.prog = {e: [] for e in ENGS}
        self.cur = {}
        self.cnt = {}
        self.waited = {e: {} for e in ENGS}
        self.pe_sems = set()
        self.own = {e: set() for e in ENGS}
        for e in ENGS:
            self._new_epoch(e)
        self.pools = {}
        self.ninst = 0

    def _new_epoch(self, e):
        self.cur[e] = self.free_sems.pop()
        self.own[e].add(self.cur[e])
        if e == "tensor":
            self.pe_sems.add(self.cur[e])
        self.cnt[e] = 0

    def make_pool(self, q, n):
        self.pools[q] = {"sems": [self.free_sems.pop() for _ in range(n)], "cnt": [0] * n, "i": 0}

    def _need(self, eng, toks):
        need = {}
        for k, v in toks:
            if need.get(k, 0) < v:
                need[k] = v
        wd = self.waited[eng]
        for k, v in need.items():
            if eng == "tensor" and k in self.pe_sems:
                continue
            if wd.get(k, 0) < v:
                self.prog[eng].append(("wait", k, v))
                wd[k] = v

    def _deps(self, eng, reads, writes):
        toks = []
        for r in reads:
            if r.w is not None:
                toks.append(r.w)
            if r.psum:
                own = self.own[eng]
                toks.extend((k, v) for k, v in r.r.items() if k not in own)
        own = self.own[eng]
        for w in writes:
            if w.w is not None and w.w[0] not in own:
                toks.append(w.w)
            toks.extend((k, v) for k, v in w.r.items() if k not in own)
        self._need(eng, toks)

    def _commit(self, tok, reads, writes):
        for r in reads:
            if r.r.get(tok[0], 0) < tok[1]:
                r.r[tok[0]] = tok[1]
        for w in writes:
            w.w = tok
            w.r = {}

    def op(self, eng, fns, reads=(), writes=()):
        if not isinstance(fns, (list, tuple)):
            fns = [fns]
        self._deps(eng, reads, writes)
        if self.cnt[eng] >= EPOCH:
            self._new_epoch(eng)
        self.cnt[eng] += 1
        tok = (self.cur[eng], self.cnt[eng])
        for f in fns[:-1]:
            self.prog[eng].append(("inst", f, None, 0))
        self.prog[eng].append(("inst", fns[-1], tok[0], 1))
        self.ninst += len(fns)
        self._commit(tok, reads, writes)

    def dma(self, q, out, in_, reads=(), writes=()):
        self._deps(q, reads, writes)
        p = self.pools[q]
        i = p["i"]
        p["i"] = (i + 1) % len(p["sems"])
        s = p["sems"][i]
        if p["cnt"][i] > 0:
            self._need(q, [(s, 16 * p["cnt"][i])])
        p["cnt"][i] += 1
        tok = (s, 16 * p["cnt"][i])
        self.prog[q].append(("inst", lambda e, out=out, in_=in_: e.dma_start(out=out, in_=in_), s, 16))
        self.ninst += 1
        self._commit(tok, reads, writes)

    def barrier(self):
        toks = [(self.cur[e], self.cnt[e]) for e in ENGS if self.cnt[e] > 0]
        for p in self.pools.values():
            for s, c in zip(p["sems"], p["cnt"]):
                if c > 0:
                    toks.append((s, 16 * c))
        for e in ENGS:
            self._need(e, toks)

    def emit(self, block):
        for e in ENGS:
            entries = self.prog[e]

            def body(eng, entries=entries):
                for ent in entries:
                    if ent[0] == "wait":
                        eng.wait_ge(ent[1], ent[2])
                    else:
                        ins = ent[1](eng)
                        if ent[2] is not None:
                            ins.then_inc(ent[2], ent[3])
            getattr(block, e)(body)


def host_consts():
    c = {}
    c["ident"] = np.eye(128, dtype=np.float32)
    c["onesb"] = np.ones((128, 128), dtype=ml_dtypes.bfloat16)
    P = np.zeros((128, 128), np.float32)
    for hb in (0, 64):
        for part in (0, 32):
            for i in range(16):
                a = hb + part + i
                b = a + 16
                P[b, a] = -1.0
                P[a, b] = 1.0
    c["prope"] = P.astype(ml_dtypes.bfloat16)
    t = np.arange(2048)
    row = (t // 64).astype(np.float64)
    col = (t % 64).astype(np.float64)
    inv = (10000.0 ** (-np.arange(16, dtype=np.float32) / 16)).astype(np.float64)
    ang = np.zeros((64, 2048))
    ang[0:16] = inv[:, None] * row[None]
    ang[16:32] = inv[:, None] * row[None]
    ang[32:48] = inv[:, None] * col[None]
    ang[48:64] = inv[:, None] * col[None]
    cs = np.zeros((128, 2, 2048), np.float32)
    cs[:, 0] = np.tile(np.cos(ang), (2, 1))
    cs[:, 1] = np.tile(np.sin(ang), (2, 1))
    c["ropecs"] = cs.astype(ml_dtypes.bfloat16)
    kk = np.arange(128)[:, None]
    qi = np.arange(128)[None, :]
    m = np.zeros((128, 2, 128), np.float32)
    m[:, 0] = (kk >= qi)
    m[:, 1] = (kk <= qi)
    c["masks"] = m.astype(ml_dtypes.bfloat16)

    def dft(L):
        N = 2 * L
        tt = np.arange(L, dtype=np.float64)[:, None] + 0.5
        ff = np.arange(L, dtype=np.float64)[None, :] + 0.5
        th = 2 * np.pi * tt * ff / N
        return np.stack([np.cos(th), np.sin(th)]).astype(ml_dtypes.bfloat16)

    def phis(L):
        N = 2 * L
        f = np.arange(L, dtype=np.float64)
        ph = np.stack([np.cos(np.pi * (f + .5) / N), np.sin(np.pi * (f + .5) / N), (-1.0) ** f], -1)
        return np.ascontiguousarray(ph.reshape(L // 128, 128, 3).transpose(1, 0, 2)).astype(np.float32)

    def zfeat(L):
        tf = np.arange(L)
        pos = np.concatenate([tf, (L - tf) % L]).astype(np.float64)
        tn = pos / (L - 1)
        bands = np.linspace(1e-4, 15, 16)
        ang = 2 * np.pi * pos[:, None] * bands[None, :] / L
        z = np.concatenate([tn[:, None], np.cos(ang), -np.sin(ang)], -1)
        tnneg = np.ascontiguousarray((-tn).reshape(2 * L // 128, 128).T).astype(np.float32)
        return np.ascontiguousarray(z.T).astype(np.float32), tnneg

    c["dftS"] = dft(2048)
    dp = dft(256)
    c["dftP"] = np.ascontiguousarray(dp.reshape(2, 2, 128, 256).transpose(2, 0, 1, 3))
    c["phiS"] = phis(2048)
    c["phiP"] = phis(256)
    c["zfS"], c["tnS"] = zfeat(2048)
    c["zfP"], c["tnP"] = zfeat(256)
    return c


WEIGHT_NAMES = ["w_mod", "b_mod", "g_ffn1", "g_mix", "g_ffn2", "g_final", "w1_gate", "w1_up", "w1_down",
                "w2_gate", "w2_up", "w2_down", "w_in", "w_out", "attn_sink", "conv_dw", "conv_dw_b",
                "conv_ln_g", "conv_ln_b", "conv_pw", "hy_short_w", "hy_short_b", "hy_w1", "hy_b1", "hy_f1",
                "hy_w2", "hy_b2", "hy_f2", "hy_w3", "hy_log_decay", "hy_bias"]


class StopBuild(Exception):
    pass


def build_nc(shapes, cshapes, stop=None):
    nc = bass.Bass("TRN2", target_bir_lowering=False)
    dbg = nc.dram_tensor("dbg", [128, 8, 2048], F32, kind="ExternalOutput").ap() if stop is not None else None
    stage = {"n": 0}

    def chk(name):
        stage["n"] += 1
        if stop is not None:
            print("stage", stage["n"], name)
            if stage["n"] >= stop:
                raise StopBuild()

    Hd = {}
    for n, (shp, dt) in {**shapes, **cshapes}.items():
        Hd[n] = nc.dram_tensor(n, list(shp), dt, kind="ExternalInput")
    A = {n: h.ap() for n, h in Hd.items()}
    yp = nc.dram_tensor("yp", [1024, D], F32, kind="ExternalOutput").ap()
    ys = nc.dram_tensor("ys", [2048, D], F32, kind="ExternalOutput").ap()
    nk = nc.dram_tensor("nk", [4, 2, 256, 128], F32, kind="ExternalOutput").ap()
    nv = nc.dram_tensor("nv", [4, 2, 256, 128], F32, kind="ExternalOutput").ap()

    with contextlib.ExitStack() as es:
        def sb(name, shape, dt):
            return es.enter_context(nc.sbuf_tensor("sb_" + name, shape, dt))
        sems = [es.enter_context(nc.semaphore(f"s{i}")) for i in range(100)]
        S = Sched(sems)
        S.make_pool("sync", 16)
        S.make_pool("gpsimd", 16)
        S.make_pool("scalar", 8)

        xT = sb("xT", [128, 8, 2048], F32)
        ident = sb("ident", [128, 128], F32)
        onesb = sb("onesb", [128, 128], BF16)
        prope = sb("prope", [128, 128], BF16)
        masks = sb("masks", [128, 2, 128], BF16)
        dftP = sb("dftP", [128, 2, 2, 256], BF16)
        phiS = sb("phiS", [128, 16, 3], F32)
        phiP = sb("phiP", [128, 2, 3], F32)
        tnS = sb("tnS", [128, 32], F32)
        tnP = sb("tnP", [128, 4], F32)
        COEF = sb("COEF", [128, 2, 2, 9, 8], F32)
        GT = sb("GT", [128, 1, 56], F32)
        MODT = sb("MODT", [128, 2, 72, 2], F32)
        epsc = sb("epsc", [128, 1], F32)
        identb = sb("identb", [128, 128], BF16)
        WSW = 32 * 1024 + 1536
        WS = sb("WS", [128, WSW], F32)
        PS = [es.enter_context(nc.psum_tensor(f"ps{i}", [128, 512], F32)) for i in range(8)]
        PR = [Reg(psum=True) for _ in range(8)]
        st = {"bank": 0, "off": 0, "flip": 0}
        RC = Reg()

        def nb():
            i = st["bank"]
            st["bank"] = (i + 1) % 7
            return PS[i], PR[i]

        def arena_reset():
            S.barrier()
            st["off"] = 0
            st["lim"] = WSW

        def mkview(off, words, shape, dt):
            v = WS[:, off:off + words]
            if dt != F32:
                v = v.bitcast(BF16)
            if len(shape) == 2:
                v = v.rearrange("p (a b) -> p a b", a=shape[0])
            return v

        def alloc_top(shape, dt):
            n = int(np.prod(shape))
            words = n if dt == F32 else (n + 1) // 2
            st["lim"] = WSW - words
            return mkview(WSW - words, words, shape, dt)

        def alloc(shape, dt):
            n = int(np.prod(shape))
            words = n if dt == F32 else (n + 1) // 2
            off = st["off"]
            st["off"] = off + words
            assert st["off"] <= st["lim"], ("arena overflow", st["off"], st["lim"])
            v = WS[:, off:off + words]
            if dt != F32:
                v = v.bitcast(BF16)
            if len(shape) == 2:
                v = v.rearrange("p (a b) -> p a b", a=shape[0])
            elif len(shape) == 3:
                v = v.rearrange("p (a b c) -> p a b c", a=shape[0], b=shape[1])
            elif len(shape) == 4:
                v = v.rearrange("p (a b c d) -> p a b c d", a=shape[0], b=shape[1], c=shape[2])
            return v

        def mm(out, pairs, reads, writes):
            n = len(pairs)
            fns = []
            for i, (l, r) in enumerate(pairs):
                fns.append(lambda e, l=l, r=r, s0=(i == 0), s1=(i == n - 1): e.matmul(out, lhsT=l, rhs=r, start=s0, stop=s1))
            S.op("tensor", fns, reads, writes)

        def tr(out, in_, reads, writes):
            k = in_.shape[0]
            S.op("tensor", lambda e: e.transpose(out, in_, ident[0:k, 0:k]), list(reads) + [RC], writes)

        def act(out, in_, func, reads, writes, scale=1.0, bias=None, accum=None):
            kw = {}
            if bias is not None:
                kw["bias"] = bias
            if accum is not None:
                kw["accum_out"] = accum
            S.op("scalar", lambda e: e.activation(out=out, in_=in_, func=func, scale=scale, **kw), reads, writes)

        def tt(eng, out, in0, in1, op, reads, writes):
            S.op(eng, lambda e: e.tensor_tensor(out=out, in0=in0, in1=in1, op=op), reads, writes)

        def ts(eng, out, in0, s1, op0, reads, writes, s2=None, op1=None):
            if op1 is None:
                S.op(eng, lambda e: e.tensor_scalar(out=out, in0=in0, scalar1=s1, scalar2=None, op0=op0), reads, writes)
            else:
                S.op(eng, lambda e: e.tensor_scalar(out=out, in0=in0, scalar1=s1, scalar2=s2, op0=op0, op1=op1), reads, writes)

        def stt(out, in0, scalar, in1, op0, op1, reads, writes):
            S.op("vector", lambda e: e.scalar_tensor_tensor(out=out, in0=in0, scalar=scalar, in1=in1, op0=op0, op1=op1), reads, writes)

        def cp(eng, out, in_, reads, writes):
            if eng == "scalar":
                act(out, in_, AF.Copy, reads, writes)
            else:
                S.op(eng, lambda e: e.tensor_copy(out=out, in_=in_), reads, writes)

        def evac(out, in_, reads, writes):
            st["flip"] ^= 1
            cp("scalar" if st["flip"] else "vector", out, in_, reads, writes)

        def recip(ap, reg):
            S.op("vector", lambda e: e.reciprocal(out=ap, in_=ap), [reg], [reg])

        def memset(eng, ap, val, writes):
            S.op(eng, lambda e: e.memset(ap, val), (), writes)

        def dap(name, off, pat):
            return bass.AP(Hd[name], off, pat)

        def panel(dst, dreg, W2d, c0, ncols):
            S.dma("gpsimd", dst, W2d[:, c0:c0 + ncols].rearrange("(k p) c -> p k c", p=128), (), [dreg])

        for nm, t_ in [("ident", ident), ("onesb", onesb), ("prope", prope), ("masks", masks), ("dftP", dftP),
                       ("phiS", phiS), ("phiP", phiP), ("tnS", tnS), ("tnP", tnP)]:
            S.dma("sync", t_[:], A[nm], (), [RC])
        memset("vector", epsc[:], EPS, [RC])
        cp("vector", identb[:], ident[:], [RC], [RC])

        def loadT(pieces, W, out, oreg):
            R = sum(p.shape[0] for p in pieces)
            stg = alloc([W], F32)
            sreg = Reg()
            r0 = 0
            for p in pieces:
                S.dma("sync", stg[r0:r0 + p.shape[0], :], p, (), [sreg])
                r0 += p.shape[0]
            for k in range(W // 128):
                pt, pr = nb()
                tr(pt[:, 0:R], stg[0:R, k * 128:(k + 1) * 128], [sreg], [pr])
                cp("vector", out[:, k, 0:R], pt[:, 0:R], [pr], [oreg])

        arena_reset()
        RX = Reg()
        ROUT = Reg()
        xin0 = [alloc([D], F32) for _ in range(2)]
        xr0 = [Reg() for _ in range(2)]
        for b in range(2048 // 128):
            i = b % 2
            S.dma("sync", xin0[i], A["xs"][b * 128:(b + 1) * 128, :], (), [xr0[i]])
            for hf in range(2):
                pt, pr = nb()
                for c4 in range(4):
                    c = hf * 4 + c4
                    tr(pt[:, c4 * 128:(c4 + 1) * 128], xin0[i][:, c * 128:(c + 1) * 128], [xr0[i]], [pr])
                evac(xT[:, hf * 4:hf * 4 + 4, b * 128:(b + 1) * 128], pt[:].rearrange("p (a b) -> p a b", a=4), [pr], [RX])
        gp = [A[n].rearrange("l (c p) -> (l c) p", p=128) for n in ("g_ffn1", "g_mix", "g_ffn2")]
        gp.append(A["g_final"].rearrange("(c p) -> c p", p=128))
        loadT(gp, 128, GT, RC)
        cst = alloc([D], F32)
        creg = Reg()
        S.dma("sync", cst[0:2, :], A["cc"], (), [creg])
        csl = alloc([D], F32)
        act(csl[0:2, :], cst[0:2, :], AF.Silu, [creg], [creg])
        scT = alloc([8, 2], BF16)
        for k in range(8):
            pt, pr = nb()
            tr(pt[:, 0:2], csl[0:2, k * 128:(k + 1) * 128], [creg], [pr])
            cp("vector", scT[:, k, :], pt[:, 0:2], [pr], [creg])
        bmT = alloc([2, 72], F32)
        for l in range(2):
            loadT([A["b_mod"][l].rearrange("(j p) -> j p", p=128)], 128, bmT[:, l:l + 1, :], creg)
        wmp = [alloc([8, 512], BF16) for _ in range(3)]
        wmr = [Reg() for _ in range(3)]
        RM = Reg()
        for l in range(2):
            for jb in range(18):
                i = (l * 18 + jb) % 3
                panel(wmp[i], wmr[i], A["w_mod"][l], jb * 512, 512)
                pt, pr = nb()
                for jc in range(4):
                    mm(pt[:, jc * 2:jc * 2 + 2],
                       [(wmp[i][:, k, jc * 128:(jc + 1) * 128], scT[:, k, :]) for k in range(8)],
                       [wmr[i], creg], [pr])
                j0 = jb * 4
                tt("vector", MODT[:, l, j0:j0 + 4, :], pt[:, 0:8].rearrange("p (a b) -> p a b", a=4),
                   bmT[:, l, j0:j0 + 4].unsqueeze(2).broadcast_to([128, 4, 2]), ALU.add, [pr, creg], [RM])
        for l in range(2):
            for j in range(2):
                def mv(k):
                    return MODT[:, l, k * 8:(k + 1) * 8, j]
                for blk, goff, half in ((0, 0, 0.5), (1, 16, 1.0), (2, 32, 0.5)):
                    gv = GT[:, 0, goff + l * 8: goff + l * 8 + 8]
                    stt(COEF[:, l, j, blk * 3 + 0, :], mv(blk * 3 + 1), 1.0, gv, ALU.add, ALU.mult, [RM, RC], [RM])
                    cp("vector", COEF[:, l, j, blk * 3 + 1, :], mv(blk * 3 + 0), [RM], [RM])
                    ts("vector", COEF[:, l, j, blk * 3 + 2, :], mv(blk * 3 + 2), half, ALU.mult, [RM], [RM])

        def load_x(xd, T, reset=True):
            if reset:
                arena_reset()
            xin = [alloc([D], F32) for _ in range(2)]
            xr = [Reg() for _ in range(2)]
            for b in range(T // 128):
                i = b % 2
                S.dma("sync", xin[i], xd[b * 128:(b + 1) * 128, :], (), [xr[i]])
                for hf in range(2):
                    pt, pr = nb()
                    for c4 in range(4):
                        c = hf * 4 + c4
                        tr(pt[:, c4 * 128:(c4 + 1) * 128], xin[i][:, c * 128:(c + 1) * 128], [xr[i]], [pr])
                    evac(xT[:, hf * 4:hf * 4 + 4, b * 128:(b + 1) * 128], pt[:].rearrange("p (a b) -> p a b", a=4), [pr], [RX])

        def pipeline(n, phases):
            for step in range(n + len(phases) - 1):
                for p, f in enumerate(phases):
                    i = step - p
                    if 0 <= i < n:
                        f(i)

        def norm_A(t0, n, sc_):
            sq, tmp, rt = sc_["sq"], sc_["tmp"], sc_["rt"]
            act(sq[:, :, 0:n], xT[:, :, t0:t0 + n], AF.Square, [RX], [sc_["r"]])
            pt, pr = nb()
            mm(pt[:, 0:n], [(onesb[:], sq[:, c, 0:n]) for c in range(8)], [RC, sc_["r"]], [pr])
            act(rt[:, 0:n], pt[:, 0:n], AF.Sqrt, [pr, RC], [sc_["r2"]], scale=1.0 / D, bias=epsc[:, 0:1])
            recip(rt[:, 0:n], sc_["r2"])
            tt("vector", tmp[:, :, 0:n], xT[:, :, t0:t0 + n], rt[:, 0:n].unsqueeze(1).broadcast_to([128, 8, n]), ALU.mult,
               [RX, sc_["r2"]], [sc_["r3"]])

        def norm_B(n, Aco, Bco, hdst, hreg, sc_):
            tmp = sc_["tmp"]
            for c in range(8):
                if Bco is None:
                    ts("vector", hdst[:, c, 0:n], tmp[:, c, 0:n], Aco[:, c:c + 1], ALU.mult, [sc_["r3"], RM, RC], [hreg])
                else:
                    act(hdst[:, c, 0:n], tmp[:, c, 0:n], AF.Identity, [sc_["r3"], RM], [hreg], scale=Aco[:, c:c + 1], bias=Bco[:, c:c + 1])

        def norm_many(items, scr, extra=None):
            sets = scr["sets"]
            ph = [lambda i: norm_A(items[i][0], items[i][1], sets[i % 2]),
                  lambda i: norm_B(items[i][1], items[i][2], items[i][3], items[i][4], items[i][5], sets[i % 2])]
            if extra is not None:
                ph.append(extra)
            pipeline(len(items), ph)

        def norm_scratch():
            return {"i": 0, "sets": [{"sq": alloc([8, 256], BF16), "tmp": alloc([8, 256], F32), "rt": alloc([256], F32),
                                      "r": Reg(), "r2": Reg(), "r3": Reg()} for _ in range(2)]}

        def ffn(l, which, j, T):
            Wg = A["w%d_gate" % which][l]
            Wu = A["w%d_up" % which][l]
            Wd = A["w%d_down" % which][l]
            kb = 0 if which == 1 else 6
            Aco, Bco, Gco = COEF[:, l, j, kb + 0, :], COEF[:, l, j, kb + 1, :], COEF[:, l, j, kb + 2, :]
            arena_reset()
            scr = norm_scratch()
            hT = alloc([8, 1024], BF16)
            actb = alloc([NFC, 1024], BF16)
            NWB = 3
            wg = [alloc([8, 256], BF16) for _ in range(NWB)]
            wu = [alloc([8, 256], BF16) for _ in range(NWB)]
            wd = [alloc([NFC, 128], BF16) for _ in range(2)]
            sg = [alloc([512], F32) for _ in range(2)]
            wgr = [Reg() for _ in range(NWB)]
            wur = [Reg() for _ in range(NWB)]
            wdr = [Reg() for _ in range(2)]
            sgr = [Reg() for _ in range(2)]
            cnt = 0
            for u0 in range(0, T, 1024):
                hreg = Reg()
                areg = [Reg() for _ in range(NFC)]
                norm_many([(u0 + s * 256, 256, Aco, Bco, hT[:, :, s * 256:(s + 1) * 256], hreg) for s in range(4)], scr)
                for fb in range(11):
                    i = fb % NWB
                    panel(wg[i], wgr[i], Wg, fb * 256, 256)
                    panel(wu[i], wur[i], Wu, fb * 256, 256)
                    for f2 in range(2):
                        fc = fb * 2 + f2
                        for tl in range(2):
                            pg, pgr = nb()
                            pu, pur = nb()
                            rhs = [hT[:, k, tl * 512:(tl + 1) * 512] for k in range(8)]
                            mm(pg[:], [(wg[i][:, k, f2 * 128:(f2 + 1) * 128], rhs[k]) for k in range(8)], [wgr[i], hreg], [pgr])
                            mm(pu[:], [(wu[i][:, k, f2 * 128:(f2 + 1) * 128], rhs[k]) for k in range(8)], [wur[i], hreg], [pur])
                            si = cnt % 2
                            cnt += 1
                            act(sg[si], pg[:], AF.Silu, [pgr], [sgr[si]])
                            tt("vector", actb[:, fc, tl * 512:(tl + 1) * 512], sg[si], pu[:], ALU.mult, [sgr[si], pur], [areg[fc]])
                for dc in range(8):
                    i = dc % 2
                    S.dma("gpsimd", wd[i], Wd[:, dc * 128:(dc + 1) * 128].rearrange("(k p) c -> p k c", p=128), (), [wdr[i]])
                    for tl in range(2):
                        pt, pr = nb()
                        mm(pt[:], [(wd[i][:, k, :], actb[:, k, tl * 512:(tl + 1) * 512]) for k in range(NFC)],
                           [wdr[i]] + areg, [pr])
                        xs_ = xT[:, dc, u0 + tl * 512:u0 + (tl + 1) * 512]
                        stt(xs_, pt[:], Gco[:, dc:dc + 1], xs_, ALU.mult, ALU.add, [pr, RM, RX], [RX])

        def inproj_fm(wpanel_ap, wreg, hT, hreg, T, dst_fn, dreg):
            for tl in range(T // 512):
                pt, pr = nb()
                mm(pt[:], [(wpanel_ap[:, k, :], hT[:, k, tl * 512:(tl + 1) * 512]) for k in range(8)], [wreg, hreg], [pr])
                d = dst_fn(tl)
                src = pt[:]
                if len(d.shape) == 3:
                    src = src.rearrange("p (a b) -> p a b", a=d.shape[1])
                evac(d, src, [pr], [dreg])

        def out_accum(l, j, srcT, sreg, row0, nch, T):
            Gco = COEF[:, l, j, 5, :]
            wo = alloc([nch, D], BF16)
            wor = Reg()
            S.dma("gpsimd", wo, A["w_out"][l][row0:row0 + nch * 128, :].rearrange("(k p) c -> p k c", p=128), (), [wor])
            for dc in range(8):
                for tl in range(T // 512):
                    pt, pr = nb()
                    mm(pt[:], [(wo[:, k, dc * 128:(dc + 1) * 128], srcT[:, k, tl * 512:(tl + 1) * 512]) for k in range(nch)],
                       [wor, sreg], [pr])
                    xs_ = xT[:, dc, tl * 512:(tl + 1) * 512]
                    stt(xs_, pt[:], Gco[:, dc:dc + 1], xs_, ALU.mult, ALU.add, [pr, RM, RX], [RX])

        def mix_attention(l, j, grp, hT, hreg):
            T, NSEQ, L = grp["T"], grp["NSEQ"], grp["L"]
            sample = grp["sample"]
            Win = A["w_in"][l]
            qT = alloc([4, T], BF16)
            kT = alloc([4, T], BF16)
            nblk = T // 128
            vo = alloc([nblk, 2, 128], BF16)
            attT = alloc([4, T], BF16)
            Eall = [alloc([512], BF16) for _ in range(10)]
            Erall = [Reg() for _ in range(10)]
            dnall = [alloc([512], F32) for _ in range(2)]
            dnrall = [Reg() for _ in range(2)]
            ESb = alloc([8], F32)
            sinkL = alloc([128], BF16)
            esfull = alloc([8, 128], BF16)
            pmark = st["off"]
            wq = [alloc([8, 256], BF16) for _ in range(2)]
            wkv = alloc([8, 256], BF16)
            wkb = alloc([8, 128], BF16)
            wr = [Reg() for _ in range(4)]
            qr, kr, vr, ar, esr = Reg(), Reg(), Reg(), Reg(), Reg()
            S.dma("sync", ESb, dap("attn_sink", l * 8, [[0, 128], [1, 8]]), (), [esr])
            act(ESb, ESb, AF.Exp, [esr], [esr])
            memset("vector", sinkL, 0.0, [esr])
            memset("vector", sinkL[0:1, 64:128], 1.0, [esr])
            memset("vector", esfull, 0.0, [esr])
            cp("vector", esfull[0:1, :, :], ESb[0:1, :].unsqueeze(2).broadcast_to([1, 8, 128]), [esr], [esr])
            memset("vector", vo[:, :, :, 64:128], 1.0, [vr])
            for pi in range(2):
                panel(wq[pi], wr[pi], Win, pi * 256, 256)
            panel(wkv, wr[2], Win, 512, 256)
            S.dma("gpsimd", wkb[:, :, 0:64], Win[:, 576:640].rearrange("(k p) c -> p k c", p=128), (), [wr[3]])
            S.dma("gpsimd", wkb[:, :, 64:128], Win[:, 512:576].rearrange("(k p) c -> p k c", p=128), (), [wr[3]])
            for c in range(4):
                inproj_fm(wq[c // 2][:, :, (c % 2) * 128:(c % 2 + 1) * 128], wr[c // 2], hT, hreg, T,
                          lambda tl, c=c: qT[:, c, tl * 512:(tl + 1) * 512], qr)
            memset("gpsimd", kT, 0.0, [kr])
            for wpan, wreg_, vtop, vbot in ((wkv[:, :, 0:128], wr[2], 0, 3), (wkb[:, :, :], wr[3], 2, 1)):
                for tl in range(T // 512):
                    pt, pr = nb()
                    mm(pt[:], [(wpan[:, k, :], hT[:, k, tl * 512:(tl + 1) * 512]) for k in range(8)], [wreg_, hreg], [pr])
                    cp("scalar", kT[0:64, vtop, tl * 512:(tl + 1) * 512], pt[0:64, :], [pr], [kr])
                    cp("scalar", kT[64:128, vbot, tl * 512:(tl + 1) * 512], pt[64:128, :], [pr], [kr])
            chk("att_inproj")
            if not sample:
                kvf = [alloc([256], F32) for _ in range(2)]
                kvr = [Reg() for _ in range(2)]
            for b in range(nblk):
                pt, pr = nb()
                if sample:
                    mm(pt[:, 0:128], [(hT[:, k, b * 128:(b + 1) * 128], wkv[:, k, 128:256]) for k in range(8)], [hreg, wr[2]], [pr])
                    evac(vo[:, b, :, 0:64], pt[:, 0:128].rearrange("p (g d) -> p g d", g=2), [pr], [vr])
                else:
                    mm(pt[:, 0:256], [(hT[:, k, b * 128:(b + 1) * 128], wkv[:, k, :]) for k in range(8)], [hreg, wr[2]], [pr])
                    i = b % 2
                    cp("scalar", kvf[i], pt[:, 0:256], [pr], [kvr[i]])
                    cp("vector", vo[:, b, :, 0:64], kvf[i][:, 128:256].rearrange("p (g d) -> p g d", g=2), [kvr[i]], [vr])
                    s_, bl = b // 2, b % 2
                    import os
                    if not os.environ.get("NO_KVDMA"):
                        S.dma("sync", nk[s_, l, bl * 128:(bl + 1) * 128, :], kvf[i][:, 0:128], [kvr[i]], ())
                        S.dma("sync", nv[s_, l, bl * 128:(bl + 1) * 128, :], kvf[i][:, 128:256], [kvr[i]], ())
            chk("att_kvtok")
            if sample:
                S.barrier()
                st["off"] = pmark
                cs = alloc([2, 2048], BF16)
                csr = Reg()
                S.dma("sync", cs, A["ropecs"], (), [csr])
                t12 = [(alloc([512], F32), alloc([512], F32), Reg(), Reg()) for _ in range(2)]
                targets = [(qT[:, c, :], Reg()) for c in range(4)] + [(kT[:, v_, :], Reg()) for v_ in range(4)]
                items_ = [(tgt, treg, tl) for tgt, treg in targets for tl in range(4)]

                def rope_p0(i_):
                    tgt, treg, tl = items_[i_]
                    t1, t2, t1r, t2r = t12[i_ % 2]
                    sl = slice(tl * 512, (tl + 1) * 512)
                    pt, pr = nb()
                    mm(pt[:], [(prope[:], tgt[:, sl])], [RC, treg], [pr])
                    tt("vector", t1, pt[:], cs[:, 1, sl], ALU.mult, [pr, csr], [t1r])
                    tt("gpsimd", t2, tgt[:, sl], cs[:, 0, sl], ALU.mult, [treg, csr], [t2r])

                def rope_p1(i_):
                    tgt, treg, tl = items_[i_]
                    t1, t2, t1r, t2r = t12[i_ % 2]
                    sl = slice(tl * 512, (tl + 1) * 512)
                    tt("vector", tgt[:, sl], t1, t2, ALU.add, [t1r, t2r], [treg, qr if i_ < 16 else kr])
                pipeline(len(items_), [rope_p0, rope_p1])
                chk("att_rope")
                kcT = alloc([4, 256], BF16)
                vco = alloc([2, 2, 128], BF16)
                kcr, vcr = Reg(), Reg()
                memset("vector", vco[:, :, :, 64:128], 1.0, [vcr])
                memset("vector", kcT, 0.0, [kcr])
                ckf = alloc([2, 2, 128], F32)
                ckr = Reg()
                for blk in range(2):
                    S.dma("sync", ckf[:, 0, blk, :], A["ck"][l, blk * 128:(blk + 1) * 128, :], (), [ckr])
                    S.dma("sync", ckf[:, 1, blk, 0:64], A["ck"][l, blk * 128:(blk + 1) * 128, 64:128], (), [ckr])
                    S.dma("sync", ckf[:, 1, blk, 64:128], A["ck"][l, blk * 128:(blk + 1) * 128, 0:64], (), [ckr])
                    S.dma("gpsimd", vco[:, blk, :, 0:64], A["cv"][l, blk * 128:(blk + 1) * 128, :].rearrange("p (g d) -> p g d", g=2), (), [vcr])
                for var in range(2):
                    for blk in range(2):
                        pt, pr = nb()
                        tr(pt[:, 0:128], ckf[:, var, blk, :], [ckr], [pr])
                        vtop, vbot = (0, 3) if var == 0 else (2, 1)
                        cp("vector", kcT[0:64, vtop, blk * 128:(blk + 1) * 128], pt[0:64, 0:128], [pr], [kcr])
                        cp("vector", kcT[64:128, vbot, blk * 128:(blk + 1) * 128], pt[64:128, 0:128], [pr], [kcr])
            chk("att_cache")
            nqb = T // 128
            itst = {}

            def att_p0(it):
                qb, g = it // 2, it % 2
                par_ = it % 2
                E = Eall[par_ * 5:par_ * 5 + 5]
                Er = Erall[par_ * 5:par_ * 5 + 5]
                keys = []
                if sample:
                    if qb >= 1:
                        keys.append((kT, kr, (qb - 1) * 128, vo[:, qb - 1, g, :], vr, 0))
                    if qb + 1 < nqb:
                        keys.append((kT, kr, (qb + 1) * 128, vo[:, qb + 1, g, :], vr, 1))
                    keys.append((kT, kr, qb * 128, vo[:, qb, g, :], vr, None))
                    for blk in range(2):
                        keys.append((kcT, kcr, blk * 128, vco[:, blk, g, :], vcr, None))
                else:
                    s_ = qb // 2
                    for blk in range(2):
                        kb_ = s_ * 2 + blk
                        keys.append((kT, kr, kb_ * 128, vo[:, kb_, g, :], vr, None))
                itst[it] = keys
                for ki, (ksrc, ksr, k0, vap, vreg, mk) in enumerate(keys):
                    pt, pr = nb()
                    for r in range(4):
                        h = g * 4 + r
                        var = g * 2 + (h % 2)
                        mm(pt[:, r * 128:(r + 1) * 128],
                           [(ksrc[:, var, k0:k0 + 128], qT[:, h // 2, qb * 128:(qb + 1) * 128])],
                           [ksr, qr], [pr])
                    act(E[ki], pt[:], AF.Exp, [pr], [Er[ki]], scale=0.125)
                    if mk is not None:
                        ev = E[ki].rearrange("p (r q) -> p r q", r=4)
                        tt("gpsimd", ev, ev, masks[:, mk, :].unsqueeze(1).broadcast_to([128, 4, 128]), ALU.mult,
                           [Er[ki], RC], [Er[ki]])

            def att_p1(it):
                qb, g = it // 2, it % 2
                par_ = it % 2
                E = Eall[par_ * 5:par_ * 5 + 5]
                Er = Erall[par_ * 5:par_ * 5 + 5]
                dn, dnr = dnall[par_], dnrall[par_]
                keys = itst.pop(it)
                po, por = nb()
                mm(po[:], [(keys[ki][3], E[ki]) for ki in range(len(keys))] +
                   [(sinkL, esfull[:, g * 4:(g + 1) * 4, :].rearrange("p a b -> p (a b)"))],
                   [Er[ki] for ki in range(len(keys))] + [keys[0][4], keys[-1][4], esr], [por])
                act(dn[64:128, :], po[64:128, :], AF.Ln, [por], [dnr])
                act(dn[64:128, :], dn[64:128, :], AF.Exp, [dnr], [dnr], scale=-1.0)
                for r in range(4):
                    h = g * 4 + r
                    base = (h % 2) * 64
                    tt("vector", attT[base:base + 64, h // 2, qb * 128:(qb + 1) * 128], po[0:64, r * 128:(r + 1) * 128],
                       dn[64:128, r * 128:(r + 1) * 128], ALU.mult, [por, dnr], [ar])
            pipeline(nqb * 2, [att_p0, att_p1])
            chk("att_main")
            out_accum(l, j, attT, ar, 0, 4, T)

        def make_diag(Dm, taps, ntap, treg, dreg):
            tt("vector", Dm, identb[:].unsqueeze(1).broadcast_to([128, ntap, 128]),
               taps.unsqueeze(2).broadcast_to([128, ntap, 128]), ALU.mult, [RC, treg], [dreg])

        def mix_conv(l, j, grp, hT, hreg):
            T, NSEQ, L = grp["T"], grp["NSEQ"], grp["L"]
            Win = A["w_in"][l]
            cw = alloc([2, 34], F32)
            cwr = Reg()
            loadT([A["conv_dw"][l], A["conv_dw_b"][l:l + 1, :], A["conv_ln_g"][l:l + 1, :], A["conv_ln_b"][l:l + 1, :]], 256, cw, cwr)
            Dm = alloc([2, 31, 128], BF16)
            dmr = Reg()
            for c in range(2):
                make_diag(Dm[:, c], cw[:, c, 0:31], 31, cwr, dmr)
            pw = alloc([2, 256], BF16)
            pwr = Reg()
            S.dma("gpsimd", pw, A["conv_pw"][l].rearrange("(k p) c -> p k c", p=128), (), [pwr])
            wp = [alloc([8, 256], BF16) for _ in range(2)]
            wpr = [Reg() for _ in range(2)]
            for pi in range(2):
                panel(wp[pi], wpr[pi], Win, 768 + pi * 256, 256)
            zc = alloc([4, T], BF16)
            zr = Reg()
            for c in range(4):
                inproj_fm(wp[c // 2][:, :, (c % 2) * 128:(c % 2 + 1) * 128], wpr[c // 2], hT, hreg, T,
                          lambda tl, c=c: zc[:, c, tl * 512:(tl + 1) * 512], zr)
            LP = L + 30
            ypad = alloc([2, NSEQ, LP], BF16)
            yr = Reg()
            memset("vector", ypad, 0.0, [yr])
            sgm = alloc([T], BF16)
            sr = Reg()
            for c in range(2):
                act(sgm, zc[:, 2 + c, :], AF.Sigmoid, [zr], [sr])
                tt("vector", ypad[:, c, :, 15:15 + L], zc[:, c, :].rearrange("p (s t) -> p s t", s=NSEQ),
                   sgm.rearrange("p (s t) -> p s t", s=NSEQ), ALU.mult, [zr, sr], [yr])
            n = min(L, 512)
            ycs = [(alloc([2, n], F32), Reg()) for _ in range(3)]
            ybs = [(alloc([2, n], BF16), alloc([2, n], BF16), Reg()) for _ in range(2)]
            mvs = [(alloc([n], F32), alloc([n], F32), Reg()) for _ in range(2)]
            sTs = [(alloc([2, n], BF16), Reg()) for _ in range(2)]
            cvT = alloc([2, T], BF16)
            cvr = Reg()
            tiles_ = [(s_, t0) for s_ in range(NSEQ) for t0 in range(0, L, n)]

            def conv_p0(ti):
                s_, t0 = tiles_[ti]
                yc, r1 = ycs[ti % 3]
                ycb, ysq, r2 = ybs[ti % 2]
                for c in range(2):
                    pt, pr = nb()
                    mm(pt[:, 0:n], [(Dm[:, c, k, :], ypad[:, c, s_, t0 + k:t0 + k + n]) for k in range(31)], [dmr, yr], [pr])
                    act(yc[:, c, :], pt[:, 0:n], AF.Identity, [pr, cwr], [r1], bias=cw[:, c, 31:32])
                cp("vector", ycb, yc, [r1], [r2])
                act(ysq, yc, AF.Square, [r1], [r2])

            def conv_p1(ti):
                ycb, ysq, r2 = ybs[ti % 2]
                mean, var, r3 = mvs[ti % 2]
                p1, p1r = nb()
                p2, p2r = nb()
                mm(p1[:, 0:n], [(onesb[:], ycb[:, c, :]) for c in range(2)], [RC, r2], [p1r])
                mm(p2[:, 0:n], [(onesb[:], ysq[:, c, :]) for c in range(2)], [RC, r2], [p2r])
                act(mean, p1[:, 0:n], AF.Copy, [p1r], [r3], scale=1.0 / 256)
                tt("vector", var, mean, mean, ALU.mult, [r3], [r3])
                stt(var, p2[:, 0:n], 1.0 / 256, var, ALU.mult, ALU.subtract, [p2r, r3], [r3])
                act(var, var, AF.Sqrt, [r3, RC], [r3], bias=epsc[:, 0:1])
                recip(var, r3)

            def conv_p2(ti):
                s_, t0 = tiles_[ti]
                yc, r1 = ycs[ti % 3]
                mean, var, r3 = mvs[ti % 2]
                sT, r4 = sTs[ti % 2]
                for c in range(2):
                    tt("vector", yc[:, c, :], yc[:, c, :], mean, ALU.subtract, [r1, r3], [r1])
                    tt("vector", yc[:, c, :], yc[:, c, :], var, ALU.mult, [r1, r3], [r1])
                    act(sT[:, c, :], yc[:, c, :], AF.Silu, [r1, cwr], [r4], scale=cw[:, c, 32:33], bias=cw[:, c, 33:34])
                for co in range(2):
                    pt, pr = nb()
                    mm(pt[:, 0:n], [(pw[:, ci, co * 128:(co + 1) * 128], sT[:, ci, :]) for ci in range(2)], [pwr, r4], [pr])
                    evac(cvT[:, co, s_ * L + t0:s_ * L + t0 + n], pt[:, 0:n], [pr], [cvr])
            pipeline(len(tiles_), [conv_p0, conv_p1, conv_p2])
            out_accum(l, j, cvT, cvr, 512, 2, T)

        def mix_hyena(l, j, grp, hT, hreg):
            T, NSEQ, L = grp["T"], grp["NSEQ"], grp["L"]
            sample = grp["sample"]
            Win = A["w_in"][l]
            NTB = L // 128
            zfn, tn, phi = ("zfS", tnS, phiS) if sample else ("zfP", tnP, phiP)
            NR = min(L, 512)
            hw_ = alloc([6, 4], F32)
            hwr = Reg()
            loadT([A["hy_short_w"][l], A["hy_short_b"][l:l + 1, :]], 768, hw_, hwr)
            hb_ = alloc([2, 2], F32)
            loadT([A["hy_bias"][l]], 256, hb_, hwr)
            Dm = alloc([6, 3, 128], BF16)
            dmr = Reg()
            for c in range(6):
                make_diag(Dm[:, c], hw_[:, c, 0:3], 3, hwr, dmr)
            fw = alloc([64 + 64 + 1024 + 8], F32)
            fr = Reg()
            S.dma("sync", fw[0:33, 0:64], A["hy_w1"][l], (), [fr])
            S.dma("sync", fw[0:64, 64:128], A["hy_w2"][l], (), [fr])
            S.dma("sync", fw[0:64, 128:1152], A["hy_w3"][l], (), [fr])
            for i_, nm in enumerate(["hy_f1", "hy_b1", "hy_f2", "hy_b2"]):
                S.dma("sync", fw[0:64, 1152 + i_:1153 + i_], dap(nm, l * 64, [[1, 64], [1, 1]]), (), [fr])
            sc = alloc([8], F32)
            scr_ = Reg()
            for m_ in range(2):
                f_ = fw[0:64, 1152 + 2 * m_:1153 + 2 * m_]
                b_ = fw[0:64, 1153 + 2 * m_:1154 + 2 * m_]
                ts("vector", sc[0:64, 4 * m_ + 0:4 * m_ + 1], f_, 0.5, ALU.mult, [fr], [scr_])
                ts("vector", sc[0:64, 4 * m_ + 2:4 * m_ + 3], f_, 0.25, ALU.mult, [fr], [scr_])
                tt("vector", sc[0:64, 4 * m_ + 1:4 * m_ + 2], sc[0:64, 4 * m_ + 0:4 * m_ + 1], b_, ALU.mult, [fr, scr_], [scr_])
                tt("vector", sc[0:64, 4 * m_ + 3:4 * m_ + 4], sc[0:64, 4 * m_ + 2:4 * m_ + 3], b_, ALU.mult, [fr, scr_], [scr_])
            dec = alloc([1024], F32)
            dcr = Reg()
            S.dma("sync", dec, dap("hy_log_decay", l * 1024, [[0, 128], [1, 1024]]), (), [dcr])
            act(dec, dec, AF.Exp, [dcr], [dcr])
            vvx = alloc([6, T], BF16)
            vxr = Reg()
            mark = st["off"]
            upad = alloc([6, NSEQ, L + 2], BF16)
            ur = Reg()
            memset("vector", upad, 0.0, [ur])
            wp = [alloc([8, 256], BF16) for _ in range(3)]
            wpr = [Reg() for _ in range(3)]
            for pi in range(3):
                panel(wp[pi], wpr[pi], Win, 1280 + pi * 256, 256)
            for c in range(6):
                inproj_fm(wp[c // 2][:, :, (c % 2) * 128:(c % 2 + 1) * 128], wpr[c // 2], hT, hreg, T,
                          (lambda tl, c=c: upad[:, c, tl * (512 // L):(tl + 1) * (512 // L), 1:1 + L]) if L < 512 else
                          (lambda tl, c=c: upad[:, c, 0, 1 + tl * 512:1 + (tl + 1) * 512]), ur)
            n = min(L, 512)
            for c in range(6):
                for s_ in range(NSEQ):
                    for t0 in range(0, L, n):
                        pt, pr = nb()
                        mm(pt[:, 0:n], [(Dm[:, c, k, :], upad[:, c, s_, t0 + k:t0 + k + n]) for k in range(3)], [dmr, ur], [pr])
                        act(vvx[:, c, s_ * L + t0:s_ * L + t0 + n], pt[:, 0:n], AF.Identity, [pr, hwr], [vxr], bias=hw_[:, c, 3:4])
            S.barrier()
            st["off"] = mark
            st["lim"] = WSW - (4096 if sample else 0)

            def alloc_fixed(i_):
                return mkview(WSW - 4096 + i_ * 1024, 1024, [4, 512], BF16)
            y1T = alloc([2, T], BF16)
            y1r = Reg()
            mark2 = st["off"]
            for o in range(2):
                S.barrier()
                st["off"] = mark2
                Hs = alloc([NTB, 2, 256], BF16)
                Hr_ = Reg()
                nrm = alloc([256], F32)
                nr = Reg()
                markB = st["off"]
                hf_ = alloc([NTB, 512], BF16)
                hfr = Reg()
                markA = st["off"]
                assert markA <= WSW - 4096
                st["lim"] = WSW
                zfs = [alloc([512], F32) for _ in range(2)]
                zrs = [Reg() for _ in range(2)]
                h2 = alloc([2 * L], F32)
                hr2 = Reg()
                h1s = [alloc([512], F32) for _ in range(2)]
                hr1s = [Reg() for _ in range(2)]
                sa = [(alloc([512], F32), alloc([512], F32), Reg(), Reg()) for _ in range(2)]
                sb_ = [(alloc([512], F32), alloc([512], F32), Reg(), Reg()) for _ in range(2)]
                t0s = list(range(0, 2 * L, 512))

                def sin_pair(pt, pr, m_, s2, s4, s2r, s4r):
                    act(s2[0:64, :], pt[0:64, :], AF.Sin, [pr, scr_], [s2r], scale=sc[0:64, 4 * m_:4 * m_ + 1], bias=sc[0:64, 4 * m_ + 1:4 * m_ + 2])
                    act(s4[0:64, :], pt[0:64, :], AF.Sin, [pr, scr_], [s4r], scale=sc[0:64, 4 * m_ + 2:4 * m_ + 3], bias=sc[0:64, 4 * m_ + 3:4 * m_ + 4])

                def sin_fin(dst_ap, dreg_, s2, s4, s2r, s4r):
                    tt("vector", s4[0:64, :], s4[0:64, :], s4[0:64, :], ALU.mult, [s4r], [s4r])
                    ts("vector", s4[0:64, :], s4[0:64, :], -4.0, ALU.mult, [s4r], [s4r], s2=2.0, op1=ALU.add)
                    tt("vector", dst_ap, s2[0:64, :], s4[0:64, :], ALU.mult, [s2r, s4r], [dreg_])

                def mlp_p0(ti_):
                    zf, zr_ = zfs[ti_ % 2], zrs[ti_ % 2]
                    S.dma("sync", zf[0:33, :], A[zfn][:, t0s[ti_]:t0s[ti_] + 512], (), [zr_])
                    pt, pr = nb()
                    mm(pt[0:64, :], [(fw[0:33, 0:64], zf[0:33, :])], [fr, zr_], [pr])
                    sin_pair(pt, pr, 0, *sa[ti_ % 2])

                def mlp_p1(ti_):
                    h1, hr1 = h1s[ti_ % 2], hr1s[ti_ % 2]
                    sin_fin(h1[0:64, :], hr1, *sa[ti_ % 2])
                    pt, pr = nb()
                    mm(pt[0:64, :], [(fw[0:64, 64:128], h1[0:64, :])], [fr, hr1], [pr])
                    sin_pair(pt, pr, 1, *sb_[ti_ % 2])

                def mlp_p2(ti_):
                    sin_fin(h2[0:64, t0s[ti_]:t0s[ti_] + 512], hr2, *sb_[ti_ % 2])
                pipeline(len(t0s), [mlp_p0, mlp_p1, mlp_p2])
                ets = [(alloc([512], F32), alloc([512], F32), alloc([512], BF16), Reg(), Reg(), Reg()) for _ in range(2)]
                def hf_p0(tb, o=o):
                    et, hwt, sqb, etr, hwr2, sqr = ets[tb % 2]
                    pt, pr = nb()
                    for d_ in range(2):
                        cols = slice(128 + o * 512 + d_ * 256, 128 + o * 512 + (d_ + 1) * 256)
                        tbb = tb + d_ * NTB
                        mm(pt[:, d_ * 256:(d_ + 1) * 256], [(h2[0:64, tbb * 128:(tbb + 1) * 128], fw[0:64, cols])], [hr2, fr], [pr])
                        act(et[:, d_ * 256:(d_ + 1) * 256], dec[:, o * 512 + d_ * 256:o * 512 + (d_ + 1) * 256], AF.Exp, [dcr, RC], [etr],
                            scale=tn[:, tbb:tbb + 1])
                    tt("vector", hwt, pt[:], et, ALU.mult, [pr, etr], [hwr2])

                def hf_p1(tb, hf_=hf_, hfr=hfr):
                    et, hwt, sqb, etr, hwr2, sqr = ets[tb % 2]
                    cp("vector", hf_[:, tb, :], hwt, [hwr2], [hfr])
                    act(sqb, hwt, AF.Square, [hwr2], [sqr])
                    S.op("tensor", lambda e, tb=tb, sqb=sqb, NTB=NTB: e.matmul(PS[7][:], lhsT=onesb[:], rhs=sqb, start=(tb == 0), stop=(tb == NTB - 1)),
                         [RC, sqr], [PR[7]])
                pipeline(NTB, [hf_p0, hf_p1])
                memset("vector", hf_[0:1, 0, 256:512], 0.0, [hfr])
                tt("vector", nrm, PS[7][:, 0:256], epsc[:, 0:1].broadcast_to([128, 256]), ALU.add, [PR[7], RC], [nr])
                tt("vector", nrm, nrm, PS[7][:, 256:512], ALU.add, [PR[7], nr], [nr])
                act(nrm, nrm, AF.Sqrt, [nr], [nr])
                recip(nrm, nr)
                S.barrier()
                st["off"] = markA
                st["lim"] = WSW - (4096 if sample else 0)
                if sample:
                    dbuf = [alloc_fixed(i_) for i_ in range(4)]
                    dbr = [Reg() for _ in range(4)]
                    dst_ = {"i": 0}

                    def dft_tiles(mat, fr_, tg):
                        i = dst_["i"]
                        dst_["i"] = (i + 1) % 4
                        S.dma("sync", dbuf[i], A["dftS"][mat, tg * 512:(tg + 1) * 512, fr_ * 512:(fr_ + 1) * 512].rearrange("(k p) c -> p k c", p=128),
                              (), [dbr[i]])
                        return [dbuf[i][:, k, :] for k in range(4)], dbr[i]
                    NTG, TPG = 4, 4
                else:
                    def dft_tiles(mat, fr_, tg):
                        return [dftP[:, mat, k, :] for k in range(2)], RC
                    NTG, TPG = 1, 2
                NFR = L // NR
                FPR = NR // 128
                acss = [(alloc([FPR, 512], F32), Reg()) for _ in range(2)]
                tABs = [(alloc([512], F32), alloc([512], F32), Reg(), Reg()) for _ in range(2)]
                for fr_ in range(NFR):
                    acs, acr = acss[fr_ % 2]
                    for mat in range(2):
                        banks = [nb() for _ in range(FPR)]
                        for tg in range(NTG):
                            tiles, treg = dft_tiles(mat, fr_, tg)
                            for k in range(TPG):
                                tb = tg * TPG + k
                                for fc in range(FPR):
                                    S.op("tensor", lambda e, fc=fc, k=k, tb=tb, tiles=tiles, banks=banks, hf_=hf_, NTB=NTB: e.matmul(
                                        banks[fc][0][:], lhsT=tiles[k][:, fc * 128:(fc + 1) * 128], rhs=hf_[:, tb, :],
                                        start=(tb == 0), stop=(tb == NTB - 1)), [treg, hfr], [banks[fc][1]])
                        if mat == 0:
                            for fc in range(FPR):
                                evac(acs[:, fc, :], banks[fc][0][:], [banks[fc][1]], [acr])
                        else:
                            for fc in range(FPR):
                                fg = fr_ * FPR + fc
                                tA, tB, tAr, tBr = tABs[fc % 2]
                                Pc, Qc = acs[:, fc, 0:256], acs[:, fc, 256:512]
                                Ps_, Qs = banks[fc][0][:, 0:256], banks[fc][0][:, 256:512]
                                sg_, cph, sph = phi[:, fg, 2:3], phi[:, fg, 0:1], phi[:, fg, 1:2]
                                stt(tA[:, 0:256], Qs, sg_, Pc, ALU.mult, ALU.add, [banks[fc][1], acr, RC], [tAr])
                                stt(tA[:, 256:512], Qc, sg_, Ps_, ALU.mult, ALU.subtract, [banks[fc][1], acr, RC], [tAr])
                                act(tB[:, 0:256], tA[:, 256:512], AF.Identity, [tAr, RC], [tBr], scale=sph)
                                stt(tB[:, 0:256], tA[:, 0:256], cph, tB[:, 0:256], ALU.mult, ALU.subtract, [tAr, tBr, RC], [tBr])
                                act(tB[:, 256:512], tA[:, 256:512], AF.Identity, [tAr, RC], [tBr], scale=cph)
                                stt(tB[:, 256:512], tA[:, 0:256], sph, tB[:, 256:512], ALU.mult, ALU.add, [tAr, tBr, RC], [tBr])
                                tt("vector", Hs[:, fg, :, :], tB.rearrange("p (a c) -> p a c", a=2),
                                   nrm.unsqueeze(1).broadcast_to([128, 2, 256]), ALU.mult, [tBr, nr], [Hr_])
                S.barrier()
                st["off"] = markB
                tABs = [(alloc([512], F32), alloc([512], F32), Reg(), Reg()) for _ in range(2)]
                ncp = 2 if NSEQ > 1 else 1
                vtoks = [(alloc([NTB, 256], BF16), Reg()) for _ in range(ncp)]
                Yss = [(alloc([NTB, 2, 256], BF16), Reg()) for _ in range(ncp)]
                ucss = [(alloc([FPR, 256], F32), Reg()) for _ in range(2)]
                yscs = [(alloc([512], F32), Reg()) for _ in range(2)]
                ysi = {"i": 0}

                def next_ysc():
                    ysi["i"] += 1
                    return yscs[ysi["i"] % 2]
                def c_p0(s_):
                    vtok, vtr = vtoks[s_ % ncp]
                    for tb in range(NTB):
                        pt, pr = nb()
                        ysc, ysr = next_ysc()
                        for c in range(2):
                            if o == 0:
                                cp("gpsimd", ysc[:, c * 128:(c + 1) * 128], vvx[:, c, s_ * L + tb * 128:s_ * L + (tb + 1) * 128], [vxr], [ysr])
                            else:
                                cp("gpsimd", ysc[:, c * 128:(c + 1) * 128], y1T[:, c, s_ * L + tb * 128:s_ * L + (tb + 1) * 128], [y1r], [ysr])
                        for c in range(2):
                            tr(pt[:, c * 128:(c + 1) * 128], ysc[:, c * 128:(c + 1) * 128], [ysr], [pr])
                        evac(vtok[:, tb, :], pt[:, 0:256], [pr], [vtr])

                def c_p1(s_):
                    vtok, vtr = vtoks[s_ % ncp]
                    Ys, Yr_ = Yss[s_ % ncp]
                    for fr_ in range(NFR):
                        ucs, ucr = ucss[(s_ * NFR + fr_) % 2]
                        for mat in range(2):
                            banks = [nb() for _ in range(FPR)]
                            for tg in range(NTG):
                                tiles, treg = dft_tiles(mat, fr_, tg)
                                for k in range(TPG):
                                    tb = tg * TPG + k
                                    for fc in range(FPR):
                                        S.op("tensor", lambda e, fc=fc, k=k, tb=tb, tiles=tiles, banks=banks, vtok=vtok, NTB=NTB: e.matmul(
                                            banks[fc][0][:, 0:256], lhsT=tiles[k][:, fc * 128:(fc + 1) * 128], rhs=vtok[:, tb, :],
                                            start=(tb == 0), stop=(tb == NTB - 1)), [treg, vtr], [banks[fc][1]])
                            if mat == 0:
                                for fc in range(FPR):
                                    evac(ucs[:, fc, :], banks[fc][0][:, 0:256], [banks[fc][1]], [ucr])
                            else:
                                sc2 = 2.0 / (2 * L)
                                for fc in range(FPR):
                                    fg = fr_ * FPR + fc
                                    tA, tB, tAr, tBr = tABs[fc % 2]
                                    Uc, Us = ucs[:, fc, :], banks[fc][0][:, 0:256]
                                    Hre, Him = Hs[:, fg, 0, :], Hs[:, fg, 1, :]
                                    tt("vector", tA[:, 0:256], Us, Him, ALU.mult, [banks[fc][1], Hr_], [tAr])
                                    tt("vector", tA[:, 256:512], Us, Hre, ALU.mult, [banks[fc][1], Hr_], [tAr])
                                    tt("gpsimd", tB[:, 0:256], Uc, Hre, ALU.mult, [ucr, Hr_], [tBr])
                                    tt("gpsimd", tB[:, 256:512], Uc, Him, ALU.mult, [ucr, Hr_], [tBr])
                                    tt("vector", tA[:, 0:256], tA[:, 0:256], tB[:, 0:256], ALU.add, [tAr, tBr], [tAr])
                                    tt("vector", tA[:, 256:512], tA[:, 256:512], tB[:, 256:512], ALU.subtract, [tAr, tBr], [tAr])
                                    act(Ys[:, fg, :, :], tA.rearrange("p (a c) -> p a c", a=2), AF.Copy, [tAr], [Yr_], scale=sc2)

                def c_p2(s_):
                    Ys, Yr_ = Yss[s_ % ncp]
                    for tr_ in range(NFR):
                        banks = [nb() for _ in range(2)]
                        first = True
                        for mat in range(2):
                            for fgp in range(NTG):
                                tiles, treg = dft_tiles(mat, tr_, fgp)
                                for k in range(TPG):
                                    fg = fgp * TPG + k
                                    last = (mat == 1 and fg == NTB - 1)
                                    for c in range(2):
                                        S.op("tensor", lambda e, c=c, k=k, fg=fg, mat=mat, tiles=tiles, banks=banks, first=first, last=last, Ys=Ys, NR=NR: e.matmul(
                                            banks[c][0][:, 0:NR], lhsT=Ys[:, fg, mat, c * 128:(c + 1) * 128], rhs=tiles[k],
                                            start=first, stop=last), [treg, Yr_], [banks[c][1]])
                                    first = False
                        for c in range(2):
                            tsl = slice(s_ * L + tr_ * NR, s_ * L + (tr_ + 1) * NR)
                            ysc, ysr = next_ysc()
                            if o == 0:
                                stt(ysc[:, 0:NR], vvx[:, c, tsl], hb_[:, c, 0:1], banks[c][0][:, 0:NR], ALU.mult, ALU.add, [vxr, hwr, banks[c][1], ysr], [ysr])
                                tt("vector", y1T[:, c, tsl], ysc[:, 0:NR], vvx[:, 2 + c, tsl], ALU.mult, [ysr, vxr], [y1r])
                            else:
                                stt(ysc[:, 0:NR], y1T[:, c, tsl], hb_[:, c, 1:2], banks[c][0][:, 0:NR], ALU.mult, ALU.add, [y1r, hwr, banks[c][1], ysr], [ysr])
                                tt("vector", vvx[:, c, tsl], ysc[:, 0:NR], vvx[:, 4 + c, tsl], ALU.mult, [ysr, vxr], [vxr])
                pipeline(NSEQ, [c_p0, c_p1, c_p2])
            S.barrier()
            st["lim"] = WSW
            out_accum(l, j, vvx, vxr, 768, 2, T)

        def mix(l, j, grp):
            T = grp["T"]
            arena_reset()
            hT = alloc_top([8, T], BF16)
            hreg = Reg()
            base = st["off"]
            scr = norm_scratch()
            norm_many([(t0, 256, COEF[:, l, j, 3, :], COEF[:, l, j, 4, :], hT[:, :, t0:t0 + 256], hreg) for t0 in range(0, T, 256)], scr)
            for fn in (mix_attention, mix_conv, mix_hyena):
                S.barrier()
                st["off"] = base
                fn(l, j, grp, hT, hreg)
                chk(fn.__name__)

        def final_out(yd, T):
            arena_reset()
            scr = norm_scratch()
            yTs = [alloc([8, 256], F32) for _ in range(2)]
            yregs = [Reg() for _ in range(2)]
            yo = [alloc([D], F32) for _ in range(2)]
            yor = [Reg() for _ in range(2)]
            nt = T // 256

            def outphase(it):
                yT, yreg = yTs[it % 2], yregs[it % 2]
                for b2 in range(2):
                    b = it * 2 + b2
                    i = b % 2
                    for hf in range(2):
                        pt, pr = nb()
                        for c4 in range(4):
                            tr(pt[:, c4 * 128:(c4 + 1) * 128], yT[:, hf * 4 + c4, b2 * 128:(b2 + 1) * 128], [yreg], [pr])
                        evac(yo[i][:, hf * 512:(hf + 1) * 512], pt[:], [pr], [yor[i]])
                    S.dma("sync", yd[b * 128:(b + 1) * 128, :], yo[i], [yor[i]], ())
            norm_many([(it * 256, 256, GT[:, 0, 48:56], None, yTs[it % 2], yregs[it % 2]) for it in range(nt)], scr, extra=outphase)

        groups = [dict(T=2048, NSEQ=1, L=2048, sample=True, x="xs", y=ys, j=1),
                  dict(T=1024, NSEQ=4, L=256, sample=False, x="xp", y=yp, j=0)]
        try:
            chk("mod")
            for gi_, grp in enumerate(groups):
                if gi_ > 0:
                    load_x(A[grp["x"]], grp["T"])
                chk("load_x")
                for l in range(2):
                    ffn(l, 1, grp["j"], grp["T"])
                    chk("ffn1")
                    mix(l, grp["j"], grp)
                    ffn(l, 2, grp["j"], grp["T"])
                    chk("ffn2")
                final_out(grp["y"], grp["T"])
                chk("final")
        except StopBuild:
            pass
        if dbg is not None:
            S.barrier()
            S.dma("sync", dbg, xT[:], [RX], [ROUT])
        S.barrier()
        print("instructions:", S.ninst, {e: len(S.prog[e]) for e in ENGS})
        with nc.Block() as block:
            S.emit(block)
    return nc


_CACHE = {}


def kernel(**inputs):
    consts = host_consts()
    n = 8
    f32 = np.float32
    shared = {k: np.ascontiguousarray(inputs[k], dtype=f32) for k in WEIGHT_NAMES}
    in_maps = []
    for i in range(n):
        m = dict(shared)
        m["xs"] = np.ascontiguousarray(inputs["x_sample"][i], dtype=f32)
        m["xp"] = np.ascontiguousarray(inputs["x_prompt"][4 * i:4 * i + 4].reshape(1024, D), dtype=f32)
        m["cc"] = np.ascontiguousarray(np.stack([inputs["c_ctx"], inputs["c"][i]]), dtype=f32)
        m["ck"] = np.ascontiguousarray(inputs["cache_k"][i].reshape(2, 256, 128), dtype=f32)
        m["cv"] = np.ascontiguousarray(inputs["cache_v"][i].reshape(2, 256, 128), dtype=f32)
        m.update(consts)
        in_maps.append(m)
    if "nc" not in _CACHE:
        shapes = {k: (v.shape, F32) for k, v in in_maps[0].items() if k not in consts}
        cshapes = {k: (v.shape, BF16 if v.dtype == ml_dtypes.bfloat16 else F32) for k, v in consts.items()}
        _CACHE["nc"] = build_nc(shapes, cshapes)
    res = run_bass_kernel_spmd(_CACHE["nc"], in_maps, core_ids=list(range(n)))
    R = res.results
    y_prompt = np.concatenate([r["yp"].reshape(4, 256, D) for r in R], 0).astype(f32)
    y_sample = np.stack([r["ys"] for r in R], 0).astype(f32)
    new_k = np.concatenate([r["nk"].reshape(4, 2, 256, 2, 64) for r in R], 0).astype(f32)
    new_v = np.concatenate([r["nv"].reshape(4, 2, 256, 2, 64) for r in R], 0).astype(f32)
    return (y_prompt, y_sample, new_k, new_v)
```
